# Optimizing a Trainium2 kernel written in Bass

```python
import math
import jax, jax.numpy as jnp
from jax import lax
import numpy as np

D_MODEL = 2048
BATCH = 16
SEQ = 2048
DEPTH = 4

N_MIXERS = 2
N_HGRN = (DEPTH + 1) // 2
N_DIFF = DEPTH // 2
D_FF = 5632
MACARON_W = 0.5
HGRN_EXPAND = 128
HGRN_HEADS = D_MODEL // HGRN_EXPAND
HGRN_KDIM = HGRN_EXPAND
HGRN_VDIM = D_MODEL // HGRN_HEADS
HGRN_KEY = HGRN_HEADS * HGRN_KDIM
HGRN_VAL = HGRN_HEADS * HGRN_VDIM
HGRN_CHUNK = 32
MIN_LOWER = 1e-30
DIFF_HEAD_DIM = 128
DIFF_HEADS = D_MODEL // (2 * DIFF_HEAD_DIM)
DIFF_QK = 2 * DIFF_HEADS * DIFF_HEAD_DIM
DIFF_V = DIFF_HEADS * 2 * DIFF_HEAD_DIM
Q_BLOCK = 128
REL_BUCKETS = 32
REL_MAX_DIST = 128
NORM_EPS = 1e-6
SUBLN_EPS = 1e-5

kernel_name = 'hybrid_hgrn2_diffattn_macaron'


def rms_norm(x, gain, eps=NORM_EPS):
    xf = x.astype(jnp.float32)
    y = xf * lax.rsqrt(jnp.mean(xf * xf, axis=-1, keepdims=True) + eps) * gain.astype(jnp.float32)
    return y.astype(x.dtype)


def swiglu_ffn(h, w_in, w_out):
    gate, up = jnp.split(h @ w_in, 2, axis=-1)
    return (jax.nn.silu(gate) * up) @ w_out


def t5_bucket(dist):
    n = jnp.maximum(dist, 0)
    max_exact = REL_BUCKETS // 2
    is_small = n < max_exact
    nf = jnp.maximum(n, max_exact).astype(jnp.float32)
    large = max_exact + (jnp.log(nf / max_exact) / math.log(REL_MAX_DIST / max_exact)
                         * (REL_BUCKETS - max_exact)).astype(jnp.int32)
    large = jnp.minimum(large, REL_BUCKETS - 1)
    return jnp.where(is_small, n, large)


def hgrn2_mixer(h, w_in, lower, norm_gain, w_out):
    B, S, _ = h.shape
    C = HGRN_CHUNK
    N = S // C
    proj = h @ w_in
    q, fz, v, og = jnp.split(proj, [HGRN_KEY, 2 * HGRN_KEY, 2 * HGRN_KEY + HGRN_VAL], axis=-1)
    q = jax.nn.silu(q.astype(jnp.float32))
    fz = fz.astype(jnp.float32)
    lb = lower.astype(jnp.float32)
    log_f = jnp.logaddexp(jnp.log(jnp.maximum(lb, MIN_LOWER)), jnp.log1p(-lb) + jax.nn.log_sigmoid(fz))
    k = (1.0 - lb) * jax.nn.sigmoid(-fz)
    v = v.astype(jnp.float32)

    def to_chunks(t, dh):
        return t.reshape(B, N, C, HGRN_HEADS, dh).transpose(1, 0, 3, 2, 4)

    qc, kc, gc = to_chunks(q, HGRN_KDIM), to_chunks(k, HGRN_KDIM), to_chunks(log_f, HGRN_KDIM)
    vc = to_chunks(v, HGRN_VDIM)
    mask = jnp.tril(jnp.ones((C, C), dtype=bool))[:, :, None]

    def step(state, inp):
        qi, ki, vi, gi = inp
        b = jnp.cumsum(gi, axis=2)
        b_last = b[:, :, -1:, :]
        diff = b[:, :, :, None, :] - b[:, :, None, :, :]
        dec = jnp.exp(jnp.where(mask, diff, -jnp.inf))
        a = jnp.einsum('bhtd,bhsd,bhtsd->bhts', qi, ki, dec)
        o = (jnp.einsum('bhts,bhsv->bhtv', a, vi)
             + jnp.einsum('bhtd,bhdv->bhtv', qi * jnp.exp(b), state))
        new_state = (jnp.exp(b_last[:, :, 0, :])[..., None] * state
                     + jnp.einsum('bhsd,bhsv->bhdv', ki * jnp.exp(b_last - b), vi))
        return new_state, o

    s0 = jnp.zeros((B, HGRN_HEADS, HGRN_KDIM, HGRN_VDIM), jnp.float32)
    _, oc = lax.scan(step, s0, (qc, kc, vc, gc))
    o = oc.transpose(1, 0, 3, 2, 4).reshape(B, S, HGRN_HEADS, HGRN_VDIM)
    o = rms_norm(o, norm_gain.reshape(HGRN_HEADS, HGRN_VDIM))
    o = o.reshape(B, S, HGRN_VAL) * jax.nn.silu(og.astype(jnp.float32))
    return o.astype(h.dtype) @ w_out


def diff_attention(h, w_in, lam_p, subln, w_out, rel_bias, layer_idx):
    B, S, _ = h.shape
    H, d = DIFF_HEADS, DIFF_HEAD_DIM
    NB = S // Q_BLOCK
    proj = h @ w_in
    q, k, v = jnp.split(proj, [DIFF_QK, 2 * DIFF_QK], axis=-1)
    q = q.reshape(B, S, H, 2, d)
    k = k.reshape(B, S, H, 2, d)
    v = v.reshape(B, S, H, 2 * d)
    k1, k2 = k[:, :, :, 0], k[:, :, :, 1]
    q1b = q[:, :, :, 0].reshape(B, NB, Q_BLOCK, H, d).transpose(1, 0, 2, 3, 4)
    q2b = q[:, :, :, 1].reshape(B, NB, Q_BLOCK, H, d).transpose(1, 0, 2, 3, 4)
    lam_init = 0.8 - 0.6 * math.exp(-0.3 * layer_idx)
    lp = lam_p.astype(jnp.float32)
    lam = jnp.exp(jnp.sum(lp[0] * lp[1])) - jnp.exp(jnp.sum(lp[2] * lp[3])) + lam_init
    scale = d ** -0.5
    key_pos = jnp.arange(S)

    def block(args):
        i, a1, a2 = args
        qpos = i * Q_BLOCK + jnp.arange(Q_BLOCK)
        dist = qpos[:, None] - key_pos[None, :]
        causal = dist >= 0
        bias = rel_bias[t5_bucket(dist)].transpose(2, 0, 1).astype(jnp.float32)
        s1 = jnp.einsum('bqhd,bkhd->bhqk', a1, k1).astype(jnp.float32) * scale + bias
        s2 = jnp.einsum('bqhd,bkhd->bhqk', a2, k2).astype(jnp.float32) * scale + bias
        p1 = jax.nn.softmax(jnp.where(causal, s1, -jnp.inf), axis=-1)
        p2 = jax.nn.softmax(jnp.where(causal, s2, -jnp.inf), axis=-1)
        w = (p1 - lam * p2).astype(v.dtype)
        return jnp.einsum('bhqk,bkhv->bqhv', w, v)

    ob = lax.map(block, (jnp.arange(NB), q1b, q2b))
    o = ob.transpose(1, 0, 2, 3, 4).reshape(B, S, H, 2 * d)
    o = rms_norm(o, subln, SUBLN_EPS) * (1.0 - lam_init)
    return o.reshape(B, S, DIFF_V).astype(h.dtype) @ w_out


def setup_inputs(seed: int = 0) -> dict:
    key = jax.random.key(seed)
    ks = jax.random.split(key, 14)

    def nrm(k, shape, scale):
        return jax.random.normal(k, shape, jnp.float32) * scale

    return {
        'x': nrm(ks[0], (BATCH, SEQ, D_MODEL), 1.0),
        'norm_gains': 1.0 + nrm(ks[1], (DEPTH, 3, D_MODEL), 0.02),
        'final_norm': 1.0 + nrm(ks[2], (D_MODEL,), 0.02),
        'ffn_w_in': nrm(ks[3], (DEPTH, 2, D_MODEL, 2 * D_FF), D_MODEL ** -0.5),
        'ffn_w_out': nrm(ks[4], (DEPTH, 2, D_FF, D_MODEL), D_FF ** -0.5),
        'hgrn_w_in': nrm(ks[5], (N_HGRN, D_MODEL, 2 * HGRN_KEY + 2 * HGRN_VAL), D_MODEL ** -0.5),
        'hgrn_lower_bounds': nrm(ks[6], (N_HGRN, HGRN_KEY), 0.5),
        'hgrn_norm': 1.0 + nrm(ks[7], (N_HGRN, HGRN_VAL), 0.02),
        'hgrn_w_out': nrm(ks[8], (N_HGRN, HGRN_VAL, D_MODEL), HGRN_VAL ** -0.5),
        'diff_w_in': nrm(ks[9], (N_DIFF, D_MODEL, 2 * DIFF_QK + DIFF_V), D_MODEL ** -0.5),
        'diff_lambda': nrm(ks[10], (N_DIFF, 4, DIFF_HEAD_DIM), 0.1),
        'diff_subln': 1.0 + nrm(ks[11], (N_DIFF, 2 * DIFF_HEAD_DIM), 0.02),
        'diff_w_out': nrm(ks[12], (N_DIFF, DIFF_V, D_MODEL), DIFF_V ** -0.5),
        'rel_bias': nrm(ks[13], (REL_BUCKETS, DIFF_HEADS), 0.5),
    }


def reference(x, norm_gains, final_norm, ffn_w_in, ffn_w_out, hgrn_w_in, hgrn_lower_bounds,
              hgrn_norm, hgrn_w_out, diff_w_in, diff_lambda, diff_subln, diff_w_out, rel_bias):
    sm = jax.nn.softmax(hgrn_lower_bounds.astype(jnp.float32), axis=0)
    lower = jnp.cumsum(sm, axis=0) - sm[0]
    h = x
    for i in range(DEPTH):
        j = i // N_MIXERS
        h = h + MACARON_W * swiglu_ffn(rms_norm(h, norm_gains[i, 0]), ffn_w_in[i, 0], ffn_w_out[i, 0])
        hn = rms_norm(h, norm_gains[i, 1])
        if i % N_MIXERS == 0:
            mix = hgrn2_mixer(hn, hgrn_w_in[j], lower[j], hgrn_norm[j], hgrn_w_out[j])
        else:
            mix = diff_attention(hn, diff_w_in[j], diff_lambda[j], diff_subln[j], diff_w_out[j], rel_bias, i)
        h = h + mix.astype(h.dtype)
        h = h + MACARON_W * swiglu_ffn(rms_norm(h, norm_gains[i, 2]), ffn_w_in[i, 1], ffn_w_out[i, 1])
    return rms_norm(h, final_norm)
```

```python
import math
import numpy as np
import concourse.bass as bass
import concourse.mybir as mybir
from concourse.bass_utils import run_bass_kernel_spmd

F32 = mybir.dt.float32
BF16 = mybir.dt.bfloat16
AF = mybir.ActivationFunctionType
ALU = mybir.AluOpType
AX = mybir.AxisListType

D = 2048
KC = 16
FF = 5632
S = 2048
NSEQ = 2
T = NSEQ * S
TB = 1024
NBLK = T // TB
NTT = TB // 128
DEPTH = 4
G = 4
GCH = 11
EPS = 1e-6
SUBLN_EPS = 1e-5
QSCALE = 128 ** -0.5
NEG = -30000.0
C_ID, C_BD, C_CH, C_RS, C_OH, C_J = 0, 128, 256, 260, 772, 1156
CW = 1284
SAME_ENG_SYNC = True


def _t5_bucket(n):
    if n < 16:
        return n
    v = 16 + int(np.float32(np.log(np.float32(n) / np.float32(16)) / np.float32(math.log(8.0)) * np.float32(16)))
    return min(v, 31)


def make_consts():
    c = np.zeros((128, CW), np.float32)
    c[:, C_ID:C_ID + 128] = np.eye(128, dtype=np.float32)
    s = np.arange(128)[:, None]
    t = np.arange(128)[None, :]
    c[:, C_BD:C_BD + 128] = ((s // 32 == t // 32) & (s <= t)).astype(np.float32)
    for j in range(4):
        c[:, C_CH + j] = (np.arange(128) // 32 == j).astype(np.float32)
    rs = np.ones(512, np.float32)
    rs[0::32] = 0.0
    c[:, C_RS:C_RS + 512] = rs[None, :]
    for i in range(384):
        dist = i - 127
        if dist < 0:
            c[32, C_OH + i] = 1.0
        else:
            c[_t5_bucket(dist), C_OH + i] += 1.0
        c[31, C_OH + i] -= 1.0
    c[:, C_J:C_J + 128] = np.eye(128, dtype=np.float32)[::-1]
    return c


class Buf:
    __slots__ = ("w", "r")

    def __init__(self):
        self.w = None
        self.r = {}


class Ctx:
    def __init__(self, nc):
        self.nc = nc
        self.E = {"pe": nc.tensor, "act": nc.scalar, "dve": nc.vector, "pool": nc.gpsimd, "sp": nc.sync}
        self.prog = {n: [nc.alloc_semaphore("prog_" + n), 0] for n in ("pe", "act", "dve", "pool")}
        self.slots = {q: [[nc.alloc_semaphore("dq_%s_%d" % (q, i)), 0] for i in range(nsl)]
                      for q, nsl in (("sp", 16), ("pool", 4))}
        self.nslot = {"sp": 0, "pool": 0}
        self.waited = {n: {} for n in self.E}

    def _wait_raw(self, en, sem, val):
        w = self.waited[en]
        if w.get(id(sem), 0) >= val:
            return
        self.E[en].wait_ge(sem, val)
        w[id(sem)] = val

    def _wait(self, en, dep):
        if dep is None:
            return
        sem, val, src = dep
        if src == en and (en == "pe" or not SAME_ENG_SYNC):
            return
        self._wait_raw(en, sem, val)

    def _deps(self, en, reads, writes):
        for b in reads:
            self._wait(en, b.w)
        for b in writes:
            self._wait(en, b.w)
            for d in b.r.values():
                self._wait(en, d)

    def _commit(self, dep, reads, writes):
        for b in reads:
            b.r[id(dep[0])] = dep
        for b in writes:
            b.w = dep
            b.r = {}

    def op(self, en, fn, reads=(), writes=()):
        self._deps(en, reads, writes)
        ins = fn(self.E[en])
        p = self.prog[en]
        p[1] += 1
        ins.then_inc(p[0], 1)
        dep = (p[0], p[1], en)
        self._commit(dep, reads, writes)
        return dep

    def dma(self, q, out, in_, reads=(), writes=(), **kw):
        sl = self.slots[q][self.nslot[q]]
        self.nslot[q] = (self.nslot[q] + 1) % len(self.slots[q])
        if sl[1] > 0:
            self._wait_raw(q, sl[0], sl[1])
        self._deps(q, reads, writes)
        ins = self.E[q].dma_start(out=out, in_=in_, **kw)
        sl[1] += 16
        ins.then_inc(sl[0], 16)
        dep = (sl[0], sl[1], "dma")
        self._commit(dep, reads, writes)
        return dep

    def barrier(self):
        sems = [p for p in self.prog.values()] + [s for q in self.slots.values() for s in q]
        for en in self.E:
            for sem, val in sems:
                if val > 0:
                    self._wait_raw(en, sem, val)


_SBN = [0]


def SB(nc, name, shape, dt):
    _SBN[0] += 1
    return nc.sbuf_tensor("%s_u%d" % (name, _SBN[0]), shape, dt)


def bank(c, b):
    return c.ps[:, b * 512:(b + 1) * 512]


def setup(c, I):
    nc = c.nc
    A = lambda nm, shp, dt: nc.alloc_sbuf_tensor("s_" + nm, shp, dt)
    c.cst = A("cst", [128, CW], F32)
    c.identb = A("identb", [128, 128], BF16)
    c.Jb = A("Jb", [128, 128], BF16)
    c.ones32 = A("ones32", [128, 128], F32)
    c.mhalf = A("mhalf", [128, 1], F32)
    c.lbraw = A("lbraw", [128, 2, 16], F32)
    c.lb = A("lb", [128, 2, 16], F32)
    c.oml = A("oml", [128, 2, 16], F32)
    c.hg = A("hg", [128, 2, 16], F32)
    c.lamraw = A("lamraw", [128, 1024], F32)
    c.lamp = A("lamp", [128, 4, 128], F32)
    c.lams = A("lams", [128, 4], F32)
    c.lame = A("lame", [128, 4], F32)
    c.nlam = A("nlam", [128, 2], F32)
    c.b31 = A("b31", [128, 8], F32)
    c.relb = A("relb", [64, 8], F32)
    c.vecs = A("vecs", [8, 384], F32)
    c.Rb = A("Rb", [128, 8, 2, 128], BF16)
    c.subg = A("subg", [128, 2, 256], F32)
    c.ps = nc.alloc_psum_tensor("ps", [128, 4096], F32)
    c.psb = [Buf() for _ in range(8)]
    b = Buf()
    c.dma("sp", c.cst[:], I["cst"][:, :], writes=[b])
    c.op("dve", lambda e: e.tensor_copy(out=c.identb[:], in_=c.cst[:, C_ID:C_ID + 128]), reads=[b])
    c.op("dve", lambda e: e.tensor_copy(out=c.Jb[:], in_=c.cst[:, C_J:C_J + 128]), reads=[b])
    c.op("dve", lambda e: e.memset(c.ones32[:], 1.0))
    c.op("dve", lambda e: e.memset(c.mhalf[:], -0.5))
    with nc.allow_non_contiguous_dma(reason="one-time small param layout"):
        b1 = Buf()
        c.dma("sp", c.lbraw[:], I["hgrn_lb"].rearrange("l (h p) -> p l h", p=128), writes=[b1])
        b2 = Buf()
        c.dma("sp", c.hg[:], I["hgrn_norm"].rearrange("l (h p) -> p l h", p=128), writes=[b2])
    blb = Buf()
    c.op("dve", lambda e: e.memset(c.lb[:], 0.0), writes=[blb])
    c.op("dve", lambda e: e.tensor_tensor(out=c.lbraw[:, 0, :], in0=c.lbraw[:, 1, :], in1=c.lbraw[:, 0, :],
                                          op=ALU.subtract), reads=[b1], writes=[b1])
    c.op("act", lambda e: e.activation(out=c.lb[:, 1, :], in_=c.lbraw[:, 0, :], func=AF.Sigmoid),
         reads=[b1], writes=[blb])
    c.op("dve", lambda e: e.tensor_scalar(out=c.oml[:], in0=c.lb[:], scalar1=-1.0, scalar2=1.0,
                                          op0=ALU.mult, op1=ALU.add), reads=[blb])
    bl = Buf()
    c.dma("sp", c.lamraw[:], I["diff_lambda"].rearrange("a b -> (a b)").partition_broadcast(128), writes=[bl])
    lv = c.lamraw[:].rearrange("p (j q w d) -> p j q w d", j=2, q=2, w=2)
    for j in range(2):
        for q in range(2):
            c.op("dve", lambda e, j=j, q=q: e.tensor_tensor(out=c.lamp[:, j * 2 + q, :], in0=lv[:, j, q, 0, :],
                                                            in1=lv[:, j, q, 1, :], op=ALU.mult), reads=[bl], writes=[bl])
    c.op("dve", lambda e: e.reduce_sum(out=c.lams[:], in_=c.lamp[:], axis=AX.X), reads=[bl], writes=[bl])
    c.op("act", lambda e: e.activation(out=c.lame[:], in_=c.lams[:], func=AF.Exp), reads=[bl], writes=[bl])
    for j in range(2):
        li = 2 * j + 1
        lam_init = 0.8 - 0.6 * math.exp(-0.3 * li)
        c.op("dve", lambda e, j=j: e.tensor_tensor(out=c.nlam[:, j:j + 1], in0=c.lame[:, 2 * j + 1:2 * j + 2],
                                                   in1=c.lame[:, 2 * j:2 * j + 1], op=ALU.subtract), reads=[bl], writes=[bl])
        c.op("dve", lambda e, j=j, v=lam_init: e.tensor_scalar(out=c.nlam[:, j:j + 1], in0=c.nlam[:, j:j + 1],
                                                               scalar1=-v, scalar2=None, op0=ALU.add), reads=[bl], writes=[bl])
    bb = Buf()
    c.dma("sp", c.b31[:], I["rel_bias"][31:32, :].partition_broadcast(128), writes=[bb])
    br = Buf()
    c.op("dve", lambda e: e.memset(c.relb[:], NEG), writes=[br])
    c.dma("sp", c.relb[0:32, :], I["rel_bias"][:, :], writes=[br])
    c.op("pe", lambda e: e.matmul(c.ps[0:8, 0:384], lhsT=c.relb[0:33, :], rhs=c.cst[0:33, C_OH:C_OH + 384],
                                  start=True, stop=True), reads=[br, b], writes=[c.psb[0]])
    bv = Buf()
    c.op("dve", lambda e: e.tensor_copy(out=c.vecs[:], in_=c.ps[0:8, 0:384]), reads=[c.psb[0]], writes=[bv])
    bd = Buf()
    c.dma("sp", I["BV"][:, :], c.vecs[:], reads=[bv], writes=[bd])
    for h in range(8):
        for dl in range(2):
            src = bass.AP(I["BV"].tensor, h * 384 + 128 * dl, [[1, 128], [1, 128]])
            c.dma("pool", c.Rb[:, h, dl, :], src, reads=[bd])
    bs = Buf()
    c.dma("sp", c.subg[:].rearrange("p a b -> p (a b)"),
          I["diff_subln"].rearrange("a b -> (a b)").partition_broadcast(128), writes=[bs])
    for j in range(2):
        li = 2 * j + 1
        lam_init = 0.8 - 0.6 * math.exp(-0.3 * li)
        c.op("dve", lambda e, j=j, v=lam_init: e.tensor_scalar(out=c.subg[:, j, :], in0=c.subg[:, j, :],
                                                               scalar1=1.0 - v, scalar2=None, op0=ALU.mult),
             reads=[bs], writes=[bs])
    c.Hb = [[Buf() for _ in range(4)] for _ in range(T // 128)]
    c.barrier()


class NT:
    def __init__(self, c, stack, with_norm=True):
        nc = c.nc
        self.hb = [stack.enter_context(SB(nc, "nt_hb%d" % i, [128, D], F32)) for i in range(2)] if with_norm else None
        self.xn = [stack.enter_context(SB(nc, "nt_xn%d" % i, [128, D], BF16)) for i in range(2)]
        self.gbc = stack.enter_context(SB(nc, "nt_gbc", [128, D], F32)) if with_norm else None
        self.st = stack.enter_context(SB(nc, "nt_st", [128, 3, 64], F32))
        self.hbb = [Buf(), Buf()]
        self.xnb = [Buf(), Buf()]
        self.stb = [Buf() for _ in range(64)]
        self.gb = Buf()
        self.n = 0
        self.pT = c.ps[:, 3072:4096].bitcast(BF16).rearrange("p (k t) -> p k t", k=16)
        self.pTb = c.psb[6]


def rstd_col(c, nt, ssap, k, dim, eps, stbuf):
    c.op("dve", lambda e: e.tensor_scalar(out=nt.st[:, 1, k:k + 1], in0=nt.st[:, 0, k:k + 1], scalar1=1.0 / dim,
                                          scalar2=eps, op0=ALU.mult, op1=ALU.add), reads=[stbuf], writes=[stbuf])
    c.op("pool", lambda e: e.tensor_tensor(out=nt.st[:, 2, k:k + 1], in0=nt.st[:, 1, k:k + 1], in1=c.mhalf[:, 0:1],
                                           op=ALU.pow), reads=[stbuf], writes=[stbuf])


def load_gain(c, nt, gain_ap):
    c.dma("sp", nt.gbc[:], gain_ap.partition_broadcast(128), writes=[nt.gb])


def norm_transpose_block(c, nt, src_fn, src_bufs_fn, xT, xTb):
    for tt in range(NTT):
        n = nt.n
        nt.n += 1
        i = n % 2
        k = n % 64
        c.dma("sp", nt.hb[i][:], src_fn(tt), reads=src_bufs_fn(tt), writes=[nt.hbb[i]])
        c.op("act", lambda e: e.activation(out=nt.xn[i][:], in_=nt.hb[i][:], func=AF.Square,
                                           accum_out=nt.st[:, 0, k:k + 1]),
             reads=[nt.hbb[i]], writes=[nt.xnb[i], nt.stb[k]])
        rstd_col(c, nt, None, k, float(D), EPS, nt.stb[k])
        c.op("dve", lambda e: e.scalar_tensor_tensor(out=nt.xn[i][:], in0=nt.hb[i][:], scalar=nt.st[:, 2, k:k + 1],
                                                     in1=nt.gbc[:], op0=ALU.mult, op1=ALU.mult),
             reads=[nt.hbb[i], nt.stb[k], nt.gb], writes=[nt.xnb[i]])
        transpose_tile(c, nt, i, tt, xT, xTb)


def transpose_tile(c, nt, i, tt, xT, xTb):
    def f(e):
        ins = None
        for kc in range(KC):
            ins = e.transpose(out=nt.pT[:, kc, :], in_=nt.xn[i][:, kc * 128:(kc + 1) * 128], identity=c.identb[:])
        return ins
    c.op("pe", f, reads=[nt.xnb[i]], writes=[nt.pTb, c.psb[7]])
    c.op("act", lambda e: e.activation(out=xT[:, :, tt * 128:(tt + 1) * 128], in_=nt.pT, func=AF.Copy),
         reads=[nt.pTb], writes=[xTb])


class GO:
    def __init__(self, c, stack, nkmax):
        nc = c.nc
        self.wo = [stack.enter_context(SB(nc, "go_wo%d" % i, [128, nkmax, 512], BF16)) for i in range(2)]
        self.ht = [stack.enter_context(SB(nc, "go_ht%d" % i, [128, 512], F32)) for i in range(2)]
        self.ho = [stack.enter_context(SB(nc, "go_ho%d" % i, [128, 512], F32)) for i in range(2)]
        self.wob = [Buf(), Buf()]
        self.htb = [Buf(), Buf()]
        self.hob = [Buf(), Buf()]
        self.n = 0
        self.ns = 0


def gemm_out(c, go, lhsT, lhsTb, nk, W, scale, blk, src_ap, H):
    Wv = W.rearrange("(c p) n -> p c n", p=128)

    def load(dq):
        i = go.ns % 2
        go.ns += 1
        c.dma("pool", go.wo[i][:, 0:nk, :], Wv[:, :, dq * 512:(dq + 1) * 512], writes=[go.wob[i]])
        return i
    its = [(dq, tt) for dq in range(4) for tt in range(NTT)]

    def load_ht(k):
        dq, tt = its[k]
        i = (go.n + (k - cur_k[0])) % 2
        gt = blk * NTT + tt
        c.dma("sp", go.ht[i][:], src_ap[gt * 128:(gt + 1) * 128, dq * 512:(dq + 1) * 512],
              reads=[c.Hb[gt][dq]], writes=[go.htb[i]])
    cur_k = [0]
    cur = load(0)
    load_ht(0)
    nxt = None
    for k, (dq, tt) in enumerate(its):
        if tt == 0:
            if dq > 0:
                cur = nxt
            nxt = load(dq + 1) if dq < 3 else None
        cur_k[0] = k
        if k + 1 < len(its):
            load_ht(k + 1)
        n = go.n
        go.n += 1
        i = n % 2
        gt = blk * NTT + tt
        rows = slice(gt * 128, (gt + 1) * 128)
        cols = slice(dq * 512, (dq + 1) * 512)
        pb = 4 + i

        def f(e, pb=pb, tt=tt, cur=cur):
            ins = None
            for j in range(nk):
                ins = e.matmul(bank(c, pb), lhsT=lhsT[:, j, tt * 128:(tt + 1) * 128], rhs=go.wo[cur][:, j, :],
                               start=(j == 0), stop=(j == nk - 1))
            return ins
        c.op("pe", f, reads=[lhsTb, go.wob[cur]], writes=[c.psb[pb]])
        c.op("dve", lambda e, pb=pb, i=i: e.scalar_tensor_tensor(out=go.ho[i][:], in0=bank(c, pb), scalar=float(scale),
                                                                 in1=go.ht[i][:], op0=ALU.mult, op1=ALU.add),
             reads=[c.psb[pb], go.htb[i]], writes=[go.hob[i]])
        c.dma("sp", H[rows, cols], go.ho[i][:], reads=[go.hob[i]], writes=[c.Hb[gt][dq]])


def h_rows_fn(src, blk):
    return lambda tt: src[(blk * NTT + tt) * 128:(blk * NTT + tt + 1) * 128, :]


def h_bufs_fn(c, blk):
    return lambda tt: c.Hb[blk * NTT + tt]


def ffn_phase(c, I, li, k, first):
    import contextlib
    nc = c.nc
    idx = li * 2 + k
    w_in = I["ffn_w_in"][idx * D:(idx + 1) * D, :]
    w_out = I["ffn_w_out"][idx * FF:(idx + 1) * FF, :]
    gain = I["norm_gains"][li * 3 + (0 if k == 0 else 2):li * 3 + (0 if k == 0 else 2) + 1, :]
    Wv = w_in.rearrange("(kc p) n -> p kc n", p=128)
    H = I["H"]
    with contextlib.ExitStack() as st:
        nt = NT(c, st)
        go = GO(c, st, GCH)
        xT = st.enter_context(SB(nc, "xT", [128, KC, TB], BF16))
        aT = st.enter_context(SB(nc, "aT", [128, GCH, TB], BF16))
        wg = [st.enter_context(SB(nc, "wg%d" % i, [128, KC, 256], BF16)) for i in range(2)]
        wu = [st.enter_context(SB(nc, "wu%d" % i, [128, KC, 256], BF16)) for i in range(2)]
        sg = [st.enter_context(SB(nc, "sg%d" % i, [128, 512], F32)) for i in range(2)]
        xTb, aTb = Buf(), Buf()
        wb = [Buf(), Buf()]
        sgb = [Buf(), Buf()]
        load_gain(c, nt, gain)
        ns = 0
        nu = 0
        slabs = [(0, 2), (2, 2), (4, 2), (6, 2), (8, 2), (10, 1)]

        def loadw(g, si):
            nonlocal ns
            i = ns % 2
            ns += 1
            c0, n = slabs[si]
            ca = (g * GCH + c0) * 128
            c.dma("pool", wg[i][:, :, 0:n * 128], Wv[:, :, ca:ca + n * 128], writes=[wb[i]])
            c.dma("pool", wu[i][:, :, 0:n * 128], Wv[:, :, FF + ca:FF + ca + n * 128], writes=[wb[i]])
            return i
        for blk in range(NBLK):
            src = I["x"] if first else H
            norm_transpose_block(c, nt, h_rows_fn(src, blk), h_bufs_fn(c, blk), xT, xTb)
            for g in range(G):
                cur = loadw(g, 0)
                for si, (c0, n) in enumerate(slabs):
                    nxt = loadw(g, si + 1) if si + 1 < len(slabs) else None
                    for ch in range(n):
                        cl = c0 + ch
                        for th in range(2):
                            u = nu % 2
                            nu += 1

                            def fg(e, w=wg[cur], pb=u, ch=ch, th=th):
                                ins = None
                                for kc in range(KC):
                                    ins = e.matmul(bank(c, pb), lhsT=w[:, kc, ch * 128:(ch + 1) * 128],
                                                   rhs=xT[:, kc, th * 512:(th + 1) * 512], start=(kc == 0), stop=(kc == KC - 1))
                                return ins
                            c.op("pe", fg, reads=[wb[cur], xTb], writes=[c.psb[u]])
                            c.op("pe", lambda e, th=th, ch=ch, u=u, cur=cur: fg(e, w=wu[cur], pb=2 + u, ch=ch, th=th),
                                 reads=[wb[cur], xTb], writes=[c.psb[2 + u]])
                            c.op("act", lambda e, u=u: e.activation(out=sg[u][:], in_=bank(c, u), func=AF.Silu),
                                 reads=[c.psb[u]], writes=[sgb[u]])
                            c.op("dve", lambda e, u=u, cl=cl, th=th: e.tensor_tensor(
                                out=aT[:, cl, th * 512:(th + 1) * 512], in0=sg[u][:], in1=bank(c, 2 + u), op=ALU.mult),
                                reads=[sgb[u], c.psb[2 + u]], writes=[aTb])
                    cur = nxt
                srcg = I["x"] if (first and g == 0) else H
                gemm_out(c, go, aT, aTb, GCH, w_out[g * GCH * 128:(g + 1) * GCH * 128, :], 0.5, blk, srcg, H)
    c.barrier()


def vproj(c, I, st_bufs, Wv, col0, xT, xTb, blk):
    wv, wvb, vo, vob, cnt = st_bufs

    def load(dq):
        i = cnt[0] % 2
        cnt[0] += 1
        c.dma("pool", wv[i][:], Wv[:, :, col0 + dq * 512:col0 + (dq + 1) * 512], writes=[wvb[i]])
        return i
    cur = load(0)
    for dq in range(4):
        nxt = load(dq + 1) if dq < 3 else None
        for tt in range(NTT):
            n = cnt[1]
            cnt[1] += 1
            i = n % 2
            pb = i

            def f(e, pb=pb, tt=tt, cur=cur):
                ins = None
                for kc in range(KC):
                    ins = e.matmul(bank(c, pb), lhsT=xT[:, kc, tt * 128:(tt + 1) * 128], rhs=wv[cur][:, kc, :],
                                   start=(kc == 0), stop=(kc == KC - 1))
                return ins
            c.op("pe", f, reads=[xTb, wvb[cur]], writes=[c.psb[pb]])
            c.op("act", lambda e, pb=pb, i=i: e.activation(out=vo[i][:], in_=bank(c, pb), func=AF.Copy),
                 reads=[c.psb[pb]], writes=[vob[i]])
            gt = blk * NTT + tt
            c.dma("sp", I["V"][gt * 128:(gt + 1) * 128, dq * 512:(dq + 1) * 512], vo[i][:], reads=[vob[i]])
        cur = nxt


def hgrn_p1(c, I, li, j):
    import contextlib
    nc = c.nc
    W = I["hgrn_w_in"][j * D:(j + 1) * D, :]
    Wv = W.rearrange("(kc p) n -> p kc n", p=128)
    gain = I["norm_gains"][li * 3 + 1:li * 3 + 2, :]
    H = I["H"]
    with contextlib.ExitStack() as st:
        nt = NT(c, st)
        xT = st.enter_context(SB(nc, "xT", [128, KC, TB], BF16))
        wq = [st.enter_context(SB(nc, "wq%d" % i, [128, KC, 3, 256], BF16)) for i in range(2)]
        wv = [st.enter_context(SB(nc, "wv%d" % i, [128, KC, 512], BF16)) for i in range(2)]
        vo = [st.enter_context(SB(nc, "vo%d" % i, [128, 512], BF16)) for i in range(2)]
        tmp = [st.enter_context(SB(nc, "tmp%d" % i, [128, 5, 512], F32)) for i in range(2)]
        ob = [st.enter_context(SB(nc, "ob%d" % i, [128, 4, 512], BF16)) for i in range(2)]
        eo = [st.enter_context(SB(nc, "eo%d" % i, [128, 16], F32)) for i in range(2)]
        xTb = Buf()
        wqb = [Buf(), Buf()]
        tb = [Buf(), Buf()]
        obb = [Buf(), Buf()]
        vst = (wv, [Buf(), Buf()], vo, [Buf(), Buf()], [0, 0])
        load_gain(c, nt, gain)
        ns = [0]
        nu = 0

        def loadw(hp):
            i = ns[0] % 2
            ns[0] += 1
            for a, base in enumerate((0, 2048, 6144)):
                c.dma("pool", wq[i][:, :, a, :], Wv[:, :, base + hp * 256:base + (hp + 1) * 256], writes=[wqb[i]])
            return i
        rsm = c.cst[:, C_RS:C_RS + 512]
        for blk in range(NBLK):
            norm_transpose_block(c, nt, h_rows_fn(H, blk), h_bufs_fn(c, blk), xT, xTb)
            cur = loadw(0)
            for hp in range(8):
                nxt = loadw(hp + 1) if hp < 7 else None
                for hh in range(2):
                    hd = hp * 2 + hh
                    for th in range(2):
                        u = nu % 2
                        nu += 1
                        tk = slice(blk * TB + th * 512, blk * TB + (th + 1) * 512)
                        for a in range(3):
                            def f(e, a=a, pb=3 * u + a, hh=hh, th=th, cur=cur):
                                ins = None
                                for kc in range(KC):
                                    ins = e.matmul(bank(c, pb), lhsT=wq[cur][:, kc, a, hh * 128:(hh + 1) * 128],
                                                   rhs=xT[:, kc, th * 512:(th + 1) * 512], start=(kc == 0), stop=(kc == KC - 1))
                                return ins
                            c.op("pe", f, reads=[wqb[cur], xTb], writes=[c.psb[3 * u + a]])
                        pq, pz, pg = bank(c, 3 * u), bank(c, 3 * u + 1), bank(c, 3 * u + 2)
                        t = tmp[u]
                        o = ob[u]
                        lbc = c.lb[:, j, hd:hd + 1]
                        omc = c.oml[:, j, hd:hd + 1]
                        R = [c.psb[3 * u], c.psb[3 * u + 1], c.psb[3 * u + 2]]
                        TBF = [tb[u]]
                        OBF = [obb[u]]
                        c.op("act", lambda e: e.activation(out=t[:, 0, :], in_=pz, func=AF.Sigmoid), reads=R, writes=TBF)
                        c.op("act", lambda e: e.activation(out=t[:, 1, :], in_=pz, func=AF.Sigmoid, scale=-1.0), reads=R, writes=TBF)
                        c.op("act", lambda e: e.activation(out=t[:, 2, :], in_=pq, func=AF.Silu), reads=R, writes=TBF)
                        c.op("act", lambda e: e.activation(out=o[:, 3, :], in_=pg, func=AF.Silu), reads=R, writes=OBF)
                        c.op("dve", lambda e: e.tensor_scalar(out=t[:, 0, :], in0=t[:, 0, :], scalar1=omc, scalar2=lbc,
                                                              op0=ALU.mult, op1=ALU.add), reads=TBF, writes=TBF)
                        c.op("act", lambda e: e.activation(out=t[:, 0, :], in_=t[:, 0, :], func=AF.Ln), reads=TBF, writes=TBF)
                        c.op("dve", lambda e: e.tensor_tensor_scan(out=t[:, 3, :], data0=rsm, data1=t[:, 0, :], initial=0.0,
                                                                   op0=ALU.mult, op1=ALU.add), reads=TBF, writes=TBF)
                        c.op("act", lambda e: e.activation(out=t[:, 0, :], in_=t[:, 3, :], func=AF.Exp), reads=TBF, writes=TBF)
                        c.op("act", lambda e: e.activation(out=t[:, 4, :], in_=t[:, 3, :], func=AF.Exp, scale=-1.0), reads=TBF, writes=TBF)
                        c.op("dve", lambda e: e.tensor_tensor(out=o[:, 0, :], in0=t[:, 2, :], in1=t[:, 0, :], op=ALU.mult),
                             reads=TBF, writes=OBF)
                        c.op("dve", lambda e: e.scalar_tensor_tensor(out=t[:, 1, :], in0=t[:, 1, :], scalar=omc, in1=t[:, 4, :],
                                                                     op0=ALU.mult, op1=ALU.mult), reads=TBF, writes=TBF)
                        c.op("act", lambda e: e.activation(out=o[:, 1, :], in_=t[:, 1, :], func=AF.Copy), reads=TBF, writes=OBF)
                        ev = t[:, 0, :].rearrange("p (c t) -> p c t", t=32)[:, :, 31:32]
                        c.op("dve", lambda e: e.tensor_tensor(out=o[:, 2, :].rearrange("p (c t) -> p c t", t=32),
                                                              in0=t[:, 1, :].rearrange("p (c t) -> p c t", t=32),
                                                              in1=ev.to_broadcast([128, 16, 32]), op=ALU.mult), reads=TBF, writes=OBF)
                        c.op("dve", lambda e: e.tensor_copy(out=eo[u][:].rearrange("p (c o) -> p c o", o=1), in_=ev),
                             reads=TBF, writes=OBF)
                        for a, nm in enumerate(("QT", "KT", "KH", "OG")):
                            c.dma("sp", I[nm][hd, :, tk], o[:, a, :], reads=OBF)
                        ck = (blk * TB + th * 512) // 32
                        c.dma("sp", I["EE"][hd, :, ck:ck + 16], eo[u][:], reads=OBF)
                cur = nxt
            vproj(c, I, vst, Wv, 4096, xT, xTb, blk)
    c.barrier()


def hgrn_p2(c, I, li, j):
    import contextlib
    nc = c.nc
    with contextlib.ExitStack() as st:
        def A(nm, shp, dt):
            return st.enter_context(SB(nc, nm, shp, dt))
        qT = [A("qT%d" % i, [128, S], BF16) for i in range(2)]
        kT = [A("kT%d" % i, [128, S], BF16) for i in range(2)]
        khT = [A("khT%d" % i, [128, S], BF16) for i in range(2)]
        ogT = [A("ogT%d" % i, [128, S], BF16) for i in range(2)]
        Vh = [A("Vh%d" % i, [128, 16, 128], BF16) for i in range(2)]
        Eh = [A("Eh%d" % i, [128, 64], F32) for i in range(2)]
        khm = [A("khm%d" % i, [128, 4, 128], BF16) for i in range(2)]
        ATm = [A("ATm%d" % i, [128, 128], BF16) for i in range(2)]
        Sst = A("Sst", [128, 128], F32)
        Sb = [A("Sb%d" % i, [128, 128], BF16) for i in range(2)]
        oT = A("oT", [128, S], F32)
        osq = A("osq", [128, S], F32)
        rst = A("rst", [128, S], F32)
        onb = [A("onb%d" % i, [128, S], BF16) for i in range(2)]
        inb = [Buf(), Buf()]
        khmb = [Buf(), Buf()]
        ATb = [Buf(), Buf()]
        Sbuf = Buf()
        Sbb = [Buf(), Buf()]
        oTb, osqb, rstb = Buf(), Buf(), Buf()
        onbb = [Buf(), Buf()]
        psT = c.ps[:, 3072:3072 + 64].bitcast(BF16)
        pairs = [(s, hd) for s in range(NSEQ) for hd in range(16)]

        def load(pi):
            s, hd = pairs[pi]
            i = pi % 2
            tk = slice(s * S, (s + 1) * S)
            for buf, nm in ((qT, "QT"), (kT, "KT"), (khT, "KH"), (ogT, "OG")):
                c.dma("sp", buf[i][:], I[nm][hd, :, tk], writes=[inb[i]])
            c.dma("sp", Eh[i][:], I["EE"][hd, :, s * 64:(s + 1) * 64], writes=[inb[i]])
            c.dma("sp", Vh[i][:], I["V"][tk, hd * 128:(hd + 1) * 128].rearrange("(t p) v -> p t v", p=128), writes=[inb[i]])
        load(0)
        nch = 0
        ntl = 0
        for pi, (s, hd) in enumerate(pairs):
            if pi + 1 < len(pairs):
                load(pi + 1)
            i = pi % 2
            IB = [inb[i]]
            c.op("dve", lambda e: e.memset(Sst[:], 0.0), writes=[Sbuf])
            c.op("dve", lambda e: e.memset(Sb[nch % 2][:], 0.0), writes=[Sbb[nch % 2]])
            for ti in range(16):
                a = ntl % 2
                ntl += 1
                tsl = slice(ti * 128, (ti + 1) * 128)
                c.op("pe", lambda e: e.transpose(out=psT, in_=khT[i][:, tsl], identity=c.identb[:]), reads=IB, writes=[c.psb[6]])
                for jj in range(4):
                    c.op("act", lambda e, jj=jj: e.activation(out=khm[a][:, jj, :], in_=psT, func=AF.Identity,
                                                              scale=c.cst[:, C_CH + jj:C_CH + jj + 1]),
                         reads=[c.psb[6]], writes=[khmb[a]])
                pA = a
                c.op("pe", lambda e: e.matmul(bank(c, pA)[:, 0:128], lhsT=kT[i][:, tsl], rhs=qT[i][:, tsl], start=True, stop=True),
                     reads=IB, writes=[c.psb[pA]])
                c.op("dve", lambda e: e.tensor_tensor(out=ATm[a][:], in0=bank(c, pA)[:, 0:128], in1=c.cst[:, C_BD:C_BD + 128],
                                                      op=ALU.mult), reads=[c.psb[pA]], writes=[ATb[a]])
                pO = 2 + a
                c.op("pe", lambda e: e.matmul(bank(c, pO)[:, 0:128], lhsT=Vh[i][:, ti, :], rhs=ATm[a][:], start=True, stop=False),
                     reads=IB + [ATb[a]], writes=[c.psb[pO]])
                for jj in range(4):
                    ch = ti * 4 + jj
                    sc = nch % 2
                    sn = (nch + 1) % 2
                    nch += 1
                    pU = 4 + (nch % 2)
                    c.op("pe", lambda e, jj=jj, sc=sc, ch=ch: e.matmul(
                        bank(c, pO)[:, jj * 32:(jj + 1) * 32], lhsT=Sb[sc][:], rhs=qT[i][:, ch * 32:(ch + 1) * 32],
                        start=False, stop=(jj == 3)), reads=IB + [Sbb[sc]], writes=[c.psb[pO]])
                    c.op("pe", lambda e, jj=jj, pU=pU: e.matmul(bank(c, pU)[:, 0:128], lhsT=khm[a][:, jj, :], rhs=Vh[i][:, ti, :],
                                                                start=True, stop=True), reads=IB + [khmb[a]], writes=[c.psb[pU]])
                    c.op("dve", lambda e, pU=pU, ch=ch: e.scalar_tensor_tensor(
                        out=Sst[:], in0=Sst[:], scalar=Eh[i][:, ch:ch + 1], in1=bank(c, pU)[:, 0:128], op0=ALU.mult, op1=ALU.add),
                        reads=IB + [c.psb[pU]], writes=[Sbuf])
                    c.op("act", lambda e, sn=sn: e.activation(out=Sb[sn][:], in_=Sst[:], func=AF.Copy), reads=[Sbuf], writes=[Sbb[sn]])
                c.op("act", lambda e: e.activation(out=oT[:, tsl], in_=bank(c, pO)[:, 0:128], func=AF.Copy),
                     reads=[c.psb[pO]], writes=[oTb])
            c.op("act", lambda e: e.activation(out=osq[:], in_=oT[:], func=AF.Square), reads=[oTb], writes=[osqb])
            for q4 in range(4):
                pS = q4 % 2
                sl = slice(q4 * 512, (q4 + 1) * 512)
                c.op("pe", lambda e: e.matmul(bank(c, pS), lhsT=c.ones32[:], rhs=osq[:, sl], start=True, stop=True),
                     reads=[osqb], writes=[c.psb[pS]])
                c.op("dve", lambda e: e.tensor_scalar(out=rst[:, sl], in0=bank(c, pS), scalar1=1.0 / 128.0, scalar2=EPS,
                                                      op0=ALU.mult, op1=ALU.add), reads=[c.psb[pS]], writes=[rstb])
            c.op("pool", lambda e: e.tensor_tensor(out=rst[:], in0=rst[:], in1=c.mhalf[:, 0:1].to_broadcast([128, S]), op=ALU.pow),
                 reads=[rstb], writes=[rstb])
            c.op("dve", lambda e: e.tensor_tensor(out=oT[:], in0=oT[:], in1=rst[:], op=ALU.mult), reads=[rstb, oTb], writes=[oTb])
            c.op("dve", lambda e: e.scalar_tensor_tensor(out=onb[i][:], in0=oT[:], scalar=c.hg[:, j, hd:hd + 1], in1=ogT[i][:],
                                                         op0=ALU.mult, op1=ALU.mult), reads=IB + [oTb], writes=[onbb[i]])
            c.dma("sp", I["ON"][hd, :, s * S:(s + 1) * S], onb[i][:], reads=[onbb[i]])
    c.barrier()


def outproj_phase(c, I, W, src_name, feature_major):
    import contextlib
    nc = c.nc
    H = I["H"]
    with contextlib.ExitStack() as st:
        go = GO(c, st, KC)
        oT = st.enter_context(SB(nc, "oT", [128, KC, TB], BF16))
        oTb = Buf()
        nt = None if feature_major else NT(c, st, with_norm=False)
        for blk in range(NBLK):
            if feature_major:
                c.dma("sp", oT[:], I[src_name].rearrange("h p t -> p h t")[:, :, blk * TB:(blk + 1) * TB], writes=[oTb])
            else:
                for tt in range(NTT):
                    n = nt.n
                    nt.n += 1
                    i = n % 2
                    gt = blk * NTT + tt
                    c.dma("sp", nt.xn[i][:], I[src_name][gt * 128:(gt + 1) * 128, :], writes=[nt.xnb[i]])
                    transpose_tile(c, nt, i, tt, oT, oTb)
            gemm_out(c, go, oT, oTb, KC, W, 1.0, blk, H, H)
    c.barrier()


def diff_d1(c, I, li, j):
    import contextlib
    nc = c.nc
    W = I["diff_w_in"][j * D:(j + 1) * D, :]
    Wv = W.rearrange("(kc p) n -> p kc n", p=128)
    gain = I["norm_gains"][li * 3 + 1:li * 3 + 2, :]
    H = I["H"]
    with contextlib.ExitStack() as st:
        nt = NT(c, st)
        xT = st.enter_context(SB(nc, "xT", [128, KC, TB], BF16))
        wq = [st.enter_context(SB(nc, "wq%d" % i, [128, KC, 256], BF16)) for i in range(2)]
        wv = [st.enter_context(SB(nc, "wv%d" % i, [128, KC, 512], BF16)) for i in range(2)]
        vo = [st.enter_context(SB(nc, "vo%d" % i, [128, 512], BF16)) for i in range(2)]
        ob = [st.enter_context(SB(nc, "ob%d" % i, [128, 512], BF16)) for i in range(2)]
        xTb = Buf()
        wqb = [Buf(), Buf()]
        obb = [Buf(), Buf()]
        vst = (wv, [Buf(), Buf()], vo, [Buf(), Buf()], [0, 0])
        load_gain(c, nt, gain)
        ns = [0]
        nu = 0

        def loadw(sl):
            i = ns[0] % 2
            ns[0] += 1
            c.dma("pool", wq[i][:], Wv[:, :, sl * 256:(sl + 1) * 256], writes=[wqb[i]])
            return i
        for blk in range(NBLK):
            norm_transpose_block(c, nt, h_rows_fn(H, blk), h_bufs_fn(c, blk), xT, xTb)
            cur = loadw(0)
            for sl in range(16):
                nxt = loadw(sl + 1) if sl < 15 else None
                for ch in range(2):
                    fc = sl * 2 + ch
                    for th in range(2):
                        u = nu % 2
                        nu += 1

                        def f(e, pb=u, ch=ch, th=th, cur=cur):
                            ins = None
                            for kc in range(KC):
                                ins = e.matmul(bank(c, pb), lhsT=wq[cur][:, kc, ch * 128:(ch + 1) * 128],
                                               rhs=xT[:, kc, th * 512:(th + 1) * 512], start=(kc == 0), stop=(kc == KC - 1))
                            return ins
                        c.op("pe", f, reads=[wqb[cur], xTb], writes=[c.psb[u]])
                        sc = QSCALE if fc < 16 else 1.0
                        c.op("act", lambda e, u=u, sc=sc: e.activation(out=ob[u][:], in_=bank(c, u), func=AF.Copy, scale=sc),
                             reads=[c.psb[u]], writes=[obb[u]])
                        tk = slice(blk * TB + th * 512, blk * TB + (th + 1) * 512)
                        c.dma("sp", I["QK"][fc, :, tk], ob[u][:], reads=[obb[u]])
                cur = nxt
            vproj(c, I, vst, Wv, 4096, xT, xTb, blk)
    c.barrier()


def diff_d2(c, I, li, j):
    import contextlib
    nc = c.nc
    with contextlib.ExitStack() as st:
        def A(nm, shp, dt):
            return st.enter_context(SB(nc, nm, shp, dt))
        qk = [A("qk%d" % i, [128, 4, S], BF16) for i in range(2)]
        Vx = [A("Vx%d" % i, [128, 16, 258], BF16) for i in range(2)]
        pT = [A("pT%d" % i, [128, 512], BF16) for i in range(2)]
        O1 = A("O1", [128, 4, 256], F32)
        Of = [A("Of%d" % i, [128, 256], F32) for i in range(2)]
        sqj = A("sqj", [128, 256], BF16)
        ob = [A("odb%d" % i, [128, 256], BF16) for i in range(2)]
        sm = A("sm", [128, 4, 64], F32)
        inb = [Buf(), Buf()]
        pTb = [Buf(), Buf()]
        O1b = Buf()
        Ofb = [Buf(), Buf()]
        obb = [Buf(), Buf()]
        smb = [Buf() for _ in range(64)]
        sqb = Buf()
        for i in range(2):
            c.op("dve", lambda e, i=i: e.memset(Vx[i][:, :, 256:258], 1.0), writes=[inb[i]])
        pairs = [(s, h) for s in range(NSEQ) for h in range(8)]

        def load(pi):
            s, h = pairs[pi]
            i = pi % 2
            tk = slice(s * S, (s + 1) * S)
            for a, fc in enumerate((2 * h, 2 * h + 1, 16 + 2 * h, 16 + 2 * h + 1)):
                c.dma("sp", qk[i][:, a, :], I["QK"][fc, :, tk], writes=[inb[i]])
            c.dma("sp", Vx[i][:, :, 0:256], I["V"][tk, h * 256:(h + 1) * 256].rearrange("(t p) v -> p t v", p=128), writes=[inb[i]])
        load(0)
        nun = 0
        nq = 0
        for pi, (s, h) in enumerate(pairs):
            if pi + 1 < len(pairs):
                load(pi + 1)
            i = pi % 2
            IB = [inb[i]]
            for qg in range(4):
                for ii in range(2):
                    qTi = qk[i][:, ii, :]
                    kTi = qk[i][:, 2 + ii, :]
                    for kt in range(4 * qg + 4):
                        u = nun % 2
                        nun += 1
                        q0 = max(kt, 4 * qg)
                        off = (q0 - 4 * qg) * 128
                        ncol = (4 * qg + 4 - q0) * 128
                        bts = [(dl, kt + dl) for dl in (0, 1) if 4 * qg <= kt + dl < 4 * qg + 4]

                        def f(e, u=u, kt=kt, q0=q0, off=off, ncol=ncol, bts=bts, qTi=qTi, kTi=kTi):
                            ins = e.matmul(bank(c, u)[:, off:off + ncol], lhsT=kTi[:, kt * 128:(kt + 1) * 128],
                                           rhs=qTi[:, q0 * 128:q0 * 128 + ncol], start=True, stop=(len(bts) == 0))
                            for bi, (dl, qt) in enumerate(bts):
                                o2 = (qt - 4 * qg) * 128
                                ins = e.matmul(bank(c, u)[:, o2:o2 + 128], lhsT=c.Jb[:], rhs=c.Rb[:, h, dl, :],
                                               start=False, stop=(bi == len(bts) - 1))
                            return ins
                        c.op("pe", f, reads=IB, writes=[c.psb[u]])
                        c.op("act", lambda e, u=u, off=off, ncol=ncol: e.activation(
                            out=pT[u][:, off:off + ncol], in_=bank(c, u)[:, off:off + ncol], func=AF.Exp, bias=c.b31[:, h:h + 1]),
                            reads=[c.psb[u]], writes=[pTb[u]])

                        def fpv(e, u=u, kt=kt, q0=q0):
                            ins = None
                            for qt in range(q0, 4 * qg + 4):
                                ql = qt - 4 * qg
                                ins = e.matmul(bank(c, 2 + ql)[:, 0:257], lhsT=pT[u][:, ql * 128:(ql + 1) * 128],
                                               rhs=Vx[i][:, kt, 0:257], start=(kt == 0), stop=(kt == qt))
                            return ins
                        c.op("pe", fpv, reads=IB + [pTb[u]], writes=[c.psb[2 + x] for x in range(q0 - 4 * qg, 4)])
                    for ql in range(4):
                        pb = 2 + ql
                        k = nq % 64
                        if ii == 0:
                            nq += 1
                            c.op("dve", lambda e, pb=pb, k=k: e.reciprocal(out=sm[:, 0, k:k + 1], in_=bank(c, pb)[:, 256:257]),
                                 reads=[c.psb[pb]], writes=[smb[k]])
                            c.op("dve", lambda e, pb=pb, k=k, ql=ql: e.tensor_scalar(
                                out=O1[:, ql, :], in0=bank(c, pb)[:, 0:256], scalar1=sm[:, 0, k:k + 1], scalar2=None, op0=ALU.mult),
                                reads=[c.psb[pb], smb[k]], writes=[O1b])
                        else:
                            nq += 1
                            w = k % 2
                            qt = 4 * qg + ql
                            c.op("dve", lambda e, pb=pb, k=k: e.reciprocal(out=sm[:, 0, k:k + 1], in_=bank(c, pb)[:, 256:257]),
                                 reads=[c.psb[pb]], writes=[smb[k]])
                            c.op("dve", lambda e, k=k: e.tensor_tensor(out=sm[:, 0, k:k + 1], in0=sm[:, 0, k:k + 1],
                                                                       in1=c.nlam[:, j:j + 1], op=ALU.mult),
                                 reads=[smb[k]], writes=[smb[k]])
                            c.op("dve", lambda e, pb=pb, k=k, ql=ql, w=w: e.scalar_tensor_tensor(
                                out=Of[w][:], in0=bank(c, pb)[:, 0:256], scalar=sm[:, 0, k:k + 1], in1=O1[:, ql, :],
                                op0=ALU.mult, op1=ALU.add), reads=[c.psb[pb], smb[k], O1b], writes=[Ofb[w]])
                            c.op("act", lambda e, k=k, w=w: e.activation(out=sqj[:], in_=Of[w][:], func=AF.Square,
                                                                         accum_out=sm[:, 1, k:k + 1]),
                                 reads=[Ofb[w]], writes=[sqb, smb[k]])
                            c.op("dve", lambda e, k=k: e.tensor_scalar(out=sm[:, 2, k:k + 1], in0=sm[:, 1, k:k + 1], scalar1=1.0 / 256.0,
                                                                       scalar2=SUBLN_EPS, op0=ALU.mult, op1=ALU.add),
                                 reads=[smb[k]], writes=[smb[k]])
                            c.op("pool", lambda e, k=k: e.tensor_tensor(out=sm[:, 3, k:k + 1], in0=sm[:, 2, k:k + 1],
                                                                        in1=c.mhalf[:, 0:1], op=ALU.pow),
                                 reads=[smb[k]], writes=[smb[k]])
                            c.op("dve", lambda e, k=k, w=w: e.scalar_tensor_tensor(
                                out=ob[w][:], in0=Of[w][:], scalar=sm[:, 3, k:k + 1], in1=c.subg[:, j, :], op0=ALU.mult, op1=ALU.mult),
                                reads=[Ofb[w], smb[k]], writes=[obb[w]])
                            r0 = s * S + qt * 128
                            c.dma("sp", I["V2"][r0:r0 + 128, h * 256:(h + 1) * 256], ob[w][:], reads=[obb[w]])
    c.barrier()


def final_phase(c, I, do_norm):
    import contextlib
    nc = c.nc
    H = I["H"]
    with contextlib.ExitStack() as st:
        nt = NT(c, st)
        ot = [st.enter_context(SB(nc, "ot%d" % i, [128, D], F32)) for i in range(2)]
        otb = [Buf(), Buf()]
        load_gain(c, nt, I["final_norm"][0:1, :])
        for gt in range(T // 128):
            n = nt.n
            nt.n += 1
            i = n % 2
            k = n % 64
            rows = slice(gt * 128, (gt + 1) * 128)
            c.dma("sp", nt.hb[i][:], H[rows, :], writes=[nt.hbb[i]])
            if do_norm:
                c.op("act", lambda e: e.activation(out=nt.xn[i][:], in_=nt.hb[i][:], func=AF.Square,
                                                   accum_out=nt.st[:, 0, k:k + 1]),
                     reads=[nt.hbb[i]], writes=[nt.xnb[i], nt.stb[k]])
                rstd_col(c, nt, None, k, float(D), EPS, nt.stb[k])
                c.op("dve", lambda e: e.scalar_tensor_tensor(out=ot[i][:], in0=nt.hb[i][:], scalar=nt.st[:, 2, k:k + 1],
                                                             in1=nt.gbc[:], op0=ALU.mult, op1=ALU.mult),
                     reads=[nt.hbb[i], nt.stb[k], nt.gb], writes=[otb[i]])
            else:
                c.op("dve", lambda e: e.tensor_copy(out=ot[i][:], in_=nt.hb[i][:]), reads=[nt.hbb[i]], writes=[otb[i]])
            c.dma("sp", I["out"][rows, :], ot[i][:], reads=[otb[i]])
    c.barrier()


IN_SHAPES = {
    "x": [T, D], "norm_gains": [12, D], "final_norm": [1, D], "ffn_w_in": [8 * D, 2 * FF], "ffn_w_out": [8 * FF, D],
    "hgrn_w_in": [2 * D, 8192], "hgrn_lb": [2, D], "hgrn_norm": [2, D], "hgrn_w_out": [2 * D, D],
    "diff_w_in": [2 * D, 6144], "diff_lambda": [2, 512], "diff_subln": [2, 256], "diff_w_out": [2 * D, D],
    "rel_bias": [32, 8], "cst": [128, CW],
}


def build(nsub=12):
    nc = bass.Bass("TRN2", target_bir_lowering=False)
    I = {k: nc.dram_tensor(k, s, F32, kind="ExternalInput").ap() for k, s in IN_SHAPES.items()}
    I["out"] = nc.dram_tensor("out", [T, D], F32, kind="ExternalOutput").ap()
    I["H"] = nc.dram_tensor("H", [T, D], F32).ap()
    I["BV"] = nc.dram_tensor("BV", [8, 384], F32).ap()
    for nm in ("QT", "KT", "KH", "OG", "ON"):
        I[nm] = nc.dram_tensor(nm, [16, 128, T], BF16).ap()
    I["EE"] = nc.dram_tensor("EE", [16, 128, T // 32], F32).ap()
    I["V"] = nc.dram_tensor("V", [T, D], BF16).ap()
    I["V2"] = nc.dram_tensor("V2", [T, D], BF16).ap()
    I["QK"] = nc.dram_tensor("QK", [32, 128, T], BF16).ap()
    c = Ctx(nc)
    setup(c, I)
    subs = []
    for li in range(DEPTH):
        subs += [("ffn", li, 0), ("mix", li, 0), ("ffn", li, 1)]
    for si, (kind, li, k) in enumerate(subs[:nsub]):
        j = li // 2
        if kind == "ffn":
            ffn_phase(c, I, li, k, first=(si == 0))
        elif li % 2 == 0:
            hgrn_p1(c, I, li, j)
            hgrn_p2(c, I, li, j)
            outproj_phase(c, I, I["hgrn_w_out"][j * D:(j + 1) * D, :], "ON", True)
        else:
            diff_d1(c, I, li, j)
            diff_d2(c, I, li, j)
            outproj_phase(c, I, I["diff_w_out"][j * D:(j + 1) * D, :], "V2", False)
    final_phase(c, I, do_norm=(nsub >= 12))
    return nc


def make_in_maps(inputs, n_cores):
    f = lambda a: np.ascontiguousarray(np.asarray(a, dtype=np.float32))
    shared = {
        "norm_gains": f(inputs["norm_gains"]).reshape(12, D),
        "final_norm": f(inputs["final_norm"]).reshape(1, D),
        "ffn_w_in": f(inputs["ffn_w_in"]).reshape(8 * D, 2 * FF),
        "ffn_w_out": f(inputs["ffn_w_out"]).reshape(8 * FF, D),
        "hgrn_w_in": f(inputs["hgrn_w_in"]).reshape(2 * D, 8192),
        "hgrn_lb": f(inputs["hgrn_lower_bounds"]).reshape(2, D),
        "hgrn_norm": f(inputs["hgrn_norm"]).reshape(2, D),
        "hgrn_w_out": f(inputs["hgrn_w_out"]).reshape(2 * D, D),
        "diff_w_in": f(inputs["diff_w_in"]).reshape(2 * D, 6144),
        "diff_lambda": f(inputs["diff_lambda"]).reshape(2, 512),
        "diff_subln": f(inputs["diff_subln"]).reshape(2, 256),
        "diff_w_out": f(inputs["diff_w_out"]).reshape(2 * D, D),
        "rel_bias": f(inputs["rel_bias"]).reshape(32, 8),
        "cst": make_consts(),
    }
    x = f(inputs["x"])
    maps = []
    for cid in range(n_cores):
        m = dict(shared)
        m["x"] = x[cid * NSEQ:(cid + 1) * NSEQ].reshape(T, D)
        maps.append(m)
    return maps


def kernel(**inputs):
    n_cores = 8
    nc = build(12)
    maps = make_in_maps(inputs, n_cores)
    res = run_bass_kernel_spmd(nc, maps, core_ids=list(range(n_cores)))
    out = np.stack([np.asarray(r["out"]).reshape(NSEQ, S, D) for r in res.results], axis=0)
    return out.reshape(n_cores * NSEQ, S, D).astype(np.float32)
```

```python
import math
import numpy as np
import concourse.bass as bass
import concourse.mybir as mybir
from concourse.bass_utils import run_bass_kernel_spmd

F32 = mybir.dt.float32
BF16 = mybir.dt.bfloat16
AF = mybir.ActivationFunctionType
ALU = mybir.AluOpType
AX = mybir.AxisListType

D = 2048
KC = 16
FF = 5632
S = 2048
NSEQ = 2
T = NSEQ * S
TB = 1024
NBLK = T // TB
NTT = TB // 128
DEPTH = 4
G = 2
GCH = 22
EPS = 1e-6
SUBLN_EPS = 1e-5
QSCALE = 128 ** -0.5
NEG = -30000.0
C_ID, C_BD, C_CH, C_J, C_RS, C_OH = 0, 128, 256, 260, 388, 900
CW = 1284
CWP = 388
SAME_ENG_SYNC = True


def _t5_bucket(n):
    if n < 16:
        return n
    v = 16 + int(np.float32(np.log(np.float32(n) / np.float32(16)) / np.float32(math.log(8.0)) * np.float32(16)))
    return min(v, 31)


def make_consts():
    c = np.zeros((128, CW), np.float32)
    c[:, C_ID:C_ID + 128] = np.eye(128, dtype=np.float32)
    s = np.arange(128)[:, None]
    t = np.arange(128)[None, :]
    c[:, C_BD:C_BD + 128] = ((s // 32 == t // 32) & (s <= t)).astype(np.float32)
    for j in range(4):
        c[:, C_CH + j] = (np.arange(128) // 32 == j).astype(np.float32)
    rs = np.ones(512, np.float32)
    rs[0::32] = 0.0
    c[:, C_RS:C_RS + 512] = rs[None, :]
    for i in range(384):
        dist = i - 127
        if dist < 0:
            c[32, C_OH + i] = 1.0
        else:
            c[_t5_bucket(dist), C_OH + i] += 1.0
        c[31, C_OH + i] -= 1.0
    c[:, C_J:C_J + 128] = np.eye(128, dtype=np.float32)[::-1]
    return c


class Buf:
    __slots__ = ("w", "r")

    def __init__(self):
        self.w = None
        self.r = {}


class Ctx:
    def __init__(self, nc):
        self.nc = nc
        self.E = {"pe": nc.tensor, "act": nc.scalar, "dve": nc.vector, "pool": nc.gpsimd, "sp": nc.sync}
        self.prog = {n: [nc.alloc_semaphore("prog_" + n), 0] for n in ("pe", "act", "dve", "pool")}
        self.slots = {q: [[nc.alloc_semaphore("dq_%s_%d" % (q, i)), 0] for i in range(nsl)]
                      for q, nsl in (("sp", 16), ("pool", 4))}
        self.nslot = {"sp": 0, "pool": 0}
        self.waited = {n: {} for n in self.E}

    def _wait_raw(self, en, sem, val):
        w = self.waited[en]
        if w.get(id(sem), 0) >= val:
            return
        self.E[en].wait_ge(sem, val)
        w[id(sem)] = val

    def _wait(self, en, dep):
        if dep is None:
            return
        sem, val, src = dep
        if src == en and (en == "pe" or not SAME_ENG_SYNC):
            return
        self._wait_raw(en, sem, val)

    def _deps(self, en, reads, writes):
        for b in reads:
            self._wait(en, b.w)
        for b in writes:
            self._wait(en, b.w)
            for d in b.r.values():
                self._wait(en, d)

    def _commit(self, dep, reads, writes):
        for b in reads:
            b.r[id(dep[0])] = dep
        for b in writes:
            b.w = dep
            b.r = {}

    def op(self, en, fn, reads=(), writes=()):
        self._deps(en, reads, writes)
        ins = fn(self.E[en])
        p = self.prog[en]
        p[1] += 1
        ins.then_inc(p[0], 1)
        dep = (p[0], p[1], en)
        self._commit(dep, reads, writes)
        return dep

    def dma(self, q, out, in_, reads=(), writes=(), **kw):
        sl = self.slots[q][self.nslot[q]]
        self.nslot[q] = (self.nslot[q] + 1) % len(self.slots[q])
        if sl[1] > 0:
            self._wait_raw(q, sl[0], sl[1])
        self._deps(q, reads, writes)
        ins = self.E[q].dma_start(out=out, in_=in_, **kw)
        sl[1] += 16
        ins.then_inc(sl[0], 16)
        dep = (sl[0], sl[1], "dma")
        self._commit(dep, reads, writes)
        return dep

    def barrier(self):
        sems = [p for p in self.prog.values()] + [s for q in self.slots.values() for s in q]
        for en in self.E:
            for sem, val in sems:
                if val > 0:
                    self._wait_raw(en, sem, val)


_SBN = [0]


def SB(nc, name, shape, dt):
    _SBN[0] += 1
    return nc.sbuf_tensor("%s_u%d" % (name, _SBN[0]), shape, dt)


def bank(c, b):
    return c.ps[:, b * 512:(b + 1) * 512]


def setup(c, I):
    nc = c.nc
    A = lambda nm, shp, dt: nc.alloc_sbuf_tensor("s_" + nm, shp, dt)
    c.cst = A("cst", [128, CWP], F32)
    c.identb = A("identb", [128, 128], BF16)
    c.Jb = A("Jb", [128, 128], BF16)
    c.ones32 = A("ones32", [128, 128], F32)
    c.mhalf = A("mhalf", [128, 1], F32)
    c.lbraw = A("lbraw", [128, 2, 16], F32)
    c.lb = A("lb", [128, 2, 16], F32)
    c.oml = A("oml", [128, 2, 16], F32)
    c.hg = A("hg", [128, 2, 16], F32)
    c.lams = A("lams", [128, 4], F32)
    c.lame = A("lame", [128, 4], F32)
    c.nlam = A("nlam", [128, 2], F32)
    c.b31 = A("b31", [128, 8], F32)
    c.relb = A("relb", [64, 8], F32)
    c.vecs = A("vecs", [8, 384], F32)
    c.Rb = A("Rb", [128, 8, 2, 128], BF16)
    c.subg = A("subg", [128, 2, 256], F32)
    c.ps = nc.alloc_psum_tensor("ps", [128, 4096], F32)
    c.psb = [Buf() for _ in range(8)]
    import contextlib
    stk = contextlib.ExitStack()
    c.lamraw = stk.enter_context(SB(nc, "lamraw", [128, 1024], F32))
    c.lamp = stk.enter_context(SB(nc, "lamp", [128, 4, 128], F32))
    oh = stk.enter_context(SB(nc, "oh", [33, 384], F32))
    b = Buf()
    c.dma("sp", c.cst[:], I["cst"][:, 0:CWP], writes=[b])
    c.dma("sp", oh[:], I["cst"][0:33, C_OH:C_OH + 384], writes=[b])
    c.op("dve", lambda e: e.tensor_copy(out=c.identb[:], in_=c.cst[:, C_ID:C_ID + 128]), reads=[b])
    c.op("dve", lambda e: e.tensor_copy(out=c.Jb[:], in_=c.cst[:, C_J:C_J + 128]), reads=[b])
    c.op("dve", lambda e: e.memset(c.ones32[:], 1.0))
    c.op("dve", lambda e: e.memset(c.mhalf[:], -0.5))
    with nc.allow_non_contiguous_dma(reason="one-time small param layout"):
        b1 = Buf()
        c.dma("sp", c.lbraw[:], I["hgrn_lb"].rearrange("l (h p) -> p l h", p=128), writes=[b1])
        b2 = Buf()
        c.dma("sp", c.hg[:], I["hgrn_norm"].rearrange("l (h p) -> p l h", p=128), writes=[b2])
    blb = Buf()
    c.op("dve", lambda e: e.memset(c.lb[:], 0.0), writes=[blb])
    c.op("dve", lambda e: e.tensor_tensor(out=c.lbraw[:, 0, :], in0=c.lbraw[:, 1, :], in1=c.lbraw[:, 0, :],
                                          op=ALU.subtract), reads=[b1], writes=[b1])
    c.op("act", lambda e: e.activation(out=c.lb[:, 1, :], in_=c.lbraw[:, 0, :], func=AF.Sigmoid),
         reads=[b1], writes=[blb])
    c.op("dve", lambda e: e.tensor_scalar(out=c.oml[:], in0=c.lb[:], scalar1=-1.0, scalar2=1.0,
                                          op0=ALU.mult, op1=ALU.add), reads=[blb])
    bl = Buf()
    c.dma("sp", c.lamraw[:], I["diff_lambda"].rearrange("a b -> (a b)").partition_broadcast(128), writes=[bl])
    lv = c.lamraw[:].rearrange("p (j q w d) -> p j q w d", j=2, q=2, w=2)
    for j in range(2):
        for q in range(2):
            c.op("dve", lambda e, j=j, q=q: e.tensor_tensor(out=c.lamp[:, j * 2 + q, :], in0=lv[:, j, q, 0, :],
                                                            in1=lv[:, j, q, 1, :], op=ALU.mult), reads=[bl], writes=[bl])
    c.op("dve", lambda e: e.reduce_sum(out=c.lams[:], in_=c.lamp[:], axis=AX.X), reads=[bl], writes=[bl])
    c.op("act", lambda e: e.activation(out=c.lame[:], in_=c.lams[:], func=AF.Exp), reads=[bl], writes=[bl])
    for j in range(2):
        li = 2 * j + 1
        lam_init = 0.8 - 0.6 * math.exp(-0.3 * li)
        c.op("dve", lambda e, j=j: e.tensor_tensor(out=c.nlam[:, j:j + 1], in0=c.lame[:, 2 * j + 1:2 * j + 2],
                                                   in1=c.lame[:, 2 * j:2 * j + 1], op=ALU.subtract), reads=[bl], writes=[bl])
        c.op("dve", lambda e, j=j, v=lam_init: e.tensor_scalar(out=c.nlam[:, j:j + 1], in0=c.nlam[:, j:j + 1],
                                                               scalar1=-v, scalar2=None, op0=ALU.add), reads=[bl], writes=[bl])
    bb = Buf()
    c.dma("sp", c.b31[:], I["rel_bias"][31:32, :].partition_broadcast(128), writes=[bb])
    br = Buf()
    c.op("dve", lambda e: e.memset(c.relb[:], NEG), writes=[br])
    c.dma("sp", c.relb[0:32, :], I["rel_bias"][:, :], writes=[br])
    c.op("pe", lambda e: e.matmul(c.ps[0:8, 0:384], lhsT=c.relb[0:33, :], rhs=oh[:],
                                  start=True, stop=True), reads=[br, b], writes=[c.psb[0]])
    bv = Buf()
    c.op("dve", lambda e: e.tensor_copy(out=c.vecs[:], in_=c.ps[0:8, 0:384]), reads=[c.psb[0]], writes=[bv])
    bd = Buf()
    c.dma("sp", I["BV"][:, :], c.vecs[:], reads=[bv], writes=[bd])
    for h in range(8):
        for dl in range(2):
            src = bass.AP(I["BV"].tensor, h * 384 + 128 * dl, [[1, 128], [1, 128]])
            c.dma("pool", c.Rb[:, h, dl, :], src, reads=[bd])
    bs = Buf()
    c.dma("sp", c.subg[:].rearrange("p a b -> p (a b)"),
          I["diff_subln"].rearrange("a b -> (a b)").partition_broadcast(128), writes=[bs])
    for j in range(2):
        li = 2 * j + 1
        lam_init = 0.8 - 0.6 * math.exp(-0.3 * li)
        c.op("dve", lambda e, j=j, v=lam_init: e.tensor_scalar(out=c.subg[:, j, :], in0=c.subg[:, j, :],
                                                               scalar1=1.0 - v, scalar2=None, op0=ALU.mult),
             reads=[bs], writes=[bs])
    c.Hb = [[Buf() for _ in range(4)] for _ in range(T // 128)]
    c.barrier()
    stk.close()


class NT:
    def __init__(self, c, stack, with_norm=True):
        nc = c.nc
        self.hb = [stack.enter_context(SB(nc, "nt_hb%d" % i, [128, D], F32)) for i in range(2)] if with_norm else None
        self.xn = [stack.enter_context(SB(nc, "nt_xn%d" % i, [128, D], BF16)) for i in range(2)]
        self.gbc = stack.enter_context(SB(nc, "nt_gbc", [128, D], F32)) if with_norm else None
        self.st = stack.enter_context(SB(nc, "nt_st", [128, 3, 64], F32))
        self.hbb = [Buf(), Buf()]
        self.xnb = [Buf(), Buf()]
        self.stb = [Buf() for _ in range(64)]
        self.gb = Buf()
        self.n = 0
        self.pT = c.ps[:, 3072:4096].bitcast(BF16).rearrange("p (k t) -> p k t", k=16)
        self.pTb = c.psb[6]


def rstd_col(c, nt, ssap, k, dim, eps, stbuf):
    c.op("dve", lambda e: e.tensor_scalar(out=nt.st[:, 1, k:k + 1], in0=nt.st[:, 0, k:k + 1], scalar1=1.0 / dim,
                                          scalar2=eps, op0=ALU.mult, op1=ALU.add), reads=[stbuf], writes=[stbuf])
    c.op("pool", lambda e: e.tensor_tensor(out=nt.st[:, 2, k:k + 1], in0=nt.st[:, 1, k:k + 1], in1=c.mhalf[:, 0:1],
                                           op=ALU.pow), reads=[stbuf], writes=[stbuf])


def load_gain(c, nt, gain_ap):
    c.dma("sp", nt.gbc[:], gain_ap.partition_broadcast(128), writes=[nt.gb])


def norm_transpose_block(c, nt, src_fn, src_bufs_fn, xT, xTb):
    base = nt.n
    nt.n += NTT

    def stage1(tt):
        n = base + tt
        i = n % 2
        k = n % 64
        c.dma("sp", nt.hb[i][:], src_fn(tt), reads=src_bufs_fn(tt), writes=[nt.hbb[i]])
        c.op("act", lambda e: e.activation(out=nt.xn[i][:], in_=nt.hb[i][:], func=AF.Square,
                                           accum_out=nt.st[:, 0, k:k + 1]),
             reads=[nt.hbb[i]], writes=[nt.xnb[i], nt.stb[k]])
        rstd_col(c, nt, None, k, float(D), EPS, nt.stb[k])
        c.op("dve", lambda e: e.scalar_tensor_tensor(out=nt.xn[i][:], in0=nt.hb[i][:], scalar=nt.st[:, 2, k:k + 1],
                                                     in1=nt.gbc[:], op0=ALU.mult, op1=ALU.mult),
             reads=[nt.hbb[i], nt.stb[k], nt.gb], writes=[nt.xnb[i]])
    stage1(0)
    for tt in range(NTT):
        if tt + 1 < NTT:
            stage1(tt + 1)
        transpose_tile(c, nt, (base + tt) % 2, tt, xT, xTb)


def transpose_tile(c, nt, i, tt, xT, xTb):
    def f(e):
        ins = None
        for kc in range(KC):
            ins = e.transpose(out=nt.pT[:, kc, :], in_=nt.xn[i][:, kc * 128:(kc + 1) * 128], identity=c.identb[:])
        return ins
    c.op("pe", f, reads=[nt.xnb[i]], writes=[nt.pTb, c.psb[7]])
    c.op("act", lambda e: e.activation(out=xT[:, :, tt * 128:(tt + 1) * 128], in_=nt.pT, func=AF.Copy),
         reads=[nt.pTb], writes=[xTb])


class GO:
    def __init__(self, c, stack, nkmax):
        nc = c.nc
        self.wo = [stack.enter_context(SB(nc, "go_wo%d" % i, [128, nkmax, 512], BF16)) for i in range(2)]
        self.ht = [stack.enter_context(SB(nc, "go_ht%d" % i, [128, 512], F32)) for i in range(2)]
        self.ho = [stack.enter_context(SB(nc, "go_ho%d" % i, [128, 512], F32)) for i in range(2)]
        self.wob = [Buf(), Buf()]
        self.htb = [Buf(), Buf()]
        self.hob = [Buf(), Buf()]
        self.n = 0
        self.ns = 0


def gemm_out(c, go, lhsT, lhsTb, nk, W, scale, blk, src_ap, H):
    Wv = W.rearrange("(c p) n -> p c n", p=128)

    def load(dq):
        i = go.ns % 2
        go.ns += 1
        c.dma("pool", go.wo[i][:, 0:nk, :], Wv[:, :, dq * 512:(dq + 1) * 512], writes=[go.wob[i]])
        return i
    its = [(dq, tt) for dq in range(4) for tt in range(NTT)]

    def load_ht(k):
        dq, tt = its[k]
        i = (go.n + (k - cur_k[0])) % 2
        gt = blk * NTT + tt
        c.dma("sp", go.ht[i][:], src_ap[gt * 128:(gt + 1) * 128, dq * 512:(dq + 1) * 512],
              reads=[c.Hb[gt][dq]], writes=[go.htb[i]])
    cur_k = [0]
    cur = load(0)
    load_ht(0)
    nxt = None
    for k, (dq, tt) in enumerate(its):
        if tt == 0:
            if dq > 0:
                cur = nxt
            nxt = load(dq + 1) if dq < 3 else None
        cur_k[0] = k
        if k + 1 < len(its):
            load_ht(k + 1)
        n = go.n
        go.n += 1
        i = n % 2
        gt = blk * NTT + tt
        rows = slice(gt * 128, (gt + 1) * 128)
        cols = slice(dq * 512, (dq + 1) * 512)
        pb = 4 + i

        def f(e, pb=pb, tt=tt, cur=cur):
            ins = None
            for j in range(nk):
                ins = e.matmul(bank(c, pb), lhsT=lhsT[:, j, tt * 128:(tt + 1) * 128], rhs=go.wo[cur][:, j, :],
                               start=(j == 0), stop=(j == nk - 1))
            return ins
        c.op("pe", f, reads=[lhsTb, go.wob[cur]], writes=[c.psb[pb]])
        c.op("dve", lambda e, pb=pb, i=i: e.scalar_tensor_tensor(out=go.ho[i][:], in0=bank(c, pb), scalar=float(scale),
                                                                 in1=go.ht[i][:], op0=ALU.mult, op1=ALU.add),
             reads=[c.psb[pb], go.htb[i]], writes=[go.hob[i]])
        c.dma("sp", H[rows, cols], go.ho[i][:], reads=[go.hob[i]], writes=[c.Hb[gt][dq]])


def h_rows_fn(src, blk):
    return lambda tt: src[(blk * NTT + tt) * 128:(blk * NTT + tt + 1) * 128, :]


def h_bufs_fn(c, blk):
    return lambda tt: c.Hb[blk * NTT + tt]


def ffn_phase(c, I, li, k, first):
    import contextlib
    nc = c.nc
    idx = li * 2 + k
    w_in = I["ffn_w_in"][idx * D:(idx + 1) * D, :]
    w_out = I["ffn_w_out"][idx * FF:(idx + 1) * FF, :]
    gain = I["norm_gains"][li * 3 + (0 if k == 0 else 2):li * 3 + (0 if k == 0 else 2) + 1, :]
    Wv = w_in.rearrange("(kc p) n -> p kc n", p=128)
    H = I["H"]
    with contextlib.ExitStack() as st:
        nt = NT(c, st)
        go = GO(c, st, GCH)
        xT = st.enter_context(SB(nc, "xT", [128, KC, TB], BF16))
        aT = st.enter_context(SB(nc, "aT", [128, GCH, TB], BF16))
        wg = [st.enter_context(SB(nc, "wg%d" % i, [128, KC, 256], BF16)) for i in range(2)]
        wu = [st.enter_context(SB(nc, "wu%d" % i, [128, KC, 256], BF16)) for i in range(2)]
        sg = [st.enter_context(SB(nc, "sg%d" % i, [128, 512], BF16)) for i in range(2)]
        xTb, aTb = Buf(), Buf()
        wb = [Buf(), Buf()]
        sgb = [Buf(), Buf()]
        load_gain(c, nt, gain)
        ns = 0
        nu = 0
        slabs = [(2 * i, 2) for i in range(11)]

        def loadw(g, si):
            nonlocal ns
            i = ns % 2
            ns += 1
            c0, n = slabs[si]
            ca = (g * GCH + c0) * 128
            c.dma("pool", wg[i][:, :, 0:n * 128], Wv[:, :, ca:ca + n * 128], writes=[wb[i]])
            c.dma("pool", wu[i][:, :, 0:n * 128], Wv[:, :, FF + ca:FF + ca + n * 128], writes=[wb[i]])
            return i
        for blk in range(NBLK):
            src = I["x"] if first else H
            norm_transpose_block(c, nt, h_rows_fn(src, blk), h_bufs_fn(c, blk), xT, xTb)
            for g in range(G):
                cur = loadw(g, 0)
                for si, (c0, n) in enumerate(slabs):
                    nxt = loadw(g, si + 1) if si + 1 < len(slabs) else None
                    for ch in range(n):
                        cl = c0 + ch
                        for th in range(2):
                            u = nu % 2
                            nu += 1

                            def fg(e, w=wg[cur], pb=u, ch=ch, th=th):
                                ins = None
                                for kc in range(KC):
                                    ins = e.matmul(bank(c, pb), lhsT=w[:, kc, ch * 128:(ch + 1) * 128],
                                                   rhs=xT[:, kc, th * 512:(th + 1) * 512], start=(kc == 0), stop=(kc == KC - 1))
                                return ins
                            c.op("pe", fg, reads=[wb[cur], xTb], writes=[c.psb[u]])
                            c.op("pe", lambda e, th=th, ch=ch, u=u, cur=cur: fg(e, w=wu[cur], pb=2 + u, ch=ch, th=th),
                                 reads=[wb[cur], xTb], writes=[c.psb[2 + u]])
                            c.op("act", lambda e, u=u: e.activation(out=sg[u][:], in_=bank(c, u), func=AF.Silu),
                                 reads=[c.psb[u]], writes=[sgb[u]])
                            c.op("dve", lambda e, u=u, cl=cl, th=th: e.tensor_tensor(
                                out=aT[:, cl, th * 512:(th + 1) * 512], in0=sg[u][:], in1=bank(c, 2 + u), op=ALU.mult),
                                reads=[sgb[u], c.psb[2 + u]], writes=[aTb])
                    cur = nxt
                srcg = I["x"] if (first and g == 0) else H
                gemm_out(c, go, aT, aTb, GCH, w_out[g * GCH * 128:(g + 1) * GCH * 128, :], 0.5, blk, srcg, H)
    c.barrier()


def vproj(c, I, st_bufs, Wv, col0, xT, xTb, blk):
    wv, wvb, vo, vob, cnt = st_bufs

    def load(dq):
        i = cnt[0] % 2
        cnt[0] += 1
        c.dma("pool", wv[i][:], Wv[:, :, col0 + dq * 512:col0 + (dq + 1) * 512], writes=[wvb[i]])
        return i
    cur = load(0)
    for dq in range(4):
        nxt = load(dq + 1) if dq < 3 else None
        for tt in range(NTT):
            n = cnt[1]
            cnt[1] += 1
            i = n % 2
            pb = i

            def f(e, pb=pb, tt=tt, cur=cur):
                ins = None
                for kc in range(KC):
                    ins = e.matmul(bank(c, pb), lhsT=xT[:, kc, tt * 128:(tt + 1) * 128], rhs=wv[cur][:, kc, :],
                                   start=(kc == 0), stop=(kc == KC - 1))
                return ins
            c.op("pe", f, reads=[xTb, wvb[cur]], writes=[c.psb[pb]])
            c.op("act", lambda e, pb=pb, i=i: e.activation(out=vo[i][:], in_=bank(c, pb), func=AF.Copy),
                 reads=[c.psb[pb]], writes=[vob[i]])
            gt = blk * NTT + tt
            c.dma("sp", I["V"][gt * 128:(gt + 1) * 128, dq * 512:(dq + 1) * 512], vo[i][:], reads=[vob[i]])
        cur = nxt


def hgrn_p1(c, I, li, j):
    import contextlib
    nc = c.nc
    W = I["hgrn_w_in"][j * D:(j + 1) * D, :]
    Wv = W.rearrange("(kc p) n -> p kc n", p=128)
    gain = I["norm_gains"][li * 3 + 1:li * 3 + 2, :]
    H = I["H"]
    with contextlib.ExitStack() as st:
        nt = NT(c, st)
        xT = st.enter_context(SB(nc, "xT", [128, KC, TB], BF16))
        wq = [st.enter_context(SB(nc, "wq%d" % i, [128, KC, 3, 256], BF16)) for i in range(2)]
        wv = [st.enter_context(SB(nc, "wv%d" % i, [128, KC, 512], BF16)) for i in range(2)]
        vo = [st.enter_context(SB(nc, "vo%d" % i, [128, 512], BF16)) for i in range(2)]
        tmp = [st.enter_context(SB(nc, "tmp%d" % i, [128, 5, 512], F32)) for i in range(2)]
        ob = [st.enter_context(SB(nc, "ob%d" % i, [128, 4, 512], BF16)) for i in range(2)]
        eo = [st.enter_context(SB(nc, "eo%d" % i, [128, 16], F32)) for i in range(2)]
        xTb = Buf()
        wqb = [Buf(), Buf()]
        tb = [Buf(), Buf()]
        obb = [Buf(), Buf()]
        vst = (wv, [Buf(), Buf()], vo, [Buf(), Buf()], [0, 0])
        load_gain(c, nt, gain)
        ns = [0]
        nu = 0

        def loadw(hp):
            i = ns[0] % 2
            ns[0] += 1
            for a, base in enumerate((0, 2048, 6144)):
                c.dma("pool", wq[i][:, :, a, :], Wv[:, :, base + hp * 256:base + (hp + 1) * 256], writes=[wqb[i]])
            return i
        rsmt = st.enter_context(SB(nc, "rsm", [128, 512], F32))
        rsb = Buf()
        c.dma("sp", rsmt[:], I["cst"][:, C_RS:C_RS + 512], writes=[rsb])
        rsm = rsmt[:]
        for blk in range(NBLK):
            norm_transpose_block(c, nt, h_rows_fn(H, blk), h_bufs_fn(c, blk), xT, xTb)
            cur = loadw(0)
            for hp in range(8):
                nxt = loadw(hp + 1) if hp < 7 else None
                for hh in range(2):
                    hd = hp * 2 + hh
                    for th in range(2):
                        u = nu % 2
                        nu += 1
                        tk = slice(blk * TB + th * 512, blk * TB + (th + 1) * 512)
                        for a in range(3):
                            def f(e, a=a, pb=3 * u + a, hh=hh, th=th, cur=cur):
                                ins = None
                                for kc in range(KC):
                                    ins = e.matmul(bank(c, pb), lhsT=wq[cur][:, kc, a, hh * 128:(hh + 1) * 128],
                                                   rhs=xT[:, kc, th * 512:(th + 1) * 512], start=(kc == 0), stop=(kc == KC - 1))
                                return ins
                            c.op("pe", f, reads=[wqb[cur], xTb], writes=[c.psb[3 * u + a]])
                        pq, pz, pg = bank(c, 3 * u), bank(c, 3 * u + 1), bank(c, 3 * u + 2)
                        t = tmp[u]
                        o = ob[u]
                        lbc = c.lb[:, j, hd:hd + 1]
                        omc = c.oml[:, j, hd:hd + 1]
                        R = [c.psb[3 * u], c.psb[3 * u + 1], c.psb[3 * u + 2]]
                        TBF = [tb[u]]
                        OBF = [obb[u]]
                        c.op("act", lambda e: e.activation(out=t[:, 0, :], in_=pz, func=AF.Sigmoid), reads=R, writes=TBF)
                        c.op("act", lambda e: e.activation(out=t[:, 1, :], in_=pz, func=AF.Sigmoid, scale=-1.0), reads=R, writes=TBF)
                        c.op("act", lambda e: e.activation(out=t[:, 2, :], in_=pq, func=AF.Silu), reads=R, writes=TBF)
                        c.op("act", lambda e: e.activation(out=o[:, 3, :], in_=pg, func=AF.Silu), reads=R, writes=OBF)
                        c.op("dve", lambda e: e.tensor_scalar(out=t[:, 0, :], in0=t[:, 0, :], scalar1=omc, scalar2=lbc,
                                                              op0=ALU.mult, op1=ALU.add), reads=TBF, writes=TBF)
                        c.op("act", lambda e: e.activation(out=t[:, 0, :], in_=t[:, 0, :], func=AF.Ln), reads=TBF, writes=TBF)
                        c.op("dve", lambda e: e.tensor_tensor_scan(out=t[:, 3, :], data0=rsm, data1=t[:, 0, :], initial=0.0,
                                                                   op0=ALU.mult, op1=ALU.add), reads=TBF + [rsb], writes=TBF)
                        c.op("act", lambda e: e.activation(out=t[:, 0, :], in_=t[:, 3, :], func=AF.Exp), reads=TBF, writes=TBF)
                        c.op("act", lambda e: e.activation(out=t[:, 4, :], in_=t[:, 3, :], func=AF.Exp, scale=-1.0), reads=TBF, writes=TBF)
                        c.op("dve", lambda e: e.tensor_tensor(out=o[:, 0, :], in0=t[:, 2, :], in1=t[:, 0, :], op=ALU.mult),
                             reads=TBF, writes=OBF)
                        c.op("dve", lambda e: e.scalar_tensor_tensor(out=t[:, 1, :], in0=t[:, 1, :], scalar=omc, in1=t[:, 4, :],
                                                                     op0=ALU.mult, op1=ALU.mult), reads=TBF, writes=TBF)
                        c.op("act", lambda e: e.activation(out=o[:, 1, :], in_=t[:, 1, :], func=AF.Copy), reads=TBF, writes=OBF)
                        ev = t[:, 0, :].rearrange("p (c t) -> p c t", t=32)[:, :, 31:32]
                        c.op("dve", lambda e: e.tensor_tensor(out=o[:, 2, :].rearrange("p (c t) -> p c t", t=32),
                                                              in0=t[:, 1, :].rearrange("p (c t) -> p c t", t=32),
                                                              in1=ev.to_broadcast([128, 16, 32]), op=ALU.mult), reads=TBF, writes=OBF)
                        c.op("dve", lambda e: e.tensor_copy(out=eo[u][:].rearrange("p (c o) -> p c o", o=1), in_=ev),
                             reads=TBF, writes=OBF)
                        for a, nm in enumerate(("QT", "KT", "KH", "OG")):
                            c.dma("sp", I[nm][hd, :, tk], o[:, a, :], reads=OBF)
                        ck = (blk * TB + th * 512) // 32
                        c.dma("sp", I["EE"][hd, :, ck:ck + 16], eo[u][:], reads=OBF)
                cur = nxt
            vproj(c, I, vst, Wv, 4096, xT, xTb, blk)
    c.barrier()


def hgrn_p2(c, I, li, j):
    import contextlib
    nc = c.nc
    NP = 4
    with contextlib.ExitStack() as st:
        def A(nm, shp, dt):
            return st.enter_context(SB(nc, nm, shp, dt))
        qT = [A("qT%d" % i, [128, S], BF16) for i in range(NP)]
        kT = [A("kT%d" % i, [128, S], BF16) for i in range(NP)]
        khT = [A("khT%d" % i, [128, S], BF16) for i in range(NP)]
        ogT = [A("ogT%d" % i, [128, S], BF16) for i in range(NP)]
        Vh = [A("Vh%d" % i, [128, 16, 128], BF16) for i in range(NP)]
        Eh = [A("Eh%d" % i, [128, 64], F32) for i in range(NP)]
        khm = [A("khm%d" % i, [128, 4, 128], BF16) for i in range(NP)]
        ATm = [A("ATm%d" % i, [128, 128], BF16) for i in range(NP)]
        Sst = [A("Sst%d" % i, [128, 128], F32) for i in range(NP)]
        Sb = [[A("Sb%d_%d" % (i, k), [128, 128], BF16) for k in range(2)] for i in range(NP)]
        oT = [A("oT%d" % i, [128, S], F32) for i in range(NP)]
        osq = A("osq", [128, S], F32)
        rst = A("rst", [128, S], F32)
        onb = [A("onb%d" % i, [128, S], BF16) for i in range(2)]
        epsb = A("epsb", [128, 1], F32)
        inA = [Buf() for _ in range(NP)]
        inB = [Buf() for _ in range(NP)]
        khmb = [Buf() for _ in range(NP)]
        ATb = [Buf() for _ in range(NP)]
        Sbuf = [Buf() for _ in range(NP)]
        Sbb = [[Buf(), Buf()] for _ in range(NP)]
        oTb = [Buf() for _ in range(NP)]
        osqb, rstb = Buf(), Buf()
        onbb = [Buf(), Buf()]
        pAb = [c.psb[0] for _ in range(NP)]
        pTb = [c.psb[6] for _ in range(NP)]
        pA = [c.ps[:, p * 128:(p + 1) * 128] for p in range(NP)]
        pU2 = [[c.ps[:, bk * 512 + p * 128:bk * 512 + (p + 1) * 128] for p in range(NP)] for bk in (1, 7)]
        pUb2 = [c.psb[1], c.psb[7]]
        pO = [c.ps[:, (2 + p) * 512:(2 + p) * 512 + 128] for p in range(NP)]
        pOb = [c.psb[2 + p] for p in range(NP)]
        pT = [c.ps[:, 3072 + p * 64:3072 + (p + 1) * 64].bitcast(BF16) for p in range(NP)]
        c.op("dve", lambda e: e.memset(epsb[:], EPS))
        pairs = [(s, hd) for s in range(NSEQ) for hd in range(16)]
        chm = c.cst[:, C_CH:C_CH + 4]

        def load(p, s, hd):
            tk = slice(s * S, (s + 1) * S)
            for buf, nm in ((qT, "QT"), (kT, "KT"), (khT, "KH")):
                c.dma("sp", buf[p][:], I[nm][hd, :, tk], writes=[inA[p]])
            c.dma("sp", Eh[p][:], I["EE"][hd, :, s * 64:(s + 1) * 64], writes=[inA[p]])
            c.dma("sp", Vh[p][:], I["V"][tk, hd * 128:(hd + 1) * 128].rearrange("(t p) v -> p t v", p=128), writes=[inA[p]])
            c.dma("sp", ogT[p][:], I["OG"][hd, :, tk], writes=[inB[p]])
        ngr = len(pairs) // NP
        for p in range(NP):
            load(p, *pairs[p])
        nend = 0
        for gi in range(ngr):
            grp = pairs[gi * NP:(gi + 1) * NP]
            for p in range(NP):
                c.op("dve", lambda e, p=p: e.memset(Sst[p][:], 0.0), writes=[Sbuf[p]])
                c.op("dve", lambda e, p=p: e.memset(Sb[p][0][:], 0.0), writes=[Sbb[p][0]])
            for ti in range(16):
                tsl = slice(ti * 128, (ti + 1) * 128)
                for p in range(NP):
                    IB = [inA[p]]
                    c.op("pe", lambda e, p=p: e.transpose(out=pT[p], in_=khT[p][:, tsl], identity=c.identb[:]),
                         reads=IB, writes=[pTb[p]])
                    c.op("pe", lambda e, p=p: e.matmul(pA[p], lhsT=kT[p][:, tsl], rhs=qT[p][:, tsl], start=True, stop=True),
                         reads=IB, writes=[pAb[p]])
                for p in range(NP):
                    c.op("dve", lambda e, p=p: e.tensor_tensor(
                        out=khm[p][:], in0=pT[p].rearrange("p (o d) -> p o d", o=1).to_broadcast([128, 4, 128]),
                        in1=chm.rearrange("p (j o) -> p j o", o=1).to_broadcast([128, 4, 128]), op=ALU.mult),
                        reads=[pTb[p]], writes=[khmb[p]])
                    c.op("dve", lambda e, p=p: e.tensor_tensor(out=ATm[p][:], in0=pA[p], in1=c.cst[:, C_BD:C_BD + 128], op=ALU.mult),
                         reads=[pAb[p]], writes=[ATb[p]])
                for p in range(NP):
                    c.op("pe", lambda e, p=p: e.matmul(pO[p], lhsT=Vh[p][:, ti, :], rhs=ATm[p][:], start=True, stop=False),
                         reads=[inA[p], ATb[p]], writes=[pOb[p]])
                for jj in range(4):
                    ch = ti * 4 + jj
                    sc = ch % 2
                    sn = (ch + 1) % 2
                    pU = pU2[ch % 2]
                    pUb = [pUb2[ch % 2]] * NP
                    for p in range(NP):
                        c.op("pe", lambda e, p=p: e.matmul(pO[p][:, jj * 32:(jj + 1) * 32], lhsT=Sb[p][sc][:],
                                                           rhs=qT[p][:, ch * 32:(ch + 1) * 32], start=False, stop=(jj == 3)),
                             reads=[inA[p], Sbb[p][sc]], writes=[pOb[p]])
                        c.op("pe", lambda e, p=p: e.matmul(pU[p], lhsT=khm[p][:, jj, :], rhs=Vh[p][:, ti, :], start=True, stop=True),
                             reads=[inA[p], khmb[p]], writes=[pUb[p]])
                    for p in range(NP):
                        c.op("dve", lambda e, p=p: e.scalar_tensor_tensor(
                            out=Sst[p][:], in0=Sst[p][:], scalar=Eh[p][:, ch:ch + 1], in1=pU[p], op0=ALU.mult, op1=ALU.add),
                            reads=[inA[p], pUb[p]], writes=[Sbuf[p]])
                        c.op("act", lambda e, p=p: e.activation(out=Sb[p][sn][:], in_=Sst[p][:], func=AF.Copy),
                             reads=[Sbuf[p]], writes=[Sbb[p][sn]])
                for p in range(NP):
                    c.op("act", lambda e, p=p: e.activation(out=oT[p][:, tsl], in_=pO[p], func=AF.Copy),
                         reads=[pOb[p]], writes=[oTb[p]])
            if gi + 1 < ngr:
                nxt = pairs[(gi + 1) * NP:(gi + 2) * NP]
            else:
                nxt = []
            for p, (s, hd) in enumerate(grp):
                w = nend % 2
                nend += 1
                c.op("act", lambda e: e.activation(out=osq[:], in_=oT[p][:], func=AF.Square), reads=[oTb[p]], writes=[osqb])
                for q4 in range(4):
                    sl = slice(q4 * 512, (q4 + 1) * 512)
                    c.op("pe", lambda e: e.matmul(bank(c, 0), lhsT=c.ones32[:], rhs=osq[:, sl], start=True, stop=True),
                         reads=[osqb], writes=[c.psb[0]])
                    c.op("act", lambda e: e.activation(out=rst[:, sl], in_=bank(c, 0), func=AF.Sqrt, scale=1.0 / 128.0,
                                                       bias=epsb[:, 0:1]), reads=[c.psb[0]], writes=[rstb])
                c.op("dve", lambda e: e.reciprocal(out=rst[:], in_=rst[:]), reads=[rstb], writes=[rstb])
                c.op("dve", lambda e: e.tensor_tensor(out=oT[p][:], in0=oT[p][:], in1=rst[:], op=ALU.mult),
                     reads=[rstb, oTb[p]], writes=[oTb[p]])
                c.op("dve", lambda e: e.scalar_tensor_tensor(out=onb[w][:], in0=oT[p][:], scalar=c.hg[:, j, hd:hd + 1], in1=ogT[p][:],
                                                             op0=ALU.mult, op1=ALU.mult), reads=[inB[p], oTb[p]], writes=[onbb[w]])
                c.dma("sp", I["ON"][hd, :, s * S:(s + 1) * S], onb[w][:], reads=[onbb[w]])
                if nxt:
                    load(p, *nxt[p])
    c.barrier()


def outproj_phase(c, I, W, src_name, feature_major):
    import contextlib
    nc = c.nc
    H = I["H"]
    with contextlib.ExitStack() as st:
        go = GO(c, st, KC)
        oT = st.enter_context(SB(nc, "oT", [128, KC, TB], BF16))
        oTb = Buf()
        nt = None if feature_major else NT(c, st, with_norm=False)
        for blk in range(NBLK):
            if feature_major:
                c.dma("sp", oT[:], I[src_name].rearrange("h p t -> p h t")[:, :, blk * TB:(blk + 1) * TB], writes=[oTb])
            else:
                for tt in range(NTT):
                    n = nt.n
                    nt.n += 1
                    i = n % 2
                    gt = blk * NTT + tt
                    c.dma("sp", nt.xn[i][:], I[src_name][gt * 128:(gt + 1) * 128, :], writes=[nt.xnb[i]])
                    transpose_tile(c, nt, i, tt, oT, oTb)
            gemm_out(c, go, oT, oTb, KC, W, 1.0, blk, H, H)
    c.barrier()


def diff_d1(c, I, li, j):
    import contextlib
    nc = c.nc
    W = I["diff_w_in"][j * D:(j + 1) * D, :]
    Wv = W.rearrange("(kc p) n -> p kc n", p=128)
    gain = I["norm_gains"][li * 3 + 1:li * 3 + 2, :]
    H = I["H"]
    with contextlib.ExitStack() as st:
        nt = NT(c, st)
        xT = st.enter_context(SB(nc, "xT", [128, KC, TB], BF16))
        wq = [st.enter_context(SB(nc, "wq%d" % i, [128, KC, 256], BF16)) for i in range(2)]
        wv = [st.enter_context(SB(nc, "wv%d" % i, [128, KC, 512], BF16)) for i in range(2)]
        vo = [st.enter_context(SB(nc, "vo%d" % i, [128, 512], BF16)) for i in range(2)]
        ob = [st.enter_context(SB(nc, "ob%d" % i, [128, 512], BF16)) for i in range(2)]
        xTb = Buf()
        wqb = [Buf(), Buf()]
        obb = [Buf(), Buf()]
        vst = (wv, [Buf(), Buf()], vo, [Buf(), Buf()], [0, 0])
        load_gain(c, nt, gain)
        ns = [0]
        nu = 0

        def loadw(sl):
            i = ns[0] % 2
            ns[0] += 1
            c.dma("pool", wq[i][:], Wv[:, :, sl * 256:(sl + 1) * 256], writes=[wqb[i]])
            return i
        for blk in range(NBLK):
            norm_transpose_block(c, nt, h_rows_fn(H, blk), h_bufs_fn(c, blk), xT, xTb)
            cur = loadw(0)
            for sl in range(16):
                nxt = loadw(sl + 1) if sl < 15 else None
                for ch in range(2):
                    fc = sl * 2 + ch
                    for th in range(2):
                        u = nu % 2
                        nu += 1

                        def f(e, pb=u, ch=ch, th=th, cur=cur):
                            ins = None
                            for kc in range(KC):
                                ins = e.matmul(bank(c, pb), lhsT=wq[cur][:, kc, ch * 128:(ch + 1) * 128],
                                               rhs=xT[:, kc, th * 512:(th + 1) * 512], start=(kc == 0), stop=(kc == KC - 1))
                            return ins
                        c.op("pe", f, reads=[wqb[cur], xTb], writes=[c.psb[u]])
                        sc = QSCALE if fc < 16 else 1.0
                        c.op("act", lambda e, u=u, sc=sc: e.activation(out=ob[u][:], in_=bank(c, u), func=AF.Copy, scale=sc),
                             reads=[c.psb[u]], writes=[obb[u]])
                        tk = slice(blk * TB + th * 512, blk * TB + (th + 1) * 512)
                        c.dma("sp", I["QK"][fc, :, tk], ob[u][:], reads=[obb[u]])
                cur = nxt
            vproj(c, I, vst, Wv, 4096, xT, xTb, blk)
    c.barrier()


def diff_d2(c, I, li, j):
    import contextlib
    nc = c.nc
    with contextlib.ExitStack() as st:
        def A(nm, shp, dt):
            return st.enter_context(SB(nc, nm, shp, dt))
        qk = [A("qk%d" % i, [128, 4, S], BF16) for i in range(2)]
        Vx = [A("Vx%d" % i, [128, 16, 258], BF16) for i in range(2)]
        pT = [A("pT%d" % i, [128, 512], BF16) for i in range(2)]
        O1 = A("O1", [128, 4, 256], F32)
        Of = [A("Of%d" % i, [128, 256], F32) for i in range(2)]
        sqj = A("sqj", [128, 256], BF16)
        ob = [A("odb%d" % i, [128, 256], BF16) for i in range(2)]
        sm = A("sm", [128, 4, 64], F32)
        inb = [Buf(), Buf()]
        pTb = [Buf(), Buf()]
        O1b = Buf()
        Ofb = [Buf(), Buf()]
        obb = [Buf(), Buf()]
        smb = [Buf() for _ in range(64)]
        sqb = Buf()
        for i in range(2):
            c.op("dve", lambda e, i=i: e.memset(Vx[i][:, :, 256:258], 1.0), writes=[inb[i]])
        pairs = [(s, h) for s in range(NSEQ) for h in range(8)]

        def load(pi):
            s, h = pairs[pi]
            i = pi % 2
            tk = slice(s * S, (s + 1) * S)
            for a, fc in enumerate((2 * h, 2 * h + 1, 16 + 2 * h, 16 + 2 * h + 1)):
                c.dma("sp", qk[i][:, a, :], I["QK"][fc, :, tk], writes=[inb[i]])
            c.dma("sp", Vx[i][:, :, 0:256], I["V"][tk, h * 256:(h + 1) * 256].rearrange("(t p) v -> p t v", p=128), writes=[inb[i]])
        load(0)
        nun = 0
        nq = 0
        for pi, (s, h) in enumerate(pairs):
            if pi + 1 < len(pairs):
                load(pi + 1)
            i = pi % 2
            IB = [inb[i]]
            for qg in range(4):
                for ii in range(2):
                    qTi = qk[i][:, ii, :]
                    kTi = qk[i][:, 2 + ii, :]
                    for kt in range(4 * qg + 4):
                        u = nun % 2
                        nun += 1
                        q0 = max(kt, 4 * qg)
                        off = (q0 - 4 * qg) * 128
                        ncol = (4 * qg + 4 - q0) * 128
                        bts = [(dl, kt + dl) for dl in (0, 1) if 4 * qg <= kt + dl < 4 * qg + 4]

                        def f(e, u=u, kt=kt, q0=q0, off=off, ncol=ncol, bts=bts, qTi=qTi, kTi=kTi):
                            ins = e.matmul(bank(c, u)[:, off:off + ncol], lhsT=kTi[:, kt * 128:(kt + 1) * 128],
                                           rhs=qTi[:, q0 * 128:q0 * 128 + ncol], start=True, stop=(len(bts) == 0))
                            for bi, (dl, qt) in enumerate(bts):
                                o2 = (qt - 4 * qg) * 128
                                ins = e.matmul(bank(c, u)[:, o2:o2 + 128], lhsT=c.Jb[:], rhs=c.Rb[:, h, dl, :],
                                               start=False, stop=(bi == len(bts) - 1))
                            return ins
                        c.op("pe", f, reads=IB, writes=[c.psb[u]])
                        c.op("act", lambda e, u=u, off=off, ncol=ncol: e.activation(
                            out=pT[u][:, off:off + ncol], in_=bank(c, u)[:, off:off + ncol], func=AF.Exp, bias=c.b31[:, h:h + 1]),
                            reads=[c.psb[u]], writes=[pTb[u]])

                        def fpv(e, u=u, kt=kt, q0=q0):
                            ins = None
                            for qt in range(q0, 4 * qg + 4):
                                ql = qt - 4 * qg
                                ins = e.matmul(bank(c, 2 + ql)[:, 0:257], lhsT=pT[u][:, ql * 128:(ql + 1) * 128],
                                               rhs=Vx[i][:, kt, 0:257], start=(kt == 0), stop=(kt == qt))
                            return ins
                        c.op("pe", fpv, reads=IB + [pTb[u]], writes=[c.psb[2 + x] for x in range(q0 - 4 * qg, 4)])
                    for ql in range(4):
                        pb = 2 + ql
                        k = nq % 64
                        if ii == 0:
                            nq += 1
                            c.op("dve", lambda e, pb=pb, k=k: e.reciprocal(out=sm[:, 0, k:k + 1], in_=bank(c, pb)[:, 256:257]),
                                 reads=[c.psb[pb]], writes=[smb[k]])
                            c.op("dve", lambda e, pb=pb, k=k, ql=ql: e.tensor_scalar(
                                out=O1[:, ql, :], in0=bank(c, pb)[:, 0:256], scalar1=sm[:, 0, k:k + 1], scalar2=None, op0=ALU.mult),
                                reads=[c.psb[pb], smb[k]], writes=[O1b])
                        else:
                            nq += 1
                            w = k % 2
                            qt = 4 * qg + ql
                            c.op("dve", lambda e, pb=pb, k=k: e.reciprocal(out=sm[:, 0, k:k + 1], in_=bank(c, pb)[:, 256:257]),
                                 reads=[c.psb[pb]], writes=[smb[k]])
                            c.op("dve", lambda e, k=k: e.tensor_tensor(out=sm[:, 0, k:k + 1], in0=sm[:, 0, k:k + 1],
                                                                       in1=c.nlam[:, j:j + 1], op=ALU.mult),
                                 reads=[smb[k]], writes=[smb[k]])
                            c.op("dve", lambda e, pb=pb, k=k, ql=ql, w=w: e.scalar_tensor_tensor(
                                out=Of[w][:], in0=bank(c, pb)[:, 0:256], scalar=sm[:, 0, k:k + 1], in1=O1[:, ql, :],
                                op0=ALU.mult, op1=ALU.add), reads=[c.psb[pb], smb[k], O1b], writes=[Ofb[w]])
                            c.op("act", lambda e, k=k, w=w: e.activation(out=sqj[:], in_=Of[w][:], func=AF.Square,
                                                                         accum_out=sm[:, 1, k:k + 1]),
                                 reads=[Ofb[w]], writes=[sqb, smb[k]])
                            c.op("dve", lambda e, k=k: e.tensor_scalar(out=sm[:, 2, k:k + 1], in0=sm[:, 1, k:k + 1], scalar1=1.0 / 256.0,
                                                                       scalar2=SUBLN_EPS, op0=ALU.mult, op1=ALU.add),
                                 reads=[smb[k]], writes=[smb[k]])
                            c.op("pool", lambda e, k=k: e.tensor_tensor(out=sm[:, 3, k:k + 1], in0=sm[:, 2, k:k + 1],
                                                                        in1=c.mhalf[:, 0:1], op=ALU.pow),
                                 reads=[smb[k]], writes=[smb[k]])
                            c.op("dve", lambda e, k=k, w=w: e.scalar_tensor_tensor(
                                out=ob[w][:], in0=Of[w][:], scalar=sm[:, 3, k:k + 1], in1=c.subg[:, j, :], op0=ALU.mult, op1=ALU.mult),
                                reads=[Ofb[w], smb[k]], writes=[obb[w]])
                            r0 = s * S + qt * 128
                            c.dma("sp", I["V2"][r0:r0 + 128, h * 256:(h + 1) * 256], ob[w][:], reads=[obb[w]])
    c.barrier()


def final_phase(c, I, do_norm):
    import contextlib
    nc = c.nc
    H = I["H"]
    with contextlib.ExitStack() as st:
        nt = NT(c, st)
        ot = [st.enter_context(SB(nc, "ot%d" % i, [128, D], F32)) for i in range(2)]
        otb = [Buf(), Buf()]
        load_gain(c, nt, I["final_norm"][0:1, :])
        for gt in range(T // 128):
            n = nt.n
            nt.n += 1
            i = n % 2
            k = n % 64
            rows = slice(gt * 128, (gt + 1) * 128)
            c.dma("sp", nt.hb[i][:], H[rows, :], writes=[nt.hbb[i]])
            if do_norm:
                c.op("act", lambda e: e.activation(out=nt.xn[i][:], in_=nt.hb[i][:], func=AF.Square,
                                                   accum_out=nt.st[:, 0, k:k + 1]),
                     reads=[nt.hbb[i]], writes=[nt.xnb[i], nt.stb[k]])
                rstd_col(c, nt, None, k, float(D), EPS, nt.stb[k])
                c.op("dve", lambda e: e.scalar_tensor_tensor(out=ot[i][:], in0=nt.hb[i][:], scalar=nt.st[:, 2, k:k + 1],
                                                             in1=nt.gbc[:], op0=ALU.mult, op1=ALU.mult),
                     reads=[nt.hbb[i], nt.stb[k], nt.gb], writes=[otb[i]])
            else:
                c.op("dve", lambda e: e.tensor_copy(out=ot[i][:], in_=nt.hb[i][:]), reads=[nt.hbb[i]], writes=[otb[i]])
            c.dma("sp", I["out"][rows, :], ot[i][:], reads=[otb[i]])
    c.barrier()


IN_SHAPES = {
    "x": [T, D], "norm_gains": [12, D], "final_norm": [1, D], "ffn_w_in": [8 * D, 2 * FF], "ffn_w_out": [8 * FF, D],
    "hgrn_w_in": [2 * D, 8192], "hgrn_lb": [2, D], "hgrn_norm": [2, D], "hgrn_w_out": [2 * D, D],
    "diff_w_in": [2 * D, 6144], "diff_lambda": [2, 512], "diff_subln": [2, 256], "diff_w_out": [2 * D, D],
    "rel_bias": [32, 8], "cst": [128, CW],
}


def build(nsub=12):
    nc = bass.Bass("TRN2", target_bir_lowering=False)
    I = {k: nc.dram_tensor(k, s, F32, kind="ExternalInput").ap() for k, s in IN_SHAPES.items()}
    I["out"] = nc.dram_tensor("out", [T, D], F32, kind="ExternalOutput").ap()
    I["H"] = nc.dram_tensor("H", [T, D], F32).ap()
    I["BV"] = nc.dram_tensor("BV", [8, 384], F32).ap()
    for nm in ("QT", "KT", "KH", "OG", "ON"):
        I[nm] = nc.dram_tensor(nm, [16, 128, T], BF16).ap()
    I["EE"] = nc.dram_tensor("EE", [16, 128, T // 32], F32).ap()
    I["V"] = nc.dram_tensor("V", [T, D], BF16).ap()
    I["V2"] = nc.dram_tensor("V2", [T, D], BF16).ap()
    I["QK"] = nc.dram_tensor("QK", [32, 128, T], BF16).ap()
    c = Ctx(nc)
    setup(c, I)
    subs = []
    for li in range(DEPTH):
        subs += [("ffn", li, 0), ("mix", li, 0), ("ffn", li, 1)]
    for si, (kind, li, k) in enumerate(subs[:nsub]):
        j = li // 2
        if kind == "ffn":
            ffn_phase(c, I, li, k, first=(si == 0))
        elif li % 2 == 0:
            hgrn_p1(c, I, li, j)
            hgrn_p2(c, I, li, j)
            outproj_phase(c, I, I["hgrn_w_out"][j * D:(j + 1) * D, :], "ON", True)
        else:
            diff_d1(c, I, li, j)
            diff_d2(c, I, li, j)
            outproj_phase(c, I, I["diff_w_out"][j * D:(j + 1) * D, :], "V2", False)
    final_phase(c, I, do_norm=(nsub >= 12))
    return nc


def make_in_maps(inputs, n_cores):
    f = lambda a: np.ascontiguousarray(np.asarray(a, dtype=np.float32))
    shared = {
        "norm_gains": f(inputs["norm_gains"]).reshape(12, D),
        "final_norm": f(inputs["final_norm"]).reshape(1, D),
        "ffn_w_in": f(inputs["ffn_w_in"]).reshape(8 * D, 2 * FF),
        "ffn_w_out": f(inputs["ffn_w_out"]).reshape(8 * FF, D),
        "hgrn_w_in": f(inputs["hgrn_w_in"]).reshape(2 * D, 8192),
        "hgrn_lb": f(inputs["hgrn_lower_bounds"]).reshape(2, D),
        "hgrn_norm": f(inputs["hgrn_norm"]).reshape(2, D),
        "hgrn_w_out": f(inputs["hgrn_w_out"]).reshape(2 * D, D),
        "diff_w_in": f(inputs["diff_w_in"]).reshape(2 * D, 6144),
        "diff_lambda": f(inputs["diff_lambda"]).reshape(2, 512),
        "diff_subln": f(inputs["diff_subln"]).reshape(2, 256),
        "diff_w_out": f(inputs["diff_w_out"]).reshape(2 * D, D),
        "rel_bias": f(inputs["rel_bias"]).reshape(32, 8),
        "cst": make_consts(),
    }
    x = f(inputs["x"])
    maps = []
    for cid in range(n_cores):
        m = dict(shared)
        m["x"] = x[cid * NSEQ:(cid + 1) * NSEQ].reshape(T, D)
        maps.append(m)
    return maps


def kernel(**inputs):
    n_cores = 8
    nc = build(12)
    maps = make_in_maps(inputs, n_cores)
    res = run_bass_kernel_spmd(nc, maps, core_ids=list(range(n_cores)))
    out = np.stack([np.asarray(r["out"]).reshape(NSEQ, S, D) for r in res.results], axis=0)
    return out.reshape(n_cores * NSEQ, S, D).astype(np.float32)
```

```python
import math
import numpy as np
import concourse.bass as bass
import concourse.mybir as mybir
from concourse.bass_utils import run_bass_kernel_spmd

F32 = mybir.dt.float32
BF16 = mybir.dt.bfloat16
AF = mybir.ActivationFunctionType
ALU = mybir.AluOpType
AX = mybir.AxisListType

D = 2048
KC = 16
FF = 5632
S = 2048
NSEQ = 2
T = NSEQ * S
TB = 1024
NBLK = T // TB
NTT = TB // 128
DEPTH = 4
G = 2
GCH = 22
EPS = 1e-6
SUBLN_EPS = 1e-5
QSCALE = 128 ** -0.5
NEG = -30000.0
C_ID, C_BD, C_CH, C_J, C_RS, C_OH = 0, 128, 256, 260, 388, 900
CW = 1284
CWP = 388
SAME_ENG_SYNC = True


def _t5_bucket(n):
    if n < 16:
        return n
    v = 16 + int(np.float32(np.log(np.float32(n) / np.float32(16)) / np.float32(math.log(8.0)) * np.float32(16)))
    return min(v, 31)


def make_consts():
    c = np.zeros((128, CW), np.float32)
    c[:, C_ID:C_ID + 128] = np.eye(128, dtype=np.float32)
    s = np.arange(128)[:, None]
    t = np.arange(128)[None, :]
    c[:, C_BD:C_BD + 128] = ((s // 32 == t // 32) & (s <= t)).astype(np.float32)
    for j in range(4):
        c[:, C_CH + j] = (np.arange(128) // 32 == j).astype(np.float32)
    rs = np.ones(512, np.float32)
    rs[0::32] = 0.0
    c[:, C_RS:C_RS + 512] = rs[None, :]
    for i in range(384):
        dist = i - 127
        if dist < 0:
            c[32, C_OH + i] = 1.0
        else:
            c[_t5_bucket(dist), C_OH + i] += 1.0
        c[31, C_OH + i] -= 1.0
    c[:, C_J:C_J + 128] = np.eye(128, dtype=np.float32)[::-1]
    return c


class Buf:
    __slots__ = ("w", "r")

    def __init__(self):
        self.w = None
        self.r = {}


class Ctx:
    def __init__(self, nc):
        self.nc = nc
        self.E = {"pe": nc.tensor, "act": nc.scalar, "dve": nc.vector, "pool": nc.gpsimd, "sp": nc.sync}
        self.prog = {n: [nc.alloc_semaphore("prog_" + n), 0] for n in ("pe", "act", "dve", "pool")}
        self.slots = {q: [[nc.alloc_semaphore("dq_%s_%d" % (q, i)), 0] for i in range(nsl)]
                      for q, nsl in (("sp", 16), ("pool", 4))}
        self.nslot = {"sp": 0, "pool": 0}
        self.waited = {n: {} for n in self.E}

    def _wait_raw(self, en, sem, val):
        w = self.waited[en]
        if w.get(id(sem), 0) >= val:
            return
        self.E[en].wait_ge(sem, val)
        w[id(sem)] = val

    def _wait(self, en, dep):
        if dep is None:
            return
        sem, val, src = dep
        if src == en and (en == "pe" or not SAME_ENG_SYNC):
            return
        self._wait_raw(en, sem, val)

    def _deps(self, en, reads, writes):
        for b in reads:
            self._wait(en, b.w)
        for b in writes:
            self._wait(en, b.w)
            for d in b.r.values():
                self._wait(en, d)

    def _commit(self, dep, reads, writes):
        for b in reads:
            b.r[id(dep[0])] = dep
        for b in writes:
            b.w = dep
            b.r = {}

    def op(self, en, fn, reads=(), writes=()):
        self._deps(en, reads, writes)
        ins = fn(self.E[en])
        p = self.prog[en]
        p[1] += 1
        ins.then_inc(p[0], 1)
        dep = (p[0], p[1], en)
        self._commit(dep, reads, writes)
        return dep

    def dma(self, q, out, in_, reads=(), writes=(), **kw):
        sl = self.slots[q][self.nslot[q]]
        self.nslot[q] = (self.nslot[q] + 1) % len(self.slots[q])
        if sl[1] > 0:
            self._wait_raw(q, sl[0], sl[1])
        self._deps(q, reads, writes)
        ins = self.E[q].dma_start(out=out, in_=in_, **kw)
        sl[1] += 16
        ins.then_inc(sl[0], 16)
        dep = (sl[0], sl[1], "dma")
        self._commit(dep, reads, writes)
        return dep

    def barrier(self):
        sems = [p for p in self.prog.values()] + [s for q in self.slots.values() for s in q]
        for en in self.E:
            for sem, val in sems:
                if val > 0:
                    self._wait_raw(en, sem, val)


_SBN = [0]


def SB(nc, name, shape, dt):
    _SBN[0] += 1
    return nc.sbuf_tensor("%s_u%d" % (name, _SBN[0]), shape, dt)


def bank(c, b):
    return c.ps[:, b * 512:(b + 1) * 512]


def setup(c, I):
    nc = c.nc
    A = lambda nm, shp, dt: nc.alloc_sbuf_tensor("s_" + nm, shp, dt)
    c.cst = A("cst", [128, CWP], F32)
    c.identb = A("identb", [128, 128], BF16)
    c.Jb = A("Jb", [128, 128], BF16)
    c.ones32 = A("ones32", [128, 128], F32)
    c.mhalf = A("mhalf", [128, 1], F32)
    c.lbraw = A("lbraw", [128, 2, 16], F32)
    c.lb = A("lb", [128, 2, 16], F32)
    c.oml = A("oml", [128, 2, 16], F32)
    c.hg = A("hg", [128, 2, 16], F32)
    c.lams = A("lams", [128, 4], F32)
    c.lame = A("lame", [128, 4], F32)
    c.nlam = A("nlam", [128, 2], F32)
    c.b31 = A("b31", [128, 8], F32)
    c.relb = A("relb", [64, 8], F32)
    c.vecs = A("vecs", [8, 384], F32)
    c.Rb = A("Rb", [128, 8, 2, 128], BF16)
    c.subg = A("subg", [128, 2, 256], F32)
    c.ps = nc.alloc_psum_tensor("ps", [128, 4096], F32)
    c.psb = [Buf() for _ in range(8)]
    import contextlib
    stk = contextlib.ExitStack()
    c.lamraw = stk.enter_context(SB(nc, "lamraw", [128, 1024], F32))
    c.lamp = stk.enter_context(SB(nc, "lamp", [128, 4, 128], F32))
    oh = stk.enter_context(SB(nc, "oh", [33, 384], F32))
    b = Buf()
    c.dma("sp", c.cst[:], I["cst"][:, 0:CWP], writes=[b])
    c.dma("sp", oh[:], I["cst"][0:33, C_OH:C_OH + 384], writes=[b])
    c.op("dve", lambda e: e.tensor_copy(out=c.identb[:], in_=c.cst[:, C_ID:C_ID + 128]), reads=[b])
    c.op("dve", lambda e: e.tensor_copy(out=c.Jb[:], in_=c.cst[:, C_J:C_J + 128]), reads=[b])
    c.op("dve", lambda e: e.memset(c.ones32[:], 1.0))
    c.op("dve", lambda e: e.memset(c.mhalf[:], -0.5))
    with nc.allow_non_contiguous_dma(reason="one-time small param layout"):
        b1 = Buf()
        c.dma("sp", c.lbraw[:], I["hgrn_lb"].rearrange("l (h p) -> p l h", p=128), writes=[b1])
        b2 = Buf()
        c.dma("sp", c.hg[:], I["hgrn_norm"].rearrange("l (h p) -> p l h", p=128), writes=[b2])
    blb = Buf()
    c.op("dve", lambda e: e.memset(c.lb[:], 0.0), writes=[blb])
    c.op("dve", lambda e: e.tensor_tensor(out=c.lbraw[:, 0, :], in0=c.lbraw[:, 1, :], in1=c.lbraw[:, 0, :],
                                          op=ALU.subtract), reads=[b1], writes=[b1])
    c.op("act", lambda e: e.activation(out=c.lb[:, 1, :], in_=c.lbraw[:, 0, :], func=AF.Sigmoid),
         reads=[b1], writes=[blb])
    c.op("dve", lambda e: e.tensor_scalar(out=c.oml[:], in0=c.lb[:], scalar1=-1.0, scalar2=1.0,
                                          op0=ALU.mult, op1=ALU.add), reads=[blb])
    bl = Buf()
    c.dma("sp", c.lamraw[:], I["diff_lambda"].rearrange("a b -> (a b)").partition_broadcast(128), writes=[bl])
    lv = c.lamraw[:].rearrange("p (j q w d) -> p j q w d", j=2, q=2, w=2)
    for j in range(2):
        for q in range(2):
            c.op("dve", lambda e, j=j, q=q: e.tensor_tensor(out=c.lamp[:, j * 2 + q, :], in0=lv[:, j, q, 0, :],
                                                            in1=lv[:, j, q, 1, :], op=ALU.mult), reads=[bl], writes=[bl])
    c.op("dve", lambda e: e.reduce_sum(out=c.lams[:], in_=c.lamp[:], axis=AX.X), reads=[bl], writes=[bl])
    c.op("act", lambda e: e.activation(out=c.lame[:], in_=c.lams[:], func=AF.Exp), reads=[bl], writes=[bl])
    for j in range(2):
        li = 2 * j + 1
        lam_init = 0.8 - 0.6 * math.exp(-0.3 * li)
        c.op("dve", lambda e, j=j: e.tensor_tensor(out=c.nlam[:, j:j + 1], in0=c.lame[:, 2 * j + 1:2 * j + 2],
                                                   in1=c.lame[:, 2 * j:2 * j + 1], op=ALU.subtract), reads=[bl], writes=[bl])
        c.op("dve", lambda e, j=j, v=lam_init: e.tensor_scalar(out=c.nlam[:, j:j + 1], in0=c.nlam[:, j:j + 1],
                                                               scalar1=-v, scalar2=None, op0=ALU.add), reads=[bl], writes=[bl])
    bb = Buf()
    c.dma("sp", c.b31[:], I["rel_bias"][31:32, :].partition_broadcast(128), writes=[bb])
    br = Buf()
    c.op("dve", lambda e: e.memset(c.relb[:], NEG), writes=[br])
    c.dma("sp", c.relb[0:32, :], I["rel_bias"][:, :], writes=[br])
    c.op("pe", lambda e: e.matmul(c.ps[0:8, 0:384], lhsT=c.relb[0:33, :], rhs=oh[:],
                                  start=True, stop=True), reads=[br, b], writes=[c.psb[0]])
    bv = Buf()
    c.op("dve", lambda e: e.tensor_copy(out=c.vecs[:], in_=c.ps[0:8, 0:384]), reads=[c.psb[0]], writes=[bv])
    bd = Buf()
    c.dma("sp", I["BV"][:, :], c.vecs[:], reads=[bv], writes=[bd])
    for h in range(8):
        for dl in range(2):
            src = bass.AP(I["BV"].tensor, h * 384 + 128 * dl, [[1, 128], [1, 128]])
            c.dma("pool", c.Rb[:, h, dl, :], src, reads=[bd])
    bs = Buf()
    c.dma("sp", c.subg[:].rearrange("p a b -> p (a b)"),
          I["diff_subln"].rearrange("a b -> (a b)").partition_broadcast(128), writes=[bs])
    for j in range(2):
        li = 2 * j + 1
        lam_init = 0.8 - 0.6 * math.exp(-0.3 * li)
        c.op("dve", lambda e, j=j, v=lam_init: e.tensor_scalar(out=c.subg[:, j, :], in0=c.subg[:, j, :],
                                                               scalar1=1.0 - v, scalar2=None, op0=ALU.mult),
             reads=[bs], writes=[bs])
    c.Hb = [[Buf() for _ in range(4)] for _ in range(T // 128)]
    c.barrier()
    stk.close()


class NT:
    def __init__(self, c, stack, with_norm=True):
        nc = c.nc
        self.hb = [stack.enter_context(SB(nc, "nt_hb%d" % i, [128, D], F32)) for i in range(2)] if with_norm else None
        self.xn = [stack.enter_context(SB(nc, "nt_xn%d" % i, [128, D], BF16)) for i in range(2)]
        self.gbc = stack.enter_context(SB(nc, "nt_gbc", [128, D], F32)) if with_norm else None
        self.st = stack.enter_context(SB(nc, "nt_st", [128, 3, 64], F32))
        self.hbb = [Buf(), Buf()]
        self.xnb = [Buf(), Buf()]
        self.stb = [Buf() for _ in range(64)]
        self.gb = Buf()
        self.n = 0
        self.pT = c.ps[:, 3072:4096].bitcast(BF16).rearrange("p (k t) -> p k t", k=16)
        self.pTb = c.psb[6]


def rstd_col(c, nt, ssap, k, dim, eps, stbuf):
    c.op("dve", lambda e: e.tensor_scalar(out=nt.st[:, 1, k:k + 1], in0=nt.st[:, 0, k:k + 1], scalar1=1.0 / dim,
                                          scalar2=eps, op0=ALU.mult, op1=ALU.add), reads=[stbuf], writes=[stbuf])
    c.op("pool", lambda e: e.tensor_tensor(out=nt.st[:, 2, k:k + 1], in0=nt.st[:, 1, k:k + 1], in1=c.mhalf[:, 0:1],
                                           op=ALU.pow), reads=[stbuf], writes=[stbuf])


def load_gain(c, nt, gain_ap):
    c.dma("sp", nt.gbc[:], gain_ap.partition_broadcast(128), writes=[nt.gb])


def norm_transpose_block(c, nt, src_fn, src_bufs_fn, xT, xTb):
    base = nt.n
    nt.n += NTT

    def stage1(tt):
        n = base + tt
        i = n % 2
        k = n % 64
        c.dma("sp", nt.hb[i][:], src_fn(tt), reads=src_bufs_fn(tt), writes=[nt.hbb[i]])
        c.op("act", lambda e: e.activation(out=nt.xn[i][:], in_=nt.hb[i][:], func=AF.Square,
                                           accum_out=nt.st[:, 0, k:k + 1]),
             reads=[nt.hbb[i]], writes=[nt.xnb[i], nt.stb[k]])
        rstd_col(c, nt, None, k, float(D), EPS, nt.stb[k])
        c.op("dve", lambda e: e.scalar_tensor_tensor(out=nt.xn[i][:], in0=nt.hb[i][:], scalar=nt.st[:, 2, k:k + 1],
                                                     in1=nt.gbc[:], op0=ALU.mult, op1=ALU.mult),
             reads=[nt.hbb[i], nt.stb[k], nt.gb], writes=[nt.xnb[i]])
    stage1(0)
    for tt in range(NTT):
        if tt + 1 < NTT:
            stage1(tt + 1)
        transpose_tile(c, nt, (base + tt) % 2, tt, xT, xTb)


def transpose_tile(c, nt, i, tt, xT, xTb):
    def f(e):
        ins = None
        for kc in range(KC):
            ins = e.transpose(out=nt.pT[:, kc, :], in_=nt.xn[i][:, kc * 128:(kc + 1) * 128], identity=c.identb[:])
        return ins
    c.op("pe", f, reads=[nt.xnb[i]], writes=[nt.pTb, c.psb[7]])
    c.op("act", lambda e: e.activation(out=xT[:, :, tt * 128:(tt + 1) * 128], in_=nt.pT, func=AF.Copy),
         reads=[nt.pTb], writes=[xTb])


class GO:
    def __init__(self, c, stack, nkmax):
        nc = c.nc
        self.wo = [stack.enter_context(SB(nc, "go_wo%d" % i, [128, nkmax, 512], BF16)) for i in range(2)]
        self.ht = [stack.enter_context(SB(nc, "go_ht%d" % i, [128, 512], F32)) for i in range(2)]
        self.ho = [stack.enter_context(SB(nc, "go_ho%d" % i, [128, 512], F32)) for i in range(2)]
        self.wob = [Buf(), Buf()]
        self.htb = [Buf(), Buf()]
        self.hob = [Buf(), Buf()]
        self.n = 0
        self.ns = 0


def gemm_out(c, go, lhsT, lhsTb, nk, W, scale, blk, src_ap, H):
    Wv = W.rearrange("(c p) n -> p c n", p=128)

    def load(dq):
        i = go.ns % 2
        go.ns += 1
        c.dma("pool", go.wo[i][:, 0:nk, :], Wv[:, :, dq * 512:(dq + 1) * 512], writes=[go.wob[i]])
        return i
    its = [(dq, tt) for dq in range(4) for tt in range(NTT)]

    def load_ht(k):
        dq, tt = its[k]
        i = (go.n + (k - cur_k[0])) % 2
        gt = blk * NTT + tt
        c.dma("sp", go.ht[i][:], src_ap[gt * 128:(gt + 1) * 128, dq * 512:(dq + 1) * 512],
              reads=[c.Hb[gt][dq]], writes=[go.htb[i]])
    cur_k = [0]
    cur = load(0)
    load_ht(0)
    nxt = None
    for k, (dq, tt) in enumerate(its):
        if tt == 0:
            if dq > 0:
                cur = nxt
            nxt = load(dq + 1) if dq < 3 else None
        cur_k[0] = k
        if k + 1 < len(its):
            load_ht(k + 1)
        n = go.n
        go.n += 1
        i = n % 2
        gt = blk * NTT + tt
        rows = slice(gt * 128, (gt + 1) * 128)
        cols = slice(dq * 512, (dq + 1) * 512)
        pb = 4 + i

        def f(e, pb=pb, tt=tt, cur=cur):
            ins = None
            for j in range(nk):
                ins = e.matmul(bank(c, pb), lhsT=lhsT[:, j, tt * 128:(tt + 1) * 128], rhs=go.wo[cur][:, j, :],
                               start=(j == 0), stop=(j == nk - 1))
            return ins
        c.op("pe", f, reads=[lhsTb, go.wob[cur]], writes=[c.psb[pb]])
        c.op("dve", lambda e, pb=pb, i=i: e.scalar_tensor_tensor(out=go.ho[i][:], in0=bank(c, pb), scalar=float(scale),
                                                                 in1=go.ht[i][:], op0=ALU.mult, op1=ALU.add),
             reads=[c.psb[pb], go.htb[i]], writes=[go.hob[i]])
        c.dma("sp", H[rows, cols], go.ho[i][:], reads=[go.hob[i]], writes=[c.Hb[gt][dq]])


def h_rows_fn(src, blk):
    return lambda tt: src[(blk * NTT + tt) * 128:(blk * NTT + tt + 1) * 128, :]


def h_bufs_fn(c, blk):
    return lambda tt: c.Hb[blk * NTT + tt]


def ffn_phase(c, I, li, k, first):
    import contextlib
    nc = c.nc
    idx = li * 2 + k
    w_in = I["ffn_w_in"][idx * D:(idx + 1) * D, :]
    w_out = I["ffn_w_out"][idx * FF:(idx + 1) * FF, :]
    gain = I["norm_gains"][li * 3 + (0 if k == 0 else 2):li * 3 + (0 if k == 0 else 2) + 1, :]
    Wv = w_in.rearrange("(kc p) n -> p kc n", p=128)
    H = I["H"]
    with contextlib.ExitStack() as st:
        nt = NT(c, st)
        go = GO(c, st, GCH)
        xT = st.enter_context(SB(nc, "xT", [128, KC, TB], BF16))
        aT = st.enter_context(SB(nc, "aT", [128, GCH, TB], BF16))
        wg = [st.enter_context(SB(nc, "wg%d" % i, [128, KC, 256], BF16)) for i in range(2)]
        wu = [st.enter_context(SB(nc, "wu%d" % i, [128, KC, 256], BF16)) for i in range(2)]
        sg = [st.enter_context(SB(nc, "sg%d" % i, [128, 512], BF16)) for i in range(2)]
        xTb, aTb = Buf(), Buf()
        wb = [Buf(), Buf()]
        sgb = [Buf(), Buf()]
        load_gain(c, nt, gain)
        ns = 0
        nu = 0
        slabs = [(2 * i, 2) for i in range(11)]

        def loadw(g, si):
            nonlocal ns
            i = ns % 2
            ns += 1
            c0, n = slabs[si]
            ca = (g * GCH + c0) * 128
            c.dma("pool", wg[i][:, :, 0:n * 128], Wv[:, :, ca:ca + n * 128], writes=[wb[i]])
            c.dma("pool", wu[i][:, :, 0:n * 128], Wv[:, :, FF + ca:FF + ca + n * 128], writes=[wb[i]])
            return i
        for blk in range(NBLK):
            src = I["x"] if first else H
            norm_transpose_block(c, nt, h_rows_fn(src, blk), h_bufs_fn(c, blk), xT, xTb)
            for g in range(G):
                cur = loadw(g, 0)
                for si, (c0, n) in enumerate(slabs):
                    nxt = loadw(g, si + 1) if si + 1 < len(slabs) else None
                    for ch in range(n):
                        cl = c0 + ch
                        for th in range(2):
                            u = nu % 2
                            nu += 1

                            def fg(e, w=wg[cur], pb=u, ch=ch, th=th):
                                ins = None
                                for kc in range(KC):
                                    ins = e.matmul(bank(c, pb), lhsT=w[:, kc, ch * 128:(ch + 1) * 128],
                                                   rhs=xT[:, kc, th * 512:(th + 1) * 512], start=(kc == 0), stop=(kc == KC - 1))
                                return ins
                            c.op("pe", fg, reads=[wb[cur], xTb], writes=[c.psb[u]])
                            c.op("pe", lambda e, th=th, ch=ch, u=u, cur=cur: fg(e, w=wu[cur], pb=2 + u, ch=ch, th=th),
                                 reads=[wb[cur], xTb], writes=[c.psb[2 + u]])
                            c.op("act", lambda e, u=u: e.activation(out=sg[u][:], in_=bank(c, u), func=AF.Silu),
                                 reads=[c.psb[u]], writes=[sgb[u]])
                            c.op("dve", lambda e, u=u, cl=cl, th=th: e.tensor_tensor(
                                out=aT[:, cl, th * 512:(th + 1) * 512], in0=sg[u][:], in1=bank(c, 2 + u), op=ALU.mult),
                                reads=[sgb[u], c.psb[2 + u]], writes=[aTb])
                    cur = nxt
                srcg = I["x"] if (first and g == 0) else H
                gemm_out(c, go, aT, aTb, GCH, w_out[g * GCH * 128:(g + 1) * GCH * 128, :], 0.5, blk, srcg, H)
    c.barrier()


def vproj(c, I, st_bufs, Wv, col0, xT, xTb, blk):
    wv, wvb, vo, vob, cnt = st_bufs

    def load(dq):
        i = cnt[0] % 2
        cnt[0] += 1
        c.dma("pool", wv[i][:], Wv[:, :, col0 + dq * 512:col0 + (dq + 1) * 512], writes=[wvb[i]])
        return i
    cur = load(0)
    for dq in range(4):
        nxt = load(dq + 1) if dq < 3 else None
        for tt in range(NTT):
            n = cnt[1]
            cnt[1] += 1
            i = n % 2
            pb = i

            def f(e, pb=pb, tt=tt, cur=cur):
                ins = None
                for kc in range(KC):
                    ins = e.matmul(bank(c, pb), lhsT=xT[:, kc, tt * 128:(tt + 1) * 128], rhs=wv[cur][:, kc, :],
                                   start=(kc == 0), stop=(kc == KC - 1))
                return ins
            c.op("pe", f, reads=[xTb, wvb[cur]], writes=[c.psb[pb]])
            c.op("act", lambda e, pb=pb, i=i: e.activation(out=vo[i][:], in_=bank(c, pb), func=AF.Copy),
                 reads=[c.psb[pb]], writes=[vob[i]])
            gt = blk * NTT + tt
            c.dma("sp", I["V"][gt * 128:(gt + 1) * 128, dq * 512:(dq + 1) * 512], vo[i][:], reads=[vob[i]])
        cur = nxt


def hgrn_p1(c, I, li, j):
    import contextlib
    nc = c.nc
    W = I["hgrn_w_in"][j * D:(j + 1) * D, :]
    Wv = W.rearrange("(kc p) n -> p kc n", p=128)
    gain = I["norm_gains"][li * 3 + 1:li * 3 + 2, :]
    H = I["H"]
    with contextlib.ExitStack() as st:
        nt = NT(c, st)
        xT = st.enter_context(SB(nc, "xT", [128, KC, TB], BF16))
        wq = [st.enter_context(SB(nc, "wq%d" % i, [128, KC, 3, 256], BF16)) for i in range(2)]
        wv = [st.enter_context(SB(nc, "wv%d" % i, [128, KC, 512], BF16)) for i in range(2)]
        vo = [st.enter_context(SB(nc, "vo%d" % i, [128, 512], BF16)) for i in range(2)]
        tmp = [st.enter_context(SB(nc, "tmp%d" % i, [128, 5, 512], F32)) for i in range(2)]
        ob = [st.enter_context(SB(nc, "ob%d" % i, [128, 4, 512], BF16)) for i in range(2)]
        eo = [st.enter_context(SB(nc, "eo%d" % i, [128, 16], F32)) for i in range(2)]
        xTb = Buf()
        wqb = [Buf(), Buf()]
        tb = [Buf(), Buf()]
        obb = [Buf(), Buf()]
        vst = (wv, [Buf(), Buf()], vo, [Buf(), Buf()], [0, 0])
        load_gain(c, nt, gain)
        ns = [0]
        nu = 0

        def loadw(hp):
            i = ns[0] % 2
            ns[0] += 1
            for a, base in enumerate((0, 2048, 6144)):
                c.dma("pool", wq[i][:, :, a, :], Wv[:, :, base + hp * 256:base + (hp + 1) * 256], writes=[wqb[i]])
            return i
        rsmt = st.enter_context(SB(nc, "rsm", [128, 512], F32))
        rsb = Buf()
        c.dma("sp", rsmt[:], I["cst"][:, C_RS:C_RS + 512], writes=[rsb])
        rsm = rsmt[:]
        for blk in range(NBLK):
            norm_transpose_block(c, nt, h_rows_fn(H, blk), h_bufs_fn(c, blk), xT, xTb)
            cur = loadw(0)
            for hp in range(8):
                nxt = loadw(hp + 1) if hp < 7 else None
                for hh in range(2):
                    hd = hp * 2 + hh
                    for th in range(2):
                        u = nu % 2
                        nu += 1
                        tk = slice(blk * TB + th * 512, blk * TB + (th + 1) * 512)
                        for a in range(3):
                            def f(e, a=a, pb=3 * u + a, hh=hh, th=th, cur=cur):
                                ins = None
                                for kc in range(KC):
                                    ins = e.matmul(bank(c, pb), lhsT=wq[cur][:, kc, a, hh * 128:(hh + 1) * 128],
                                                   rhs=xT[:, kc, th * 512:(th + 1) * 512], start=(kc == 0), stop=(kc == KC - 1))
                                return ins
                            c.op("pe", f, reads=[wqb[cur], xTb], writes=[c.psb[3 * u + a]])
                        pq, pz, pg = bank(c, 3 * u), bank(c, 3 * u + 1), bank(c, 3 * u + 2)
                        t = tmp[u]
                        o = ob[u]
                        lbc = c.lb[:, j, hd:hd + 1]
                        omc = c.oml[:, j, hd:hd + 1]
                        R = [c.psb[3 * u], c.psb[3 * u + 1], c.psb[3 * u + 2]]
                        TBF = [tb[u]]
                        OBF = [obb[u]]
                        c.op("act", lambda e: e.activation(out=t[:, 0, :], in_=pz, func=AF.Sigmoid), reads=R, writes=TBF)
                        c.op("act", lambda e: e.activation(out=t[:, 1, :], in_=pz, func=AF.Sigmoid, scale=-1.0), reads=R, writes=TBF)
                        c.op("act", lambda e: e.activation(out=t[:, 2, :], in_=pq, func=AF.Silu), reads=R, writes=TBF)
                        c.op("act", lambda e: e.activation(out=o[:, 3, :], in_=pg, func=AF.Silu), reads=R, writes=OBF)
                        c.op("dve", lambda e: e.tensor_scalar(out=t[:, 0, :], in0=t[:, 0, :], scalar1=omc, scalar2=lbc,
                                                              op0=ALU.mult, op1=ALU.add), reads=TBF, writes=TBF)
                        c.op("act", lambda e: e.activation(out=t[:, 0, :], in_=t[:, 0, :], func=AF.Ln), reads=TBF, writes=TBF)
                        c.op("dve", lambda e: e.tensor_tensor_scan(out=t[:, 3, :], data0=rsm, data1=t[:, 0, :], initial=0.0,
                                                                   op0=ALU.mult, op1=ALU.add), reads=TBF + [rsb], writes=TBF)
                        c.op("act", lambda e: e.activation(out=t[:, 0, :], in_=t[:, 3, :], func=AF.Exp), reads=TBF, writes=TBF)
                        c.op("act", lambda e: e.activation(out=t[:, 4, :], in_=t[:, 3, :], func=AF.Exp, scale=-1.0), reads=TBF, writes=TBF)
                        c.op("dve", lambda e: e.tensor_tensor(out=o[:, 0, :], in0=t[:, 2, :], in1=t[:, 0, :], op=ALU.mult),
                             reads=TBF, writes=OBF)
                        c.op("dve", lambda e: e.scalar_tensor_tensor(out=t[:, 1, :], in0=t[:, 1, :], scalar=omc, in1=t[:, 4, :],
                                                                     op0=ALU.mult, op1=ALU.mult), reads=TBF, writes=TBF)
                        c.op("act", lambda e: e.activation(out=o[:, 1, :], in_=t[:, 1, :], func=AF.Copy), reads=TBF, writes=OBF)
                        ev = t[:, 0, :].rearrange("p (c t) -> p c t", t=32)[:, :, 31:32]
                        c.op("dve", lambda e: e.tensor_tensor(out=o[:, 2, :].rearrange("p (c t) -> p c t", t=32),
                                                              in0=t[:, 1, :].rearrange("p (c t) -> p c t", t=32),
                                                              in1=ev.to_broadcast([128, 16, 32]), op=ALU.mult), reads=TBF, writes=OBF)
                        c.op("dve", lambda e: e.tensor_copy(out=eo[u][:].rearrange("p (c o) -> p c o", o=1), in_=ev),
                             reads=TBF, writes=OBF)
                        for a, nm in enumerate(("QT", "KT", "KH", "OG")):
                            c.dma("sp", I[nm][hd, :, tk], o[:, a, :], reads=OBF)
                        ck = (blk * TB + th * 512) // 32
                        c.dma("sp", I["EE"][hd, :, ck:ck + 16], eo[u][:], reads=OBF)
                cur = nxt
            vproj(c, I, vst, Wv, 4096, xT, xTb, blk)
    c.barrier()


def hgrn_p2(c, I, li, j):
    import contextlib
    nc = c.nc
    NP = 4
    with contextlib.ExitStack() as st:
        def A(nm, shp, dt):
            return st.enter_context(SB(nc, nm, shp, dt))
        qT = [A("qT%d" % i, [128, S], BF16) for i in range(NP)]
        kT = [A("kT%d" % i, [128, S], BF16) for i in range(NP)]
        khT = [A("khT%d" % i, [128, S], BF16) for i in range(NP)]
        ogT = [A("ogT%d" % i, [128, S], BF16) for i in range(NP)]
        Vh = [A("Vh%d" % i, [128, 16, 128], BF16) for i in range(NP)]
        Eh = [A("Eh%d" % i, [128, 64], F32) for i in range(NP)]
        khm = [A("khm%d" % i, [128, 4, 128], BF16) for i in range(NP)]
        ATm = [A("ATm%d" % i, [128, 128], BF16) for i in range(NP)]
        Sst = [A("Sst%d" % i, [128, 128], F32) for i in range(NP)]
        Sb = [[A("Sb%d_%d" % (i, k), [128, 128], BF16) for k in range(2)] for i in range(NP)]
        oT = [A("oT%d" % i, [128, S], F32) for i in range(NP)]
        osq = A("osq", [128, S], F32)
        rst = A("rst", [128, S], F32)
        onb = [A("onb%d" % i, [128, S], BF16) for i in range(2)]
        epsb = A("epsb", [128, 1], F32)
        inA = [Buf() for _ in range(NP)]
        inB = [Buf() for _ in range(NP)]
        khmb = [Buf() for _ in range(NP)]
        ATb = [Buf() for _ in range(NP)]
        Sbuf = [Buf() for _ in range(NP)]
        Sbb = [[Buf(), Buf()] for _ in range(NP)]
        oTb = [Buf() for _ in range(NP)]
        osqb, rstb = Buf(), Buf()
        onbb = [Buf(), Buf()]
        pAb = [c.psb[0] for _ in range(NP)]
        pTb = [c.psb[6] for _ in range(NP)]
        pA = [c.ps[:, p * 128:(p + 1) * 128] for p in range(NP)]
        pU2 = [[c.ps[:, bk * 512 + p * 128:bk * 512 + (p + 1) * 128] for p in range(NP)] for bk in (1, 7)]
        pUb2 = [c.psb[1], c.psb[7]]
        pO = [c.ps[:, (2 + p) * 512:(2 + p) * 512 + 128] for p in range(NP)]
        pOb = [c.psb[2 + p] for p in range(NP)]
        pT = [c.ps[:, 3072 + p * 64:3072 + (p + 1) * 64].bitcast(BF16) for p in range(NP)]
        c.op("dve", lambda e: e.memset(epsb[:], EPS))
        pairs = [(s, hd) for s in range(NSEQ) for hd in range(16)]
        chm = c.cst[:, C_CH:C_CH + 4]

        def load(p, s, hd):
            tk = slice(s * S, (s + 1) * S)
            for buf, nm in ((qT, "QT"), (kT, "KT"), (khT, "KH")):
                c.dma("sp", buf[p][:], I[nm][hd, :, tk], writes=[inA[p]])
            c.dma("sp", Eh[p][:], I["EE"][hd, :, s * 64:(s + 1) * 64], writes=[inA[p]])
            c.dma("sp", Vh[p][:], I["V"][tk, hd * 128:(hd + 1) * 128].rearrange("(t p) v -> p t v", p=128), writes=[inA[p]])
            c.dma("sp", ogT[p][:], I["OG"][hd, :, tk], writes=[inB[p]])
        ngr = len(pairs) // NP
        for p in range(NP):
            load(p, *pairs[p])
        nend = 0
        for gi in range(ngr):
            grp = pairs[gi * NP:(gi + 1) * NP]
            for p in range(NP):
                c.op("dve", lambda e, p=p: e.memset(Sst[p][:], 0.0), writes=[Sbuf[p]])
                c.op("dve", lambda e, p=p: e.memset(Sb[p][0][:], 0.0), writes=[Sbb[p][0]])
            for ti in range(16):
                tsl = slice(ti * 128, (ti + 1) * 128)
                for p in range(NP):
                    IB = [inA[p]]
                    c.op("pe", lambda e, p=p: e.transpose(out=pT[p], in_=khT[p][:, tsl], identity=c.identb[:]),
                         reads=IB, writes=[pTb[p]])
                    c.op("pe", lambda e, p=p: e.matmul(pA[p], lhsT=kT[p][:, tsl], rhs=qT[p][:, tsl], start=True, stop=True),
                         reads=IB, writes=[pAb[p]])
                for p in range(NP):
                    c.op("dve", lambda e, p=p: e.tensor_tensor(
                        out=khm[p][:], in0=pT[p].rearrange("p (o d) -> p o d", o=1).to_broadcast([128, 4, 128]),
                        in1=chm.rearrange("p (j o) -> p j o", o=1).to_broadcast([128, 4, 128]), op=ALU.mult),
                        reads=[pTb[p]], writes=[khmb[p]])
                    c.op("dve", lambda e, p=p: e.tensor_tensor(out=ATm[p][:], in0=pA[p], in1=c.cst[:, C_BD:C_BD + 128], op=ALU.mult),
                         reads=[pAb[p]], writes=[ATb[p]])
                for p in range(NP):
                    c.op("pe", lambda e, p=p: e.matmul(pO[p], lhsT=Vh[p][:, ti, :], rhs=ATm[p][:], start=True, stop=False),
                         reads=[inA[p], ATb[p]], writes=[pOb[p]])
                for jj in range(4):
                    ch = ti * 4 + jj
                    sc = ch % 2
                    sn = (ch + 1) % 2
                    pU = pU2[ch % 2]
                    pUb = [pUb2[ch % 2]] * NP
                    for p in range(NP):
                        c.op("pe", lambda e, p=p: e.matmul(pO[p][:, jj * 32:(jj + 1) * 32], lhsT=Sb[p][sc][:],
                                                           rhs=qT[p][:, ch * 32:(ch + 1) * 32], start=False, stop=(jj == 3)),
                             reads=[inA[p], Sbb[p][sc]], writes=[pOb[p]])
                        c.op("pe", lambda e, p=p: e.matmul(pU[p], lhsT=khm[p][:, jj, :], rhs=Vh[p][:, ti, :], start=True, stop=True),
                             reads=[inA[p], khmb[p]], writes=[pUb[p]])
                    for p in range(NP):
                        c.op("dve", lambda e, p=p: e.scalar_tensor_tensor(
                            out=Sst[p][:], in0=Sst[p][:], scalar=Eh[p][:, ch:ch + 1], in1=pU[p], op0=ALU.mult, op1=ALU.add),
                            reads=[inA[p], pUb[p]], writes=[Sbuf[p]])
                        c.op("act", lambda e, p=p: e.activation(out=Sb[p][sn][:], in_=Sst[p][:], func=AF.Copy),
                             reads=[Sbuf[p]], writes=[Sbb[p][sn]])
                for p in range(NP):
                    c.op("act", lambda e, p=p: e.activation(out=oT[p][:, tsl], in_=pO[p], func=AF.Copy),
                         reads=[pOb[p]], writes=[oTb[p]])
            if gi + 1 < ngr:
                nxt = pairs[(gi + 1) * NP:(gi + 2) * NP]
            else:
                nxt = []
            for p, (s, hd) in enumerate(grp):
                w = nend % 2
                nend += 1
                c.op("act", lambda e: e.activation(out=osq[:], in_=oT[p][:], func=AF.Square), reads=[oTb[p]], writes=[osqb])
                for q4 in range(4):
                    sl = slice(q4 * 512, (q4 + 1) * 512)
                    c.op("pe", lambda e: e.matmul(bank(c, 0), lhsT=c.ones32[:], rhs=osq[:, sl], start=True, stop=True),
                         reads=[osqb], writes=[c.psb[0]])
                    c.op("act", lambda e: e.activation(out=rst[:, sl], in_=bank(c, 0), func=AF.Sqrt, scale=1.0 / 128.0,
                                                       bias=epsb[:, 0:1]), reads=[c.psb[0]], writes=[rstb])
                c.op("dve", lambda e: e.reciprocal(out=rst[:], in_=rst[:]), reads=[rstb], writes=[rstb])
                c.op("dve", lambda e: e.tensor_tensor(out=oT[p][:], in0=oT[p][:], in1=rst[:], op=ALU.mult),
                     reads=[rstb, oTb[p]], writes=[oTb[p]])
                c.op("dve", lambda e: e.scalar_tensor_tensor(out=onb[w][:], in0=oT[p][:], scalar=c.hg[:, j, hd:hd + 1], in1=ogT[p][:],
                                                             op0=ALU.mult, op1=ALU.mult), reads=[inB[p], oTb[p]], writes=[onbb[w]])
                c.dma("sp", I["ON"][hd, :, s * S:(s + 1) * S], onb[w][:], reads=[onbb[w]])
                if nxt:
                    load(p, *nxt[p])
    c.barrier()


def outproj_phase(c, I, W, src_name, feature_major):
    import contextlib
    nc = c.nc
    H = I["H"]
    with contextlib.ExitStack() as st:
        go = GO(c, st, KC)
        oT = st.enter_context(SB(nc, "oT", [128, KC, TB], BF16))
        oTb = Buf()
        nt = None if feature_major else NT(c, st, with_norm=False)
        for blk in range(NBLK):
            if feature_major:
                c.dma("sp", oT[:], I[src_name].rearrange("h p t -> p h t")[:, :, blk * TB:(blk + 1) * TB], writes=[oTb])
            else:
                for tt in range(NTT):
                    n = nt.n
                    nt.n += 1
                    i = n % 2
                    gt = blk * NTT + tt
                    c.dma("sp", nt.xn[i][:], I[src_name][gt * 128:(gt + 1) * 128, :], writes=[nt.xnb[i]])
                    transpose_tile(c, nt, i, tt, oT, oTb)
            gemm_out(c, go, oT, oTb, KC, W, 1.0, blk, H, H)
    c.barrier()


def diff_d1(c, I, li, j):
    import contextlib
    nc = c.nc
    W = I["diff_w_in"][j * D:(j + 1) * D, :]
    Wv = W.rearrange("(kc p) n -> p kc n", p=128)
    gain = I["norm_gains"][li * 3 + 1:li * 3 + 2, :]
    H = I["H"]
    with contextlib.ExitStack() as st:
        nt = NT(c, st)
        xT = st.enter_context(SB(nc, "xT", [128, KC, TB], BF16))
        wq = [st.enter_context(SB(nc, "wq%d" % i, [128, KC, 256], BF16)) for i in range(2)]
        wv = [st.enter_context(SB(nc, "wv%d" % i, [128, KC, 512], BF16)) for i in range(2)]
        vo = [st.enter_context(SB(nc, "vo%d" % i, [128, 512], BF16)) for i in range(2)]
        ob = [st.enter_context(SB(nc, "ob%d" % i, [128, 512], BF16)) for i in range(2)]
        xTb = Buf()
        wqb = [Buf(), Buf()]
        obb = [Buf(), Buf()]
        vst = (wv, [Buf(), Buf()], vo, [Buf(), Buf()], [0, 0])
        load_gain(c, nt, gain)
        ns = [0]
        nu = 0

        def loadw(sl):
            i = ns[0] % 2
            ns[0] += 1
            c.dma("pool", wq[i][:], Wv[:, :, sl * 256:(sl + 1) * 256], writes=[wqb[i]])
            return i
        for blk in range(NBLK):
            norm_transpose_block(c, nt, h_rows_fn(H, blk), h_bufs_fn(c, blk), xT, xTb)
            cur = loadw(0)
            for sl in range(16):
                nxt = loadw(sl + 1) if sl < 15 else None
                for ch in range(2):
                    fc = sl * 2 + ch
                    for th in range(2):
                        u = nu % 2
                        nu += 1

                        def f(e, pb=u, ch=ch, th=th, cur=cur):
                            ins = None
                            for kc in range(KC):
                                ins = e.matmul(bank(c, pb), lhsT=wq[cur][:, kc, ch * 128:(ch + 1) * 128],
                                               rhs=xT[:, kc, th * 512:(th + 1) * 512], start=(kc == 0), stop=(kc == KC - 1))
                            return ins
                        c.op("pe", f, reads=[wqb[cur], xTb], writes=[c.psb[u]])
                        sc = QSCALE if fc < 16 else 1.0
                        c.op("act", lambda e, u=u, sc=sc: e.activation(out=ob[u][:], in_=bank(c, u), func=AF.Copy, scale=sc),
                             reads=[c.psb[u]], writes=[obb[u]])
                        tk = slice(blk * TB + th * 512, blk * TB + (th + 1) * 512)
                        c.dma("sp", I["QK"][fc, :, tk], ob[u][:], reads=[obb[u]])
                cur = nxt
            vproj(c, I, vst, Wv, 4096, xT, xTb, blk)
    c.barrier()


def diff_d2(c, I, li, j):
    import contextlib
    nc = c.nc
    with contextlib.ExitStack() as st:
        def A(nm, shp, dt):
            return st.enter_context(SB(nc, nm, shp, dt))
        qk = [A("qk%d" % i, [128, 4, S], BF16) for i in range(2)]
        Vx = [A("Vx%d" % i, [128, 16, 258], BF16) for i in range(2)]
        pT = [A("pT%d" % i, [128, 512], BF16) for i in range(2)]
        O1 = A("O1", [128, 4, 256], F32)
        Of = [A("Of%d" % i, [128, 256], F32) for i in range(2)]
        sqj = A("sqj", [128, 256], BF16)
        ob = [A("odb%d" % i, [128, 256], BF16) for i in range(2)]
        sm = A("sm", [128, 4, 64], F32)
        inb = [Buf(), Buf()]
        pTb = [Buf(), Buf()]
        O1b = Buf()
        Ofb = [Buf(), Buf()]
        obb = [Buf(), Buf()]
        smb = [Buf() for _ in range(64)]
        sqb = Buf()
        for i in range(2):
            c.op("dve", lambda e, i=i: e.memset(Vx[i][:, :, 256:258], 1.0), writes=[inb[i]])
        pairs = [(s, h) for s in range(NSEQ) for h in range(8)]

        def load(pi):
            s, h = pairs[pi]
            i = pi % 2
            tk = slice(s * S, (s + 1) * S)
            for a, fc in enumerate((2 * h, 2 * h + 1, 16 + 2 * h, 16 + 2 * h + 1)):
                c.dma("sp", qk[i][:, a, :], I["QK"][fc, :, tk], writes=[inb[i]])
            c.dma("sp", Vx[i][:, :, 0:256], I["V"][tk, h * 256:(h + 1) * 256].rearrange("(t p) v -> p t v", p=128), writes=[inb[i]])
        load(0)
        nun = 0
        nq = 0
        for pi, (s, h) in enumerate(pairs):
            if pi + 1 < len(pairs):
                load(pi + 1)
            i = pi % 2
            IB = [inb[i]]
            for qg in range(4):
                for ii in range(2):
                    qTi = qk[i][:, ii, :]
                    kTi = qk[i][:, 2 + ii, :]
                    nkt = 4 * qg + 4
                    ubase = nun
                    nun += nkt

                    def geom(kt):
                        q0 = max(kt, 4 * qg)
                        return q0, (q0 - 4 * qg) * 128, (4 * qg + 4 - q0) * 128

                    def emitS(kt):
                        u = (ubase + kt) % 2
                        q0, off, ncol = geom(kt)
                        bts = [(dl, kt + dl) for dl in (0, 1) if 4 * qg <= kt + dl < 4 * qg + 4]

                        def f(e):
                            ins = e.matmul(bank(c, u)[:, off:off + ncol], lhsT=kTi[:, kt * 128:(kt + 1) * 128],
                                           rhs=qTi[:, q0 * 128:q0 * 128 + ncol], start=True, stop=(len(bts) == 0))
                            for bi, (dl, qt) in enumerate(bts):
                                o2 = (qt - 4 * qg) * 128
                                ins = e.matmul(bank(c, u)[:, o2:o2 + 128], lhsT=c.Jb[:], rhs=c.Rb[:, h, dl, :],
                                               start=False, stop=(bi == len(bts) - 1))
                            return ins
                        c.op("pe", f, reads=IB, writes=[c.psb[u]])

                    def emitEP(kt):
                        u = (ubase + kt) % 2
                        q0, off, ncol = geom(kt)
                        c.op("act", lambda e: e.activation(out=pT[u][:, off:off + ncol], in_=bank(c, u)[:, off:off + ncol],
                                                           func=AF.Exp, bias=c.b31[:, h:h + 1]),
                             reads=[c.psb[u]], writes=[pTb[u]])

                        def fpv(e):
                            ins = None
                            for qt in range(q0, 4 * qg + 4):
                                ql = qt - 4 * qg
                                ins = e.matmul(bank(c, 2 + ql)[:, 0:257], lhsT=pT[u][:, ql * 128:(ql + 1) * 128],
                                               rhs=Vx[i][:, kt, 0:257], start=(kt == 0), stop=(kt == qt))
                            return ins
                        c.op("pe", fpv, reads=IB + [pTb[u]], writes=[c.psb[2 + x] for x in range(q0 - 4 * qg, 4)])
                    emitS(0)
                    for kt in range(nkt):
                        if kt + 1 < nkt:
                            emitS(kt + 1)
                        emitEP(kt)
                    for ql in range(4):
                        pb = 2 + ql
                        k = nq % 64
                        if ii == 0:
                            nq += 1
                            c.op("dve", lambda e, pb=pb, k=k: e.reciprocal(out=sm[:, 0, k:k + 1], in_=bank(c, pb)[:, 256:257]),
                                 reads=[c.psb[pb]], writes=[smb[k]])
                            c.op("dve", lambda e, pb=pb, k=k, ql=ql: e.tensor_scalar(
                                out=O1[:, ql, :], in0=bank(c, pb)[:, 0:256], scalar1=sm[:, 0, k:k + 1], scalar2=None, op0=ALU.mult),
                                reads=[c.psb[pb], smb[k]], writes=[O1b])
                        else:
                            nq += 1
                            w = k % 2
                            qt = 4 * qg + ql
                            c.op("dve", lambda e, pb=pb, k=k: e.reciprocal(out=sm[:, 0, k:k + 1], in_=bank(c, pb)[:, 256:257]),
                                 reads=[c.psb[pb]], writes=[smb[k]])
                            c.op("dve", lambda e, k=k: e.tensor_tensor(out=sm[:, 0, k:k + 1], in0=sm[:, 0, k:k + 1],
                                                                       in1=c.nlam[:, j:j + 1], op=ALU.mult),
                                 reads=[smb[k]], writes=[smb[k]])
                            c.op("dve", lambda e, pb=pb, k=k, ql=ql, w=w: e.scalar_tensor_tensor(
                                out=Of[w][:], in0=bank(c, pb)[:, 0:256], scalar=sm[:, 0, k:k + 1], in1=O1[:, ql, :],
                                op0=ALU.mult, op1=ALU.add), reads=[c.psb[pb], smb[k], O1b], writes=[Ofb[w]])
                            c.op("act", lambda e, k=k, w=w: e.activation(out=sqj[:], in_=Of[w][:], func=AF.Square,
                                                                         accum_out=sm[:, 1, k:k + 1]),
                                 reads=[Ofb[w]], writes=[sqb, smb[k]])
                            c.op("dve", lambda e, k=k: e.tensor_scalar(out=sm[:, 2, k:k + 1], in0=sm[:, 1, k:k + 1], scalar1=1.0 / 256.0,
                                                                       scalar2=SUBLN_EPS, op0=ALU.mult, op1=ALU.add),
                                 reads=[smb[k]], writes=[smb[k]])
                            c.op("pool", lambda e, k=k: e.tensor_tensor(out=sm[:, 3, k:k + 1], in0=sm[:, 2, k:k + 1],
                                                                        in1=c.mhalf[:, 0:1], op=ALU.pow),
                                 reads=[smb[k]], writes=[smb[k]])
                            c.op("dve", lambda e, k=k, w=w: e.scalar_tensor_tensor(
                                out=ob[w][:], in0=Of[w][:], scalar=sm[:, 3, k:k + 1], in1=c.subg[:, j, :], op0=ALU.mult, op1=ALU.mult),
                                reads=[Ofb[w], smb[k]], writes=[obb[w]])
                            r0 = s * S + qt * 128
                            c.dma("sp", I["V2"][r0:r0 + 128, h * 256:(h + 1) * 256], ob[w][:], reads=[obb[w]])
    c.barrier()


def final_phase(c, I, do_norm):
    import contextlib
    nc = c.nc
    H = I["H"]
    with contextlib.ExitStack() as st:
        nt = NT(c, st)
        ot = [st.enter_context(SB(nc, "ot%d" % i, [128, D], F32)) for i in range(2)]
        otb = [Buf(), Buf()]
        load_gain(c, nt, I["final_norm"][0:1, :])
        for gt in range(T // 128):
            n = nt.n
            nt.n += 1
            i = n % 2
            k = n % 64
            rows = slice(gt * 128, (gt + 1) * 128)
            c.dma("sp", nt.hb[i][:], H[rows, :], writes=[nt.hbb[i]])
            if do_norm:
                c.op("act", lambda e: e.activation(out=nt.xn[i][:], in_=nt.hb[i][:], func=AF.Square,
                                                   accum_out=nt.st[:, 0, k:k + 1]),
                     reads=[nt.hbb[i]], writes=[nt.xnb[i], nt.stb[k]])
                rstd_col(c, nt, None, k, float(D), EPS, nt.stb[k])
                c.op("dve", lambda e: e.scalar_tensor_tensor(out=ot[i][:], in0=nt.hb[i][:], scalar=nt.st[:, 2, k:k + 1],
                                                             in1=nt.gbc[:], op0=ALU.mult, op1=ALU.mult),
                     reads=[nt.hbb[i], nt.stb[k], nt.gb], writes=[otb[i]])
            else:
                c.op("dve", lambda e: e.tensor_copy(out=ot[i][:], in_=nt.hb[i][:]), reads=[nt.hbb[i]], writes=[otb[i]])
            c.dma("sp", I["out"][rows, :], ot[i][:], reads=[otb[i]])
    c.barrier()


IN_SHAPES = {
    "x": [T, D], "norm_gains": [12, D], "final_norm": [1, D], "ffn_w_in": [8 * D, 2 * FF], "ffn_w_out": [8 * FF, D],
    "hgrn_w_in": [2 * D, 8192], "hgrn_lb": [2, D], "hgrn_norm": [2, D], "hgrn_w_out": [2 * D, D],
    "diff_w_in": [2 * D, 6144], "diff_lambda": [2, 512], "diff_subln": [2, 256], "diff_w_out": [2 * D, D],
    "rel_bias": [32, 8], "cst": [128, CW],
}


def build(nsub=12):
    nc = bass.Bass("TRN2", target_bir_lowering=False)
    I = {k: nc.dram_tensor(k, s, F32, kind="ExternalInput").ap() for k, s in IN_SHAPES.items()}
    I["out"] = nc.dram_tensor("out", [T, D], F32, kind="ExternalOutput").ap()
    I["H"] = nc.dram_tensor("H", [T, D], F32).ap()
    I["BV"] = nc.dram_tensor("BV", [8, 384], F32).ap()
    for nm in ("QT", "KT", "KH", "OG", "ON"):
        I[nm] = nc.dram_tensor(nm, [16, 128, T], BF16).ap()
    I["EE"] = nc.dram_tensor("EE", [16, 128, T // 32], F32).ap()
    I["V"] = nc.dram_tensor("V", [T, D], BF16).ap()
    I["V2"] = nc.dram_tensor("V2", [T, D], BF16).ap()
    I["QK"] = nc.dram_tensor("QK", [32, 128, T], BF16).ap()
    c = Ctx(nc)
    setup(c, I)
    subs = []
    for li in range(DEPTH):
        subs += [("ffn", li, 0), ("mix", li, 0), ("ffn", li, 1)]
    for si, (kind, li, k) in enumerate(subs[:nsub]):
        j = li // 2
        if kind == "ffn":
            ffn_phase(c, I, li, k, first=(si == 0))
        elif li % 2 == 0:
            hgrn_p1(c, I, li, j)
            hgrn_p2(c, I, li, j)
            outproj_phase(c, I, I["hgrn_w_out"][j * D:(j + 1) * D, :], "ON", True)
        else:
            diff_d1(c, I, li, j)
            diff_d2(c, I, li, j)
            outproj_phase(c, I, I["diff_w_out"][j * D:(j + 1) * D, :], "V2", False)
    final_phase(c, I, do_norm=(nsub >= 12))
    return nc


def make_in_maps(inputs, n_cores):
    f = lambda a: np.ascontiguousarray(np.asarray(a, dtype=np.float32))
    shared = {
        "norm_gains": f(inputs["norm_gains"]).reshape(12, D),
        "final_norm": f(inputs["final_norm"]).reshape(1, D),
        "ffn_w_in": f(inputs["ffn_w_in"]).reshape(8 * D, 2 * FF),
        "ffn_w_out": f(inputs["ffn_w_out"]).reshape(8 * FF, D),
        "hgrn_w_in": f(inputs["hgrn_w_in"]).reshape(2 * D, 8192),
        "hgrn_lb": f(inputs["hgrn_lower_bounds"]).reshape(2, D),
        "hgrn_norm": f(inputs["hgrn_norm"]).reshape(2, D),
        "hgrn_w_out": f(inputs["hgrn_w_out"]).reshape(2 * D, D),
        "diff_w_in": f(inputs["diff_w_in"]).reshape(2 * D, 6144),
        "diff_lambda": f(inputs["diff_lambda"]).reshape(2, 512),
        "diff_subln": f(inputs["diff_subln"]).reshape(2, 256),
        "diff_w_out": f(inputs["diff_w_out"]).reshape(2 * D, D),
        "rel_bias": f(inputs["rel_bias"]).reshape(32, 8),
        "cst": make_consts(),
    }
    x = f(inputs["x"])
    maps = []
    for cid in range(n_cores):
        m = dict(shared)
        m["x"] = x[cid * NSEQ:(cid + 1) * NSEQ].reshape(T, D)
        maps.append(m)
    return maps


def kernel(**inputs):
    n_cores = 8
    nc = build(12)
    maps = make_in_maps(inputs, n_cores)
    res = run_bass_kernel_spmd(nc, maps, core_ids=list(range(n_cores)))
    out = np.stack([np.asarray(r["out"]).reshape(NSEQ, S, D) for r in res.results], axis=0)
    return out.reshape(n_cores * NSEQ, S, D).astype(np.float32)
```

```python
import math
import numpy as np
import concourse.bass as bass
import concourse.mybir as mybir
from concourse.bass_utils import run_bass_kernel_spmd

F32 = mybir.dt.float32
BF16 = mybir.dt.bfloat16
AF = mybir.ActivationFunctionType
ALU = mybir.AluOpType
AX = mybir.AxisListType

D = 2048
KC = 16
FF = 5632
S = 2048
NSEQ = 2
T = NSEQ * S
TB = 1024
NBLK = T // TB
NTT = TB // 128
DEPTH = 4
G = 2
GCH = 22
EPS = 1e-6
SUBLN_EPS = 1e-5
QSCALE = 128 ** -0.5
NEG = -30000.0
C_ID, C_BD, C_CH, C_J, C_RS, C_OH = 0, 128, 256, 260, 388, 900
CW = 1284
CWP = 388
SAME_ENG_SYNC = True


def _t5_bucket(n):
    if n < 16:
        return n
    v = 16 + int(np.float32(np.log(np.float32(n) / np.float32(16)) / np.float32(math.log(8.0)) * np.float32(16)))
    return min(v, 31)


def make_consts():
    c = np.zeros((128, CW), np.float32)
    c[:, C_ID:C_ID + 128] = np.eye(128, dtype=np.float32)
    s = np.arange(128)[:, None]
    t = np.arange(128)[None, :]
    c[:, C_BD:C_BD + 128] = ((s // 32 == t // 32) & (s <= t)).astype(np.float32)
    for j in range(4):
        c[:, C_CH + j] = (np.arange(128) // 32 == j).astype(np.float32)
    rs = np.ones(512, np.float32)
    rs[0::32] = 0.0
    c[:, C_RS:C_RS + 512] = rs[None, :]
    for i in range(384):
        dist = i - 127
        if dist < 0:
            c[32, C_OH + i] = 1.0
        else:
            c[_t5_bucket(dist), C_OH + i] += 1.0
        c[31, C_OH + i] -= 1.0
    c[:, C_J:C_J + 128] = np.eye(128, dtype=np.float32)[::-1]
    return c


class Buf:
    __slots__ = ("w", "r")

    def __init__(self):
        self.w = None
        self.r = {}


class Ctx:
    def __init__(self, nc):
        self.nc = nc
        self.E = {"pe": nc.tensor, "act": nc.scalar, "dve": nc.vector, "pool": nc.gpsimd, "sp": nc.sync}
        self.prog = {n: [nc.alloc_semaphore("prog_" + n), 0] for n in ("pe", "act", "dve", "pool")}
        self.slots = {q: [[nc.alloc_semaphore("dq_%s_%d" % (q, i)), 0] for i in range(nsl)]
                      for q, nsl in (("sp", 16), ("pool", 4))}
        self.nslot = {"sp": 0, "pool": 0}
        self.waited = {n: {} for n in self.E}

    def _wait_raw(self, en, sem, val):
        w = self.waited[en]
        if w.get(id(sem), 0) >= val:
            return
        self.E[en].wait_ge(sem, val)
        w[id(sem)] = val

    def _wait(self, en, dep):
        if dep is None:
            return
        sem, val, src = dep
        if src == en and (en == "pe" or not SAME_ENG_SYNC):
            return
        self._wait_raw(en, sem, val)

    def _deps(self, en, reads, writes):
        for b in reads:
            self._wait(en, b.w)
        for b in writes:
            self._wait(en, b.w)
            for d in b.r.values():
                self._wait(en, d)

    def _commit(self, dep, reads, writes):
        for b in reads:
            b.r[id(dep[0])] = dep
        for b in writes:
            b.w = dep
            b.r = {}

    def op(self, en, fn, reads=(), writes=()):
        self._deps(en, reads, writes)
        ins = fn(self.E[en])
        p = self.prog[en]
        p[1] += 1
        ins.then_inc(p[0], 1)
        dep = (p[0], p[1], en)
        self._commit(dep, reads, writes)
        return dep

    def dma(self, q, out, in_, reads=(), writes=(), **kw):
        sl = self.slots[q][self.nslot[q]]
        self.nslot[q] = (self.nslot[q] + 1) % len(self.slots[q])
        if sl[1] > 0:
            self._wait_raw(q, sl[0], sl[1])
        self._deps(q, reads, writes)
        ins = self.E[q].dma_start(out=out, in_=in_, **kw)
        sl[1] += 16
        ins.then_inc(sl[0], 16)
        dep = (sl[0], sl[1], "dma")
        self._commit(dep, reads, writes)
        return dep

    def barrier(self):
        sems = [p for p in self.prog.values()] + [s for q in self.slots.values() for s in q]
        for en in self.E:
            for sem, val in sems:
                if val > 0:
                    self._wait_raw(en, sem, val)


_SBN = [0]


def SB(nc, name, shape, dt):
    _SBN[0] += 1
    return nc.sbuf_tensor("%s_u%d" % (name, _SBN[0]), shape, dt)


def bank(c, b):
    return c.ps[:, b * 512:(b + 1) * 512]


def setup(c, I):
    nc = c.nc
    A = lambda nm, shp, dt: nc.alloc_sbuf_tensor("s_" + nm, shp, dt)
    c.cst = A("cst", [128, CWP], F32)
    c.identb = A("identb", [128, 128], BF16)
    c.Jb = A("Jb", [128, 128], BF16)
    c.ones32 = A("ones32", [128, 128], F32)
    c.mhalf = A("mhalf", [128, 1], F32)
    c.lbraw = A("lbraw", [128, 2, 16], F32)
    c.lb = A("lb", [128, 2, 16], F32)
    c.oml = A("oml", [128, 2, 16], F32)
    c.hg = A("hg", [128, 2, 16], F32)
    c.lams = A("lams", [128, 4], F32)
    c.lame = A("lame", [128, 4], F32)
    c.nlam = A("nlam", [128, 2], F32)
    c.b31 = A("b31", [128, 8], F32)
    c.relb = A("relb", [64, 8], F32)
    c.vecs = A("vecs", [8, 384], F32)
    c.Rb = A("Rb", [128, 8, 2, 128], BF16)
    c.subg = A("subg", [128, 2, 256], F32)
    c.ps = nc.alloc_psum_tensor("ps", [128, 4096], F32)
    c.psb = [Buf() for _ in range(8)]
    import contextlib
    stk = contextlib.ExitStack()
    c.lamraw = stk.enter_context(SB(nc, "lamraw", [128, 1024], F32))
    c.lamp = stk.enter_context(SB(nc, "lamp", [128, 4, 128], F32))
    oh = stk.enter_context(SB(nc, "oh", [33, 384], F32))
    b = Buf()
    c.dma("sp", c.cst[:], I["cst"][:, 0:CWP], writes=[b])
    c.dma("sp", oh[:], I["cst"][0:33, C_OH:C_OH + 384], writes=[b])
    c.op("dve", lambda e: e.tensor_copy(out=c.identb[:], in_=c.cst[:, C_ID:C_ID + 128]), reads=[b])
    c.op("dve", lambda e: e.tensor_copy(out=c.Jb[:], in_=c.cst[:, C_J:C_J + 128]), reads=[b])
    c.op("dve", lambda e: e.memset(c.ones32[:], 1.0))
    c.op("dve", lambda e: e.memset(c.mhalf[:], -0.5))
    with nc.allow_non_contiguous_dma(reason="one-time small param layout"):
        b1 = Buf()
        c.dma("sp", c.lbraw[:], I["hgrn_lb"].rearrange("l (h p) -> p l h", p=128), writes=[b1])
        b2 = Buf()
        c.dma("sp", c.hg[:], I["hgrn_norm"].rearrange("l (h p) -> p l h", p=128), writes=[b2])
    blb = Buf()
    c.op("dve", lambda e: e.memset(c.lb[:], 0.0), writes=[blb])
    c.op("dve", lambda e: e.tensor_tensor(out=c.lbraw[:, 0, :], in0=c.lbraw[:, 1, :], in1=c.lbraw[:, 0, :],
                                          op=ALU.subtract), reads=[b1], writes=[b1])
    c.op("act", lambda e: e.activation(out=c.lb[:, 1, :], in_=c.lbraw[:, 0, :], func=AF.Sigmoid),
         reads=[b1], writes=[blb])
    c.op("dve", lambda e: e.tensor_scalar(out=c.oml[:], in0=c.lb[:], scalar1=-1.0, scalar2=1.0,
                                          op0=ALU.mult, op1=ALU.add), reads=[blb])
    bl = Buf()
    c.dma("sp", c.lamraw[:], I["diff_lambda"].rearrange("a b -> (a b)").partition_broadcast(128), writes=[bl])
    lv = c.lamraw[:].rearrange("p (j q w d) -> p j q w d", j=2, q=2, w=2)
    for j in range(2):
        for q in range(2):
            c.op("dve", lambda e, j=j, q=q: e.tensor_tensor(out=c.lamp[:, j * 2 + q, :], in0=lv[:, j, q, 0, :],
                                                            in1=lv[:, j, q, 1, :], op=ALU.mult), reads=[bl], writes=[bl])
    c.op("dve", lambda e: e.reduce_sum(out=c.lams[:], in_=c.lamp[:], axis=AX.X), reads=[bl], writes=[bl])
    c.op("act", lambda e: e.activation(out=c.lame[:], in_=c.lams[:], func=AF.Exp), reads=[bl], writes=[bl])
    for j in range(2):
        li = 2 * j + 1
        lam_init = 0.8 - 0.6 * math.exp(-0.3 * li)
        c.op("dve", lambda e, j=j: e.tensor_tensor(out=c.nlam[:, j:j + 1], in0=c.lame[:, 2 * j + 1:2 * j + 2],
                                                   in1=c.lame[:, 2 * j:2 * j + 1], op=ALU.subtract), reads=[bl], writes=[bl])
        c.op("dve", lambda e, j=j, v=lam_init: e.tensor_scalar(out=c.nlam[:, j:j + 1], in0=c.nlam[:, j:j + 1],
                                                               scalar1=-v, scalar2=None, op0=ALU.add), reads=[bl], writes=[bl])
    bb = Buf()
    c.dma("sp", c.b31[:], I["rel_bias"][31:32, :].partition_broadcast(128), writes=[bb])
    br = Buf()
    c.op("dve", lambda e: e.memset(c.relb[:], NEG), writes=[br])
    c.dma("sp", c.relb[0:32, :], I["rel_bias"][:, :], writes=[br])
    c.op("pe", lambda e: e.matmul(c.ps[0:8, 0:384], lhsT=c.relb[0:33, :], rhs=oh[:],
                                  start=True, stop=True), reads=[br, b], writes=[c.psb[0]])
    bv = Buf()
    c.op("dve", lambda e: e.tensor_copy(out=c.vecs[:], in_=c.ps[0:8, 0:384]), reads=[c.psb[0]], writes=[bv])
    bd = Buf()
    c.dma("sp", I["BV"][:, :], c.vecs[:], reads=[bv], writes=[bd])
    for h in range(8):
        for dl in range(2):
            src = bass.AP(I["BV"].tensor, h * 384 + 128 * dl, [[1, 128], [1, 128]])
            c.dma("pool", c.Rb[:, h, dl, :], src, reads=[bd])
    bs = Buf()
    c.dma("sp", c.subg[:].rearrange("p a b -> p (a b)"),
          I["diff_subln"].rearrange("a b -> (a b)").partition_broadcast(128), writes=[bs])
    for j in range(2):
        li = 2 * j + 1
        lam_init = 0.8 - 0.6 * math.exp(-0.3 * li)
        c.op("dve", lambda e, j=j, v=lam_init: e.tensor_scalar(out=c.subg[:, j, :], in0=c.subg[:, j, :],
                                                               scalar1=1.0 - v, scalar2=None, op0=ALU.mult),
             reads=[bs], writes=[bs])
    c.Hb = [[Buf() for _ in range(4)] for _ in range(T // 128)]
    c.barrier()
    stk.close()


class NT:
    def __init__(self, c, stack, with_norm=True):
        nc = c.nc
        self.hb = [stack.enter_context(SB(nc, "nt_hb%d" % i, [128, D], F32)) for i in range(2)] if with_norm else None
        self.xn = [stack.enter_context(SB(nc, "nt_xn%d" % i, [128, D], BF16)) for i in range(2)]
        self.gbc = stack.enter_context(SB(nc, "nt_gbc", [128, D], F32)) if with_norm else None
        self.st = stack.enter_context(SB(nc, "nt_st", [128, 3, 64], F32))
        self.hbb = [Buf(), Buf()]
        self.xnb = [Buf(), Buf()]
        self.stb = [Buf() for _ in range(64)]
        self.gb = Buf()
        self.n = 0
        self.pT = c.ps[:, 3072:4096].bitcast(BF16).rearrange("p (k t) -> p k t", k=16)
        self.pTb = c.psb[6]


def rstd_col(c, nt, ssap, k, dim, eps, stbuf):
    c.op("dve", lambda e: e.tensor_scalar(out=nt.st[:, 1, k:k + 1], in0=nt.st[:, 0, k:k + 1], scalar1=1.0 / dim,
                                          scalar2=eps, op0=ALU.mult, op1=ALU.add), reads=[stbuf], writes=[stbuf])
    c.op("pool", lambda e: e.tensor_tensor(out=nt.st[:, 2, k:k + 1], in0=nt.st[:, 1, k:k + 1], in1=c.mhalf[:, 0:1],
                                           op=ALU.pow), reads=[stbuf], writes=[stbuf])


def load_gain(c, nt, gain_ap):
    c.dma("sp", nt.gbc[:], gain_ap.partition_broadcast(128), writes=[nt.gb])


class PhaseA:
    def __init__(self, c, nt, src_fn, src_bufs_fn, xT, xTb):
        self.c, self.nt, self.src_fn, self.src_bufs_fn, self.xT, self.xTb = c, nt, src_fn, src_bufs_fn, xT, xTb
        self.base = nt.n
        nt.n += NTT

    def stage1(self, tt):
        c, nt = self.c, self.nt
        n = self.base + tt
        i = n % 2
        k = n % 64
        c.dma("sp", nt.hb[i][:], self.src_fn(tt), reads=self.src_bufs_fn(tt), writes=[nt.hbb[i]])
        c.op("act", lambda e: e.activation(out=nt.xn[i][:], in_=nt.hb[i][:], func=AF.Square,
                                           accum_out=nt.st[:, 0, k:k + 1]),
             reads=[nt.hbb[i]], writes=[nt.xnb[i], nt.stb[k]])
        rstd_col(c, nt, None, k, float(D), EPS, nt.stb[k])
        c.op("dve", lambda e: e.scalar_tensor_tensor(out=nt.xn[i][:], in0=nt.hb[i][:], scalar=nt.st[:, 2, k:k + 1],
                                                     in1=nt.gbc[:], op0=ALU.mult, op1=ALU.mult),
             reads=[nt.hbb[i], nt.stb[k], nt.gb], writes=[nt.xnb[i]])

    def stage2(self, tt):
        transpose_tile(self.c, self.nt, (self.base + tt) % 2, tt, self.xT, self.xTb)

    def hook(self, k):
        if k % 4 == 0:
            t = k // 4
            if t < NTT:
                self.stage1(t)
            if 1 <= t <= NTT:
                self.stage2(t - 1)

    def finish(self):
        self.stage2(NTT - 1)


def norm_transpose_block(c, nt, src_fn, src_bufs_fn, xT, xTb):
    pa = PhaseA(c, nt, src_fn, src_bufs_fn, xT, xTb)
    pa.stage1(0)
    for tt in range(NTT):
        if tt + 1 < NTT:
            pa.stage1(tt + 1)
        pa.stage2(tt)


def transpose_tile(c, nt, i, tt, xT, xTb):
    def f(e):
        ins = None
        for kc in range(KC):
            ins = e.transpose(out=nt.pT[:, kc, :], in_=nt.xn[i][:, kc * 128:(kc + 1) * 128], identity=c.identb[:])
        return ins
    c.op("pe", f, reads=[nt.xnb[i]], writes=[nt.pTb, c.psb[7]])
    c.op("act", lambda e: e.activation(out=xT[:, :, tt * 128:(tt + 1) * 128], in_=nt.pT, func=AF.Copy),
         reads=[nt.pTb], writes=[xTb])


class GO:
    def __init__(self, c, stack, nkmax):
        nc = c.nc
        self.wo = [stack.enter_context(SB(nc, "go_wo%d" % i, [128, nkmax, 512], BF16)) for i in range(2)]
        self.ht = [stack.enter_context(SB(nc, "go_ht%d" % i, [128, 512], F32)) for i in range(2)]
        self.ho = [stack.enter_context(SB(nc, "go_ho%d" % i, [128, 512], F32)) for i in range(2)]
        self.wob = [Buf(), Buf()]
        self.htb = [Buf(), Buf()]
        self.hob = [Buf(), Buf()]
        self.n = 0
        self.ns = 0


def gemm_out(c, go, lhsT, lhsTb, nk, W, scale, blk, src_ap, H, hook=None):
    Wv = W.rearrange("(c p) n -> p c n", p=128)

    def load(dq):
        i = go.ns % 2
        go.ns += 1
        c.dma("pool", go.wo[i][:, 0:nk, :], Wv[:, :, dq * 512:(dq + 1) * 512], writes=[go.wob[i]])
        return i
    its = [(dq, tt) for dq in range(4) for tt in range(NTT)]

    def load_ht(k):
        dq, tt = its[k]
        i = (go.n + (k - cur_k[0])) % 2
        gt = blk * NTT + tt
        c.dma("sp", go.ht[i][:], src_ap[gt * 128:(gt + 1) * 128, dq * 512:(dq + 1) * 512],
              reads=[c.Hb[gt][dq]], writes=[go.htb[i]])
    cur_k = [0]
    cur = load(0)
    load_ht(0)
    nxt = None
    for k, (dq, tt) in enumerate(its):
        if tt == 0:
            if dq > 0:
                cur = nxt
            nxt = load(dq + 1) if dq < 3 else None
        cur_k[0] = k
        if k + 1 < len(its):
            load_ht(k + 1)
        if hook is not None:
            hook(k)
        n = go.n
        go.n += 1
        i = n % 2
        gt = blk * NTT + tt
        rows = slice(gt * 128, (gt + 1) * 128)
        cols = slice(dq * 512, (dq + 1) * 512)
        pb = 4 + i

        def f(e, pb=pb, tt=tt, cur=cur):
            ins = None
            for j in range(nk):
                ins = e.matmul(bank(c, pb), lhsT=lhsT[:, j, tt * 128:(tt + 1) * 128], rhs=go.wo[cur][:, j, :],
                               start=(j == 0), stop=(j == nk - 1))
            return ins
        c.op("pe", f, reads=[lhsTb, go.wob[cur]], writes=[c.psb[pb]])
        c.op("dve", lambda e, pb=pb, i=i: e.scalar_tensor_tensor(out=go.ho[i][:], in0=bank(c, pb), scalar=float(scale),
                                                                 in1=go.ht[i][:], op0=ALU.mult, op1=ALU.add),
             reads=[c.psb[pb], go.htb[i]], writes=[go.hob[i]])
        c.dma("sp", H[rows, cols], go.ho[i][:], reads=[go.hob[i]], writes=[c.Hb[gt][dq]])


def h_rows_fn(src, blk):
    return lambda tt: src[(blk * NTT + tt) * 128:(blk * NTT + tt + 1) * 128, :]


def h_bufs_fn(c, blk):
    return lambda tt: c.Hb[blk * NTT + tt]


def ffn_phase(c, I, li, k, first):
    import contextlib
    nc = c.nc
    idx = li * 2 + k
    w_in = I["ffn_w_in"][idx * D:(idx + 1) * D, :]
    w_out = I["ffn_w_out"][idx * FF:(idx + 1) * FF, :]
    gain = I["norm_gains"][li * 3 + (0 if k == 0 else 2):li * 3 + (0 if k == 0 else 2) + 1, :]
    Wv = w_in.rearrange("(kc p) n -> p kc n", p=128)
    H = I["H"]
    with contextlib.ExitStack() as st:
        nt = NT(c, st)
        go = GO(c, st, GCH)
        xT = st.enter_context(SB(nc, "xT", [128, KC, TB], BF16))
        aT = st.enter_context(SB(nc, "aT", [128, GCH, TB], BF16))
        wg = [st.enter_context(SB(nc, "wg%d" % i, [128, KC, 256], BF16)) for i in range(2)]
        wu = [st.enter_context(SB(nc, "wu%d" % i, [128, KC, 256], BF16)) for i in range(2)]
        sg = [st.enter_context(SB(nc, "sg%d" % i, [128, 512], BF16)) for i in range(2)]
        xTb, aTb = Buf(), Buf()
        wb = [Buf(), Buf()]
        sgb = [Buf(), Buf()]
        load_gain(c, nt, gain)
        ns = 0
        nu = 0
        slabs = [(2 * i, 2) for i in range(11)]

        def loadw(g, si):
            nonlocal ns
            i = ns % 2
            ns += 1
            c0, n = slabs[si]
            ca = (g * GCH + c0) * 128
            c.dma("pool", wg[i][:, :, 0:n * 128], Wv[:, :, ca:ca + n * 128], writes=[wb[i]])
            c.dma("pool", wu[i][:, :, 0:n * 128], Wv[:, :, FF + ca:FF + ca + n * 128], writes=[wb[i]])
            return i
        src = I["x"] if first else H
        norm_transpose_block(c, nt, h_rows_fn(src, 0), h_bufs_fn(c, 0), xT, xTb)
        for blk in range(NBLK):
            for g in range(G):
                cur = loadw(g, 0)
                for si, (c0, n) in enumerate(slabs):
                    nxt = loadw(g, si + 1) if si + 1 < len(slabs) else None
                    for ch in range(n):
                        cl = c0 + ch
                        for th in range(2):
                            u = nu % 2
                            nu += 1

                            def fg(e, w=wg[cur], pb=u, ch=ch, th=th):
                                ins = None
                                for kc in range(KC):
                                    ins = e.matmul(bank(c, pb), lhsT=w[:, kc, ch * 128:(ch + 1) * 128],
                                                   rhs=xT[:, kc, th * 512:(th + 1) * 512], start=(kc == 0), stop=(kc == KC - 1))
                                return ins
                            c.op("pe", fg, reads=[wb[cur], xTb], writes=[c.psb[u]])
                            c.op("pe", lambda e, th=th, ch=ch, u=u, cur=cur: fg(e, w=wu[cur], pb=2 + u, ch=ch, th=th),
                                 reads=[wb[cur], xTb], writes=[c.psb[2 + u]])
                            c.op("act", lambda e, u=u: e.activation(out=sg[u][:], in_=bank(c, u), func=AF.Silu),
                                 reads=[c.psb[u]], writes=[sgb[u]])
                            c.op("dve", lambda e, u=u, cl=cl, th=th: e.tensor_tensor(
                                out=aT[:, cl, th * 512:(th + 1) * 512], in0=sg[u][:], in1=bank(c, 2 + u), op=ALU.mult),
                                reads=[sgb[u], c.psb[2 + u]], writes=[aTb])
                    cur = nxt
                srcg = I["x"] if (first and g == 0) else H
                if g == G - 1 and blk + 1 < NBLK:
                    pa = PhaseA(c, nt, h_rows_fn(src, blk + 1), h_bufs_fn(c, blk + 1), xT, xTb)
                    gemm_out(c, go, aT, aTb, GCH, w_out[g * GCH * 128:(g + 1) * GCH * 128, :], 0.5, blk, srcg, H, hook=pa.hook)
                    pa.finish()
                else:
                    gemm_out(c, go, aT, aTb, GCH, w_out[g * GCH * 128:(g + 1) * GCH * 128, :], 0.5, blk, srcg, H)
    c.barrier()


def vproj(c, I, st_bufs, Wv, col0, xT, xTb, blk):
    wv, wvb, vo, vob, cnt = st_bufs

    def load(dq):
        i = cnt[0] % 2
        cnt[0] += 1
        c.dma("pool", wv[i][:], Wv[:, :, col0 + dq * 512:col0 + (dq + 1) * 512], writes=[wvb[i]])
        return i
    cur = load(0)
    for dq in range(4):
        nxt = load(dq + 1) if dq < 3 else None
        for tt in range(NTT):
            n = cnt[1]
            cnt[1] += 1
            i = n % 2
            pb = i

            def f(e, pb=pb, tt=tt, cur=cur):
                ins = None
                for kc in range(KC):
                    ins = e.matmul(bank(c, pb), lhsT=xT[:, kc, tt * 128:(tt + 1) * 128], rhs=wv[cur][:, kc, :],
                                   start=(kc == 0), stop=(kc == KC - 1))
                return ins
            c.op("pe", f, reads=[xTb, wvb[cur]], writes=[c.psb[pb]])
            c.op("act", lambda e, pb=pb, i=i: e.activation(out=vo[i][:], in_=bank(c, pb), func=AF.Copy),
                 reads=[c.psb[pb]], writes=[vob[i]])
            gt = blk * NTT + tt
            c.dma("sp", I["V"][gt * 128:(gt + 1) * 128, dq * 512:(dq + 1) * 512], vo[i][:], reads=[vob[i]])
        cur = nxt


def hgrn_p1(c, I, li, j):
    import contextlib
    nc = c.nc
    W = I["hgrn_w_in"][j * D:(j + 1) * D, :]
    Wv = W.rearrange("(kc p) n -> p kc n", p=128)
    gain = I["norm_gains"][li * 3 + 1:li * 3 + 2, :]
    H = I["H"]
    with contextlib.ExitStack() as st:
        nt = NT(c, st)
        xT = st.enter_context(SB(nc, "xT", [128, KC, TB], BF16))
        wq = [st.enter_context(SB(nc, "wq%d" % i, [128, KC, 3, 256], BF16)) for i in range(2)]
        wv = [st.enter_context(SB(nc, "wv%d" % i, [128, KC, 512], BF16)) for i in range(2)]
        vo = [st.enter_context(SB(nc, "vo%d" % i, [128, 512], BF16)) for i in range(2)]
        tmp = [st.enter_context(SB(nc, "tmp%d" % i, [128, 5, 512], F32)) for i in range(2)]
        ob = [st.enter_context(SB(nc, "ob%d" % i, [128, 4, 512], BF16)) for i in range(2)]
        eo = [st.enter_context(SB(nc, "eo%d" % i, [128, 16], F32)) for i in range(2)]
        xTb = Buf()
        wqb = [Buf(), Buf()]
        tb = [Buf(), Buf()]
        obb = [Buf(), Buf()]
        vst = (wv, [Buf(), Buf()], vo, [Buf(), Buf()], [0, 0])
        load_gain(c, nt, gain)
        ns = [0]
        nu = 0

        def loadw(hp):
            i = ns[0] % 2
            ns[0] += 1
            for a, base in enumerate((0, 2048, 6144)):
                c.dma("pool", wq[i][:, :, a, :], Wv[:, :, base + hp * 256:base + (hp + 1) * 256], writes=[wqb[i]])
            return i
        rsmt = st.enter_context(SB(nc, "rsm", [128, 512], F32))
        rsb = Buf()
        c.dma("sp", rsmt[:], I["cst"][:, C_RS:C_RS + 512], writes=[rsb])
        rsm = rsmt[:]
        for blk in range(NBLK):
            norm_transpose_block(c, nt, h_rows_fn(H, blk), h_bufs_fn(c, blk), xT, xTb)
            cur = loadw(0)
            for hp in range(8):
                nxt = loadw(hp + 1) if hp < 7 else None
                for hh in range(2):
                    hd = hp * 2 + hh
                    for th in range(2):
                        u = nu % 2
                        nu += 1
                        tk = slice(blk * TB + th * 512, blk * TB + (th + 1) * 512)
                        for a in range(3):
                            def f(e, a=a, pb=3 * u + a, hh=hh, th=th, cur=cur):
                                ins = None
                                for kc in range(KC):
                                    ins = e.matmul(bank(c, pb), lhsT=wq[cur][:, kc, a, hh * 128:(hh + 1) * 128],
                                                   rhs=xT[:, kc, th * 512:(th + 1) * 512], start=(kc == 0), stop=(kc == KC - 1))
                                return ins
                            c.op("pe", f, reads=[wqb[cur], xTb], writes=[c.psb[3 * u + a]])
                        pq, pz, pg = bank(c, 3 * u), bank(c, 3 * u + 1), bank(c, 3 * u + 2)
                        t = tmp[u]
                        o = ob[u]
                        lbc = c.lb[:, j, hd:hd + 1]
                        omc = c.oml[:, j, hd:hd + 1]
                        R = [c.psb[3 * u], c.psb[3 * u + 1], c.psb[3 * u + 2]]
                        TBF = [tb[u]]
                        OBF = [obb[u]]
                        c.op("act", lambda e: e.activation(out=t[:, 0, :], in_=pz, func=AF.Sigmoid), reads=R, writes=TBF)
                        c.op("act", lambda e: e.activation(out=t[:, 1, :], in_=pz, func=AF.Sigmoid, scale=-1.0), reads=R, writes=TBF)
                        c.op("act", lambda e: e.activation(out=t[:, 2, :], in_=pq, func=AF.Silu), reads=R, writes=TBF)
                        c.op("act", lambda e: e.activation(out=o[:, 3, :], in_=pg, func=AF.Silu), reads=R, writes=OBF)
                        c.op("dve", lambda e: e.tensor_scalar(out=t[:, 0, :], in0=t[:, 0, :], scalar1=omc, scalar2=lbc,
                                                              op0=ALU.mult, op1=ALU.add), reads=TBF, writes=TBF)
                        c.op("act", lambda e: e.activation(out=t[:, 0, :], in_=t[:, 0, :], func=AF.Ln), reads=TBF, writes=TBF)
                        c.op("dve", lambda e: e.tensor_tensor_scan(out=t[:, 3, :], data0=rsm, data1=t[:, 0, :], initial=0.0,
                                                                   op0=ALU.mult, op1=ALU.add), reads=TBF + [rsb], writes=TBF)
                        c.op("act", lambda e: e.activation(out=t[:, 0, :], in_=t[:, 3, :], func=AF.Exp), reads=TBF, writes=TBF)
                        c.op("act", lambda e: e.activation(out=t[:, 4, :], in_=t[:, 3, :], func=AF.Exp, scale=-1.0), reads=TBF, writes=TBF)
                        c.op("dve", lambda e: e.tensor_tensor(out=o[:, 0, :], in0=t[:, 2, :], in1=t[:, 0, :], op=ALU.mult),
                             reads=TBF, writes=OBF)
                        c.op("dve", lambda e: e.scalar_tensor_tensor(out=t[:, 1, :], in0=t[:, 1, :], scalar=omc, in1=t[:, 4, :],
                                                                     op0=ALU.mult, op1=ALU.mult), reads=TBF, writes=TBF)
                        c.op("act", lambda e: e.activation(out=o[:, 1, :], in_=t[:, 1, :], func=AF.Copy), reads=TBF, writes=OBF)
                        ev = t[:, 0, :].rearrange("p (c t) -> p c t", t=32)[:, :, 31:32]
                        c.op("dve", lambda e: e.tensor_tensor(out=o[:, 2, :].rearrange("p (c t) -> p c t", t=32),
                                                              in0=t[:, 1, :].rearrange("p (c t) -> p c t", t=32),
                                                              in1=ev.to_broadcast([128, 16, 32]), op=ALU.mult), reads=TBF, writes=OBF)
                        c.op("dve", lambda e: e.tensor_copy(out=eo[u][:].rearrange("p (c o) -> p c o", o=1), in_=ev),
                             reads=TBF, writes=OBF)
                        for a, nm in enumerate(("QT", "KT", "KH", "OG")):
                            c.dma("sp", I[nm][hd, :, tk], o[:, a, :], reads=OBF)
                        ck = (blk * TB + th * 512) // 32
                        c.dma("sp", I["EE"][hd, :, ck:ck + 16], eo[u][:], reads=OBF)
                cur = nxt
            vproj(c, I, vst, Wv, 4096, xT, xTb, blk)
    c.barrier()


def hgrn_p2(c, I, li, j):
    import contextlib
    nc = c.nc
    NP = 4
    with contextlib.ExitStack() as st:
        def A(nm, shp, dt):
            return st.enter_context(SB(nc, nm, shp, dt))
        qT = [A("qT%d" % i, [128, S], BF16) for i in range(NP)]
        kT = [A("kT%d" % i, [128, S], BF16) for i in range(NP)]
        khT = [A("khT%d" % i, [128, S], BF16) for i in range(NP)]
        ogT = [A("ogT%d" % i, [128, S], BF16) for i in range(NP)]
        Vh = [A("Vh%d" % i, [128, 16, 128], BF16) for i in range(NP)]
        Eh = [A("Eh%d" % i, [128, 64], F32) for i in range(NP)]
        khm2 = [[A("khm%d_%d" % (i, k), [128, 4, 128], BF16) for i in range(NP)] for k in range(2)]
        ATm2 = [[A("ATm%d_%d" % (i, k), [128, 128], BF16) for i in range(NP)] for k in range(2)]
        Sst = [A("Sst%d" % i, [128, 128], F32) for i in range(NP)]
        Sb = [[A("Sb%d_%d" % (i, k), [128, 128], BF16) for k in range(2)] for i in range(NP)]
        oT = [A("oT%d" % i, [128, S], F32) for i in range(NP)]
        osq = A("osq", [128, S], F32)
        rst = A("rst", [128, S], F32)
        onb = [A("onb%d" % i, [128, S], BF16) for i in range(2)]
        epsb = A("epsb", [128, 1], F32)
        inA = [Buf() for _ in range(NP)]
        inB = [Buf() for _ in range(NP)]
        khmb2 = [[Buf() for _ in range(NP)] for _ in range(2)]
        ATb2 = [[Buf() for _ in range(NP)] for _ in range(2)]
        Sbuf = [Buf() for _ in range(NP)]
        Sbb = [[Buf(), Buf()] for _ in range(NP)]
        oTb = [Buf() for _ in range(NP)]
        osqb, rstb = Buf(), Buf()
        onbb = [Buf(), Buf()]
        pAb = [c.psb[0] for _ in range(NP)]
        pTb = [c.psb[6] for _ in range(NP)]
        pA = [c.ps[:, p * 128:(p + 1) * 128] for p in range(NP)]
        pU2 = [[c.ps[:, bk * 512 + p * 128:bk * 512 + (p + 1) * 128] for p in range(NP)] for bk in (1, 7)]
        pUb2 = [c.psb[1], c.psb[7]]
        pO = [c.ps[:, (2 + p) * 512:(2 + p) * 512 + 128] for p in range(NP)]
        pOb = [c.psb[2 + p] for p in range(NP)]
        pT = [c.ps[:, 3072 + p * 64:3072 + (p + 1) * 64].bitcast(BF16) for p in range(NP)]
        c.op("dve", lambda e: e.memset(epsb[:], EPS))
        pairs = [(s, hd) for s in range(NSEQ) for hd in range(16)]
        chm = c.cst[:, C_CH:C_CH + 4]

        def load(p, s, hd):
            tk = slice(s * S, (s + 1) * S)
            for buf, nm in ((qT, "QT"), (kT, "KT"), (khT, "KH")):
                c.dma("sp", buf[p][:], I[nm][hd, :, tk], writes=[inA[p]])
            c.dma("sp", Eh[p][:], I["EE"][hd, :, s * 64:(s + 1) * 64], writes=[inA[p]])
            c.dma("sp", Vh[p][:], I["V"][tk, hd * 128:(hd + 1) * 128].rearrange("(t p) v -> p t v", p=128), writes=[inA[p]])
            c.dma("sp", ogT[p][:], I["OG"][hd, :, tk], writes=[inB[p]])
        ngr = len(pairs) // NP
        for p in range(NP):
            load(p, *pairs[p])
        nend = 0
        for gi in range(ngr):
            grp = pairs[gi * NP:(gi + 1) * NP]
            for p in range(NP):
                c.op("dve", lambda e, p=p: e.memset(Sst[p][:], 0.0), writes=[Sbuf[p]])
                c.op("dve", lambda e, p=p: e.memset(Sb[p][0][:], 0.0), writes=[Sbb[p][0]])
            def prep(ti):
                tsl = slice(ti * 128, (ti + 1) * 128)
                khm, khmb, ATm, ATb = khm2[ti % 2], khmb2[ti % 2], ATm2[ti % 2], ATb2[ti % 2]
                for p in range(NP):
                    IB = [inA[p]]
                    c.op("pe", lambda e, p=p: e.transpose(out=pT[p], in_=khT[p][:, tsl], identity=c.identb[:]),
                         reads=IB, writes=[pTb[p]])
                    c.op("pe", lambda e, p=p: e.matmul(pA[p], lhsT=kT[p][:, tsl], rhs=qT[p][:, tsl], start=True, stop=True),
                         reads=IB, writes=[pAb[p]])
                for p in range(NP):
                    c.op("dve", lambda e, p=p: e.tensor_tensor(
                        out=khm[p][:], in0=pT[p].rearrange("p (o d) -> p o d", o=1).to_broadcast([128, 4, 128]),
                        in1=chm.rearrange("p (j o) -> p j o", o=1).to_broadcast([128, 4, 128]), op=ALU.mult),
                        reads=[pTb[p]], writes=[khmb[p]])
                    c.op("dve", lambda e, p=p: e.tensor_tensor(out=ATm[p][:], in0=pA[p], in1=c.cst[:, C_BD:C_BD + 128], op=ALU.mult),
                         reads=[pAb[p]], writes=[ATb[p]])
            prep(0)
            for ti in range(16):
                tsl = slice(ti * 128, (ti + 1) * 128)
                khm, khmb, ATm, ATb = khm2[ti % 2], khmb2[ti % 2], ATm2[ti % 2], ATb2[ti % 2]
                for p in range(NP):
                    c.op("pe", lambda e, p=p: e.matmul(pO[p], lhsT=Vh[p][:, ti, :], rhs=ATm[p][:], start=True, stop=False),
                         reads=[inA[p], ATb[p]], writes=[pOb[p]])
                for jj in range(4):
                    ch = ti * 4 + jj
                    sc = ch % 2
                    sn = (ch + 1) % 2
                    pU = pU2[ch % 2]
                    pUb = [pUb2[ch % 2]] * NP
                    for p in range(NP):
                        c.op("pe", lambda e, p=p: e.matmul(pO[p][:, jj * 32:(jj + 1) * 32], lhsT=Sb[p][sc][:],
                                                           rhs=qT[p][:, ch * 32:(ch + 1) * 32], start=False, stop=(jj == 3)),
                             reads=[inA[p], Sbb[p][sc]], writes=[pOb[p]])
                        c.op("pe", lambda e, p=p: e.matmul(pU[p], lhsT=khm[p][:, jj, :], rhs=Vh[p][:, ti, :], start=True, stop=True),
                             reads=[inA[p], khmb[p]], writes=[pUb[p]])
                    for p in range(NP):
                        c.op("dve", lambda e, p=p: e.scalar_tensor_tensor(
                            out=Sst[p][:], in0=Sst[p][:], scalar=Eh[p][:, ch:ch + 1], in1=pU[p], op0=ALU.mult, op1=ALU.add),
                            reads=[inA[p], pUb[p]], writes=[Sbuf[p]])
                        c.op("act", lambda e, p=p: e.activation(out=Sb[p][sn][:], in_=Sst[p][:], func=AF.Copy),
                             reads=[Sbuf[p]], writes=[Sbb[p][sn]])
                    if jj == 0 and ti + 1 < 16:
                        prep(ti + 1)
                for p in range(NP):
                    c.op("act", lambda e, p=p: e.activation(out=oT[p][:, tsl], in_=pO[p], func=AF.Copy),
                         reads=[pOb[p]], writes=[oTb[p]])
            if gi + 1 < ngr:
                nxt = pairs[(gi + 1) * NP:(gi + 2) * NP]
            else:
                nxt = []
            for p, (s, hd) in enumerate(grp):
                w = nend % 2
                nend += 1
                c.op("act", lambda e: e.activation(out=osq[:], in_=oT[p][:], func=AF.Square), reads=[oTb[p]], writes=[osqb])
                for q4 in range(4):
                    sl = slice(q4 * 512, (q4 + 1) * 512)
                    c.op("pe", lambda e: e.matmul(bank(c, 0), lhsT=c.ones32[:], rhs=osq[:, sl], start=True, stop=True),
                         reads=[osqb], writes=[c.psb[0]])
                    c.op("act", lambda e: e.activation(out=rst[:, sl], in_=bank(c, 0), func=AF.Ln, scale=1.0 / 128.0,
                                                       bias=epsb[:, 0:1]), reads=[c.psb[0]], writes=[rstb])
                c.op("act", lambda e: e.activation(out=rst[:], in_=rst[:], func=AF.Exp, scale=-0.5), reads=[rstb], writes=[rstb])
                c.op("dve", lambda e: e.tensor_tensor(out=oT[p][:], in0=oT[p][:], in1=rst[:], op=ALU.mult),
                     reads=[rstb, oTb[p]], writes=[oTb[p]])
                c.op("dve", lambda e: e.scalar_tensor_tensor(out=onb[w][:], in0=oT[p][:], scalar=c.hg[:, j, hd:hd + 1], in1=ogT[p][:],
                                                             op0=ALU.mult, op1=ALU.mult), reads=[inB[p], oTb[p]], writes=[onbb[w]])
                c.dma("sp", I["ON"][hd, :, s * S:(s + 1) * S], onb[w][:], reads=[onbb[w]])
                if nxt:
                    load(p, *nxt[p])
    c.barrier()


def outproj_phase(c, I, W, src_name, feature_major):
    import contextlib
    nc = c.nc
    H = I["H"]
    with contextlib.ExitStack() as st:
        go = GO(c, st, KC)
        oT = st.enter_context(SB(nc, "oT", [128, KC, TB], BF16))
        oTb = Buf()
        nt = None if feature_major else NT(c, st, with_norm=False)
        for blk in range(NBLK):
            if feature_major:
                c.dma("sp", oT[:], I[src_name].rearrange("h p t -> p h t")[:, :, blk * TB:(blk + 1) * TB], writes=[oTb])
            else:
                for tt in range(NTT):
                    n = nt.n
                    nt.n += 1
                    i = n % 2
                    gt = blk * NTT + tt
                    c.dma("sp", nt.xn[i][:], I[src_name][gt * 128:(gt + 1) * 128, :], writes=[nt.xnb[i]])
                    transpose_tile(c, nt, i, tt, oT, oTb)
            gemm_out(c, go, oT, oTb, KC, W, 1.0, blk, H, H)
    c.barrier()


def diff_d1(c, I, li, j):
    import contextlib
    nc = c.nc
    W = I["diff_w_in"][j * D:(j + 1) * D, :]
    Wv = W.rearrange("(kc p) n -> p kc n", p=128)
    gain = I["norm_gains"][li * 3 + 1:li * 3 + 2, :]
    H = I["H"]
    with contextlib.ExitStack() as st:
        nt = NT(c, st)
        xT = st.enter_context(SB(nc, "xT", [128, KC, TB], BF16))
        wq = [st.enter_context(SB(nc, "wq%d" % i, [128, KC, 256], BF16)) for i in range(2)]
        wv = [st.enter_context(SB(nc, "wv%d" % i, [128, KC, 512], BF16)) for i in range(2)]
        vo = [st.enter_context(SB(nc, "vo%d" % i, [128, 512], BF16)) for i in range(2)]
        ob = [st.enter_context(SB(nc, "ob%d" % i, [128, 512], BF16)) for i in range(2)]
        xTb = Buf()
        wqb = [Buf(), Buf()]
        obb = [Buf(), Buf()]
        vst = (wv, [Buf(), Buf()], vo, [Buf(), Buf()], [0, 0])
        load_gain(c, nt, gain)
        ns = [0]
        nu = 0

        def loadw(sl):
            i = ns[0] % 2
            ns[0] += 1
            c.dma("pool", wq[i][:], Wv[:, :, sl * 256:(sl + 1) * 256], writes=[wqb[i]])
            return i
        for blk in range(NBLK):
            norm_transpose_block(c, nt, h_rows_fn(H, blk), h_bufs_fn(c, blk), xT, xTb)
            cur = loadw(0)
            for sl in range(16):
                nxt = loadw(sl + 1) if sl < 15 else None
                for ch in range(2):
                    fc = sl * 2 + ch
                    for th in range(2):
                        u = nu % 2
                        nu += 1

                        def f(e, pb=u, ch=ch, th=th, cur=cur):
                            ins = None
                            for kc in range(KC):
                                ins = e.matmul(bank(c, pb), lhsT=wq[cur][:, kc, ch * 128:(ch + 1) * 128],
                                               rhs=xT[:, kc, th * 512:(th + 1) * 512], start=(kc == 0), stop=(kc == KC - 1))
                            return ins
                        c.op("pe", f, reads=[wqb[cur], xTb], writes=[c.psb[u]])
                        sc = QSCALE if fc < 16 else 1.0
                        c.op("act", lambda e, u=u, sc=sc: e.activation(out=ob[u][:], in_=bank(c, u), func=AF.Copy, scale=sc),
                             reads=[c.psb[u]], writes=[obb[u]])
                        tk = slice(blk * TB + th * 512, blk * TB + (th + 1) * 512)
                        c.dma("sp", I["QK"][fc, :, tk], ob[u][:], reads=[obb[u]])
                cur = nxt
            vproj(c, I, vst, Wv, 4096, xT, xTb, blk)
    c.barrier()


def diff_d2(c, I, li, j):
    import contextlib
    nc = c.nc
    with contextlib.ExitStack() as st:
        def A(nm, shp, dt):
            return st.enter_context(SB(nc, nm, shp, dt))
        qk = [A("qk%d" % i, [128, 4, S], BF16) for i in range(2)]
        Vx = [A("Vx%d" % i, [128, 16, 258], BF16) for i in range(2)]
        pT = [A("pT%d" % i, [128, 512], BF16) for i in range(4)]
        O1 = A("O1", [128, 4, 256], F32)
        Of = [A("Of%d" % i, [128, 256], F32) for i in range(2)]
        sqj = A("sqj", [128, 256], BF16)
        ob = [A("odb%d" % i, [128, 256], BF16) for i in range(2)]
        sm = A("sm", [128, 4, 64], F32)
        inb = [Buf(), Buf()]
        pTb = [Buf() for _ in range(4)]
        SBK = [0, 1, 6, 7]
        O1b = Buf()
        Ofb = [Buf(), Buf()]
        obb = [Buf(), Buf()]
        smb = [Buf() for _ in range(64)]
        sqb = Buf()
        for i in range(2):
            c.op("dve", lambda e, i=i: e.memset(Vx[i][:, :, 256:258], 1.0), writes=[inb[i]])
        pairs = [(s, h) for s in range(NSEQ) for h in range(8)]

        def load(pi):
            s, h = pairs[pi]
            i = pi % 2
            tk = slice(s * S, (s + 1) * S)
            for a, fc in enumerate((2 * h, 2 * h + 1, 16 + 2 * h, 16 + 2 * h + 1)):
                c.dma("sp", qk[i][:, a, :], I["QK"][fc, :, tk], writes=[inb[i]])
            c.dma("sp", Vx[i][:, :, 0:256], I["V"][tk, h * 256:(h + 1) * 256].rearrange("(t p) v -> p t v", p=128), writes=[inb[i]])
        load(0)
        nun = 0
        nq = 0
        for pi, (s, h) in enumerate(pairs):
            if pi + 1 < len(pairs):
                load(pi + 1)
            i = pi % 2
            IB = [inb[i]]
            for qg in range(4):
                for ii in range(2):
                    qTi = qk[i][:, ii, :]
                    kTi = qk[i][:, 2 + ii, :]
                    nkt = 4 * qg + 4
                    ubase = nun
                    nun += nkt

                    def geom(kt):
                        q0 = max(kt, 4 * qg)
                        return q0, (q0 - 4 * qg) * 128, (4 * qg + 4 - q0) * 128

                    def emitS(kt):
                        u = (ubase + kt) % 4
                        q0, off, ncol = geom(kt)
                        bts = [(dl, kt + dl) for dl in (0, 1) if 4 * qg <= kt + dl < 4 * qg + 4]

                        def f(e):
                            ins = e.matmul(bank(c, SBK[u])[:, off:off + ncol], lhsT=kTi[:, kt * 128:(kt + 1) * 128],
                                           rhs=qTi[:, q0 * 128:q0 * 128 + ncol], start=True, stop=(len(bts) == 0))
                            for bi, (dl, qt) in enumerate(bts):
                                o2 = (qt - 4 * qg) * 128
                                ins = e.matmul(bank(c, SBK[u])[:, o2:o2 + 128], lhsT=c.Jb[:], rhs=c.Rb[:, h, dl, :],
                                               start=False, stop=(bi == len(bts) - 1))
                            return ins
                        c.op("pe", f, reads=IB, writes=[c.psb[SBK[u]]])

                    def emitEP(kt):
                        u = (ubase + kt) % 4
                        q0, off, ncol = geom(kt)
                        c.op("act", lambda e: e.activation(out=pT[u][:, off:off + ncol], in_=bank(c, SBK[u])[:, off:off + ncol],
                                                           func=AF.Exp, bias=c.b31[:, h:h + 1]),
                             reads=[c.psb[SBK[u]]], writes=[pTb[u]])

                        def fpv(e):
                            ins = None
                            for qt in range(q0, 4 * qg + 4):
                                ql = qt - 4 * qg
                                ins = e.matmul(bank(c, 2 + ql)[:, 0:257], lhsT=pT[u][:, ql * 128:(ql + 1) * 128],
                                               rhs=Vx[i][:, kt, 0:257], start=(kt == 0), stop=(kt == qt))
                            return ins
                        c.op("pe", fpv, reads=IB + [pTb[u]], writes=[c.psb[2 + x] for x in range(q0 - 4 * qg, 4)])
                    for kt in range(min(3, nkt)):
                        emitS(kt)
                    for kt in range(nkt):
                        if kt + 3 < nkt:
                            emitS(kt + 3)
                        emitEP(kt)
                    for ql in range(4):
                        pb = 2 + ql
                        k = nq % 64
                        if ii == 0:
                            nq += 1
                            c.op("dve", lambda e, pb=pb, k=k: e.reciprocal(out=sm[:, 0, k:k + 1], in_=bank(c, pb)[:, 256:257]),
                                 reads=[c.psb[pb]], writes=[smb[k]])
                            c.op("dve", lambda e, pb=pb, k=k, ql=ql: e.tensor_scalar(
                                out=O1[:, ql, :], in0=bank(c, pb)[:, 0:256], scalar1=sm[:, 0, k:k + 1], scalar2=None, op0=ALU.mult),
                                reads=[c.psb[pb], smb[k]], writes=[O1b])
                        else:
                            nq += 1
                            w = k % 2
                            qt = 4 * qg + ql
                            c.op("dve", lambda e, pb=pb, k=k: e.reciprocal(out=sm[:, 0, k:k + 1], in_=bank(c, pb)[:, 256:257]),
                                 reads=[c.psb[pb]], writes=[smb[k]])
                            c.op("dve", lambda e, k=k: e.tensor_tensor(out=sm[:, 0, k:k + 1], in0=sm[:, 0, k:k + 1],
                                                                       in1=c.nlam[:, j:j + 1], op=ALU.mult),
                                 reads=[smb[k]], writes=[smb[k]])
                            c.op("dve", lambda e, pb=pb, k=k, ql=ql, w=w: e.scalar_tensor_tensor(
                                out=Of[w][:], in0=bank(c, pb)[:, 0:256], scalar=sm[:, 0, k:k + 1], in1=O1[:, ql, :],
                                op0=ALU.mult, op1=ALU.add), reads=[c.psb[pb], smb[k], O1b], writes=[Ofb[w]])
                            c.op("act", lambda e, k=k, w=w: e.activation(out=sqj[:], in_=Of[w][:], func=AF.Square,
                                                                         accum_out=sm[:, 1, k:k + 1]),
                                 reads=[Ofb[w]], writes=[sqb, smb[k]])
                            c.op("dve", lambda e, k=k: e.tensor_scalar(out=sm[:, 2, k:k + 1], in0=sm[:, 1, k:k + 1], scalar1=1.0 / 256.0,
                                                                       scalar2=SUBLN_EPS, op0=ALU.mult, op1=ALU.add),
                                 reads=[smb[k]], writes=[smb[k]])
                            c.op("pool", lambda e, k=k: e.tensor_tensor(out=sm[:, 3, k:k + 1], in0=sm[:, 2, k:k + 1],
                                                                        in1=c.mhalf[:, 0:1], op=ALU.pow),
                                 reads=[smb[k]], writes=[smb[k]])
                            c.op("dve", lambda e, k=k, w=w: e.scalar_tensor_tensor(
                                out=ob[w][:], in0=Of[w][:], scalar=sm[:, 3, k:k + 1], in1=c.subg[:, j, :], op0=ALU.mult, op1=ALU.mult),
                                reads=[Ofb[w], smb[k]], writes=[obb[w]])
                            r0 = s * S + qt * 128
                            c.dma("sp", I["V2"][r0:r0 + 128, h * 256:(h + 1) * 256], ob[w][:], reads=[obb[w]])
    c.barrier()


def final_phase(c, I, do_norm):
    import contextlib
    nc = c.nc
    H = I["H"]
    with contextlib.ExitStack() as st:
        nt = NT(c, st)
        ot = [st.enter_context(SB(nc, "ot%d" % i, [128, D], F32)) for i in range(2)]
        otb = [Buf(), Buf()]
        load_gain(c, nt, I["final_norm"][0:1, :])
        for gt in range(T // 128):
            n = nt.n
            nt.n += 1
            i = n % 2
            k = n % 64
            rows = slice(gt * 128, (gt + 1) * 128)
            c.dma("sp", nt.hb[i][:], H[rows, :], writes=[nt.hbb[i]])
            if do_norm:
                c.op("act", lambda e: e.activation(out=nt.xn[i][:], in_=nt.hb[i][:], func=AF.Square,
                                                   accum_out=nt.st[:, 0, k:k + 1]),
                     reads=[nt.hbb[i]], writes=[nt.xnb[i], nt.stb[k]])
                rstd_col(c, nt, None, k, float(D), EPS, nt.stb[k])
                c.op("dve", lambda e: e.scalar_tensor_tensor(out=ot[i][:], in0=nt.hb[i][:], scalar=nt.st[:, 2, k:k + 1],
                                                             in1=nt.gbc[:], op0=ALU.mult, op1=ALU.mult),
                     reads=[nt.hbb[i], nt.stb[k], nt.gb], writes=[otb[i]])
            else:
                c.op("dve", lambda e: e.tensor_copy(out=ot[i][:], in_=nt.hb[i][:]), reads=[nt.hbb[i]], writes=[otb[i]])
            c.dma("sp", I["out"][rows, :], ot[i][:], reads=[otb[i]])
    c.barrier()


IN_SHAPES = {
    "x": [T, D], "norm_gains": [12, D], "final_norm": [1, D], "ffn_w_in": [8 * D, 2 * FF], "ffn_w_out": [8 * FF, D],
    "hgrn_w_in": [2 * D, 8192], "hgrn_lb": [2, D], "hgrn_norm": [2, D], "hgrn_w_out": [2 * D, D],
    "diff_w_in": [2 * D, 6144], "diff_lambda": [2, 512], "diff_subln": [2, 256], "diff_w_out": [2 * D, D],
    "rel_bias": [32, 8], "cst": [128, CW],
}


def build(nsub=12):
    nc = bass.Bass("TRN2", target_bir_lowering=False)
    I = {k: nc.dram_tensor(k, s, F32, kind="ExternalInput").ap() for k, s in IN_SHAPES.items()}
    I["out"] = nc.dram_tensor("out", [T, D], F32, kind="ExternalOutput").ap()
    I["H"] = nc.dram_tensor("H", [T, D], F32).ap()
    I["BV"] = nc.dram_tensor("BV", [8, 384], F32).ap()
    for nm in ("QT", "KT", "KH", "OG", "ON"):
        I[nm] = nc.dram_tensor(nm, [16, 128, T], BF16).ap()
    I["EE"] = nc.dram_tensor("EE", [16, 128, T // 32], F32).ap()
    I["V"] = nc.dram_tensor("V", [T, D], BF16).ap()
    I["V2"] = nc.dram_tensor("V2", [T, D], BF16).ap()
    I["QK"] = nc.dram_tensor("QK", [32, 128, T], BF16).ap()
    c = Ctx(nc)
    setup(c, I)
    subs = []
    for li in range(DEPTH):
        subs += [("ffn", li, 0), ("mix", li, 0), ("ffn", li, 1)]
    for si, (kind, li, k) in enumerate(subs[:nsub]):
        j = li // 2
        if kind == "ffn":
            ffn_phase(c, I, li, k, first=(si == 0))
        elif li % 2 == 0:
            hgrn_p1(c, I, li, j)
            hgrn_p2(c, I, li, j)
            outproj_phase(c, I, I["hgrn_w_out"][j * D:(j + 1) * D, :], "ON", True)
        else:
            diff_d1(c, I, li, j)
            diff_d2(c, I, li, j)
            outproj_phase(c, I, I["diff_w_out"][j * D:(j + 1) * D, :], "V2", False)
    final_phase(c, I, do_norm=(nsub >= 12))
    return nc


def make_in_maps(inputs, n_cores):
    f = lambda a: np.ascontiguousarray(np.asarray(a, dtype=np.float32))
    shared = {
        "norm_gains": f(inputs["norm_gains"]).reshape(12, D),
        "final_norm": f(inputs["final_norm"]).reshape(1, D),
        "ffn_w_in": f(inputs["ffn_w_in"]).reshape(8 * D, 2 * FF),
        "ffn_w_out": f(inputs["ffn_w_out"]).reshape(8 * FF, D),
        "hgrn_w_in": f(inputs["hgrn_w_in"]).reshape(2 * D, 8192),
        "hgrn_lb": f(inputs["hgrn_lower_bounds"]).reshape(2, D),
        "hgrn_norm": f(inputs["hgrn_norm"]).reshape(2, D),
        "hgrn_w_out": f(inputs["hgrn_w_out"]).reshape(2 * D, D),
        "diff_w_in": f(inputs["diff_w_in"]).reshape(2 * D, 6144),
        "diff_lambda": f(inputs["diff_lambda"]).reshape(2, 512),
        "diff_subln": f(inputs["diff_subln"]).reshape(2, 256),
        "diff_w_out": f(inputs["diff_w_out"]).reshape(2 * D, D),
        "rel_bias": f(inputs["rel_bias"]).reshape(32, 8),
        "cst": make_consts(),
    }
    x = f(inputs["x"])
    maps = []
    for cid in range(n_cores):
        m = dict(shared)
        m["x"] = x[cid * NSEQ:(cid + 1) * NSEQ].reshape(T, D)
        maps.append(m)
    return maps


def kernel(**inputs):
    n_cores = 8
    nc = build(12)
    maps = make_in_maps(inputs, n_cores)
    res = run_bass_kernel_spmd(nc, maps, core_ids=list(range(n_cores)))
    out = np.stack([np.asarray(r["out"]).reshape(NSEQ, S, D) for r in res.results], axis=0)
    return out.reshape(n_cores * NSEQ, S, D).astype(np.float32)
```
